# Optimizing a Trainium2 kernel written in Bass

```python
import math
import jax, jax.numpy as jnp
from jax import lax
import numpy as np

D_MODEL = 1024
BATCH = 4
SEQ = 4096
DEPTH = 2

N_A = DEPTH // 2
N_B = DEPTH - N_A
CONV_W = 3
N_HEADS = 8
HEAD_DIM = D_MODEL // N_HEADS
BLOCK = 256
TOPK = 3
Q_CHUNK = 32
ROT_DIM = HEAD_DIM // 4
ROPE_THETA = 500000.0
MEM_LEN = 256
MEM_HEADS = 4
MEM_HEAD_DIM = D_MODEL // MEM_HEADS
D_FF = ((8 * D_MODEL + 3 * 256 - 1) // (3 * 256)) * 256
EPS = 1e-6

kernel_name = "yoco_shortconv_moba_hybrid"


def rms_norm(x, g):
    x32 = x.astype(jnp.float32)
    y = x32 * lax.rsqrt(jnp.mean(x32 * x32, axis=-1, keepdims=True) + EPS)
    return (y * g.astype(jnp.float32)).astype(x.dtype)


def rope_cos_sin(positions):
    inv_freq = ROPE_THETA ** (-jnp.arange(0, ROT_DIM, 2, dtype=jnp.float32) / ROT_DIM)
    ang = positions.astype(jnp.float32)[..., None] * inv_freq
    return jnp.cos(ang)[:, :, None, :], jnp.sin(ang)[:, :, None, :]


def apply_partial_rope(x, cos, sin):
    xr = x[..., :ROT_DIM].astype(jnp.float32)
    x1, x2 = xr[..., : ROT_DIM // 2], xr[..., ROT_DIM // 2:]
    rot = jnp.concatenate([x1 * cos - x2 * sin, x2 * cos + x1 * sin], axis=-1)
    return jnp.concatenate([rot.astype(x.dtype), x[..., ROT_DIM:]], axis=-1)


def short_conv_mixer(h, w_in, w_conv, w_out):
    b_gate, c_gate, u = jnp.split(h @ w_in, 3, axis=-1)
    z = c_gate * u
    conv = lax.conv_general_dilated(
        z, w_conv[:, None, :].astype(z.dtype), window_strides=(1,),
        padding=[(CONV_W - 1, 0)], dimension_numbers=("NWC", "WIO", "NWC"),
        feature_group_count=D_MODEL)
    return (b_gate * conv) @ w_out


def mem_cross_attention(h, mem_n, w_q, w_kv, w_o):
    bsz, seq, _ = h.shape
    q = (h @ w_q).reshape(bsz, seq, MEM_HEADS, MEM_HEAD_DIM)
    k, v = jnp.split(mem_n @ w_kv, 2, axis=-1)
    k = k.reshape(bsz, -1, MEM_HEADS, MEM_HEAD_DIM)
    v = v.reshape(bsz, -1, MEM_HEADS, MEM_HEAD_DIM)
    s = jnp.einsum("bshd,bmhd->bhsm", q, k).astype(jnp.float32) * (MEM_HEAD_DIM ** -0.5)
    p = jax.nn.softmax(s, axis=-1).astype(v.dtype)
    o = jnp.einsum("bhsm,bmhd->bshd", p, v).reshape(bsz, seq, D_MODEL)
    return o @ w_o


def swiglu(h, w_gu, w_down):
    g, u = jnp.split(h @ w_gu, 2, axis=-1)
    return (jax.nn.silu(g) * u) @ w_down


def shared_kv(x, kv_norm, w_kv, cos, sin):
    bsz, seq, _ = x.shape
    n_blocks = -(-seq // BLOCK)
    pad = n_blocks * BLOCK - seq
    k, v = jnp.split(rms_norm(x, kv_norm) @ w_kv, 2, axis=-1)
    k = apply_partial_rope(k.reshape(bsz, seq, N_HEADS, HEAD_DIM), cos, sin)
    v = v.reshape(bsz, seq, N_HEADS, HEAD_DIM)
    k = jnp.pad(k, ((0, 0), (0, pad), (0, 0), (0, 0)))
    v = jnp.pad(v, ((0, 0), (0, pad), (0, 0), (0, 0)))
    k_blk = k.transpose(0, 2, 1, 3).reshape(bsz, N_HEADS, n_blocks, BLOCK, HEAD_DIM)
    v_blk = v.transpose(0, 2, 1, 3).reshape(bsz, N_HEADS, n_blocks, BLOCK, HEAD_DIM)
    k_mean = jnp.mean(k_blk.astype(jnp.float32), axis=3)
    return k_blk, v_blk, k_mean


def moba_attention(h, w_q, w_o, cos, sin, k_blk, v_blk, k_mean):
    bsz, seq, _ = h.shape
    n_blocks = k_blk.shape[2]
    k_sel = min(TOPK, n_blocks)
    n_chunks = seq // Q_CHUNK
    scale = HEAD_DIM ** -0.5
    q = apply_partial_rope((h @ w_q).reshape(bsz, seq, N_HEADS, HEAD_DIM), cos, sin)
    q_chunks = q.transpose(0, 2, 1, 3).reshape(bsz, N_HEADS, n_chunks, Q_CHUNK, HEAD_DIM)
    q_chunks = q_chunks.transpose(2, 0, 1, 3, 4)
    bi = jnp.arange(bsz)[:, None, None, None]
    hi = jnp.arange(N_HEADS)[None, :, None, None]

    def chunk_fn(args):
        c, q_c = args
        qpos = c * Q_CHUNK + jnp.arange(Q_CHUNK)
        qb = (c * Q_CHUNK) // BLOCK
        gate = jnp.einsum("bhqd,bhnd->bhqn", q_c.astype(jnp.float32), k_mean)
        past = jnp.arange(n_blocks) < qb
        gate = jnp.where(past[None, None, None, :], gate, -jnp.inf)
        _, idx = lax.top_k(gate, k_sel)
        valid = idx < qb
        kg = k_blk[bi, hi, idx]
        vg = v_blk[bi, hi, idx]
        s_sel = jnp.einsum("bhqd,bhqrkd->bhqrk", q_c, kg).astype(jnp.float32) * scale
        s_sel = jnp.where(valid[..., None], s_sel, -jnp.inf)
        s_sel = s_sel.reshape(bsz, N_HEADS, Q_CHUNK, k_sel * BLOCK)
        k_own = lax.dynamic_index_in_dim(k_blk, qb, axis=2, keepdims=False)
        v_own = lax.dynamic_index_in_dim(v_blk, qb, axis=2, keepdims=False)
        s_own = jnp.einsum("bhqd,bhkd->bhqk", q_c, k_own).astype(jnp.float32) * scale
        kpos = qb * BLOCK + jnp.arange(BLOCK)
        s_own = jnp.where((kpos[None, :] <= qpos[:, None])[None, None], s_own, -jnp.inf)
        p = jax.nn.softmax(jnp.concatenate([s_sel, s_own], axis=-1), axis=-1)
        p_sel = p[..., : k_sel * BLOCK].reshape(bsz, N_HEADS, Q_CHUNK, k_sel, BLOCK)
        p_own = p[..., k_sel * BLOCK:]
        o = jnp.einsum("bhqrk,bhqrkd->bhqd", p_sel.astype(vg.dtype), vg)
        o = o + jnp.einsum("bhqk,bhkd->bhqd", p_own.astype(v_own.dtype), v_own)
        return o

    out = lax.map(chunk_fn, (jnp.arange(n_chunks, dtype=jnp.int32), q_chunks))
    out = out.transpose(1, 0, 3, 2, 4).reshape(bsz, seq, D_MODEL)
    return out @ w_o


def setup_inputs(seed: int = 0) -> dict:
    key = jax.random.key(seed)
    ks = jax.random.split(key, 24)
    f32 = jnp.float32

    def w(k, shape, fan_in):
        return jax.random.normal(k, shape, f32) * (fan_in ** -0.5)

    def gain(k, shape):
        return 1.0 + 0.02 * jax.random.normal(k, shape, f32)

    x = jax.random.normal(ks[0], (BATCH, SEQ, D_MODEL), f32)
    mem = jax.random.normal(ks[1], (BATCH, MEM_LEN, D_MODEL), f32)
    offset = jax.random.randint(ks[2], (BATCH, 1), 0, 8192, dtype=jnp.int32)
    positions = offset + jnp.arange(SEQ, dtype=jnp.int32)[None, :]
    return {
        "x": x,
        "mem": mem,
        "positions": positions,
        "norm_mix": gain(ks[3], (DEPTH, D_MODEL)),
        "norm_mem": gain(ks[4], (DEPTH, D_MODEL)),
        "norm_memkv": gain(ks[5], (DEPTH, D_MODEL)),
        "norm_ffn": gain(ks[6], (DEPTH, D_MODEL)),
        "norm_final": gain(ks[7], (D_MODEL,)),
        "conv_w_in": w(ks[8], (N_A, D_MODEL, 3 * D_MODEL), D_MODEL),
        "conv_w": w(ks[9], (N_A, CONV_W, D_MODEL), CONV_W),
        "conv_w_out": w(ks[10], (N_A, D_MODEL, D_MODEL), D_MODEL),
        "kv_norm": gain(ks[11], (D_MODEL,)),
        "w_kv": w(ks[12], (D_MODEL, 2 * D_MODEL), D_MODEL),
        "moba_w_q": w(ks[13], (N_B, D_MODEL, D_MODEL), D_MODEL),
        "moba_w_o": w(ks[14], (N_B, D_MODEL, D_MODEL), D_MODEL),
        "mem_w_q": w(ks[15], (DEPTH, D_MODEL, D_MODEL), D_MODEL),
        "mem_w_kv": w(ks[16], (DEPTH, D_MODEL, 2 * D_MODEL), D_MODEL),
        "mem_w_o": w(ks[17], (DEPTH, D_MODEL, D_MODEL), D_MODEL),
        "ffn_w_gu": w(ks[18], (DEPTH, D_MODEL, 2 * D_FF), D_MODEL),
        "ffn_w_down": w(ks[19], (DEPTH, D_FF, D_MODEL), D_FF),
    }


def reference(x, mem, positions, norm_mix, norm_mem, norm_memkv, norm_ffn, norm_final,
              conv_w_in, conv_w, conv_w_out, kv_norm, w_kv, moba_w_q, moba_w_o,
              mem_w_q, mem_w_kv, mem_w_o, ffn_w_gu, ffn_w_down):
    cos, sin = rope_cos_sin(positions)
    k_blk = v_blk = k_mean = None
    for l in range(DEPTH):
        h = rms_norm(x, norm_mix[l])
        if l < N_A:
            x = x + short_conv_mixer(h, conv_w_in[l], conv_w[l], conv_w_out[l])
        else:
            j = l - N_A
            x = x + moba_attention(h, moba_w_q[j], moba_w_o[j], cos, sin, k_blk, v_blk, k_mean)
        x = x + mem_cross_attention(rms_norm(x, norm_mem[l]), rms_norm(mem, norm_memkv[l]),
                                    mem_w_q[l], mem_w_kv[l], mem_w_o[l])
        x = x + swiglu(rms_norm(x, norm_ffn[l]), ffn_w_gu[l], ffn_w_down[l])
        if l == N_A - 1:
            k_blk, v_blk, k_mean = shared_kv(x, kv_norm, w_kv, cos, sin)
    return rms_norm(x, norm_final)
```

```python
import numpy as np
from contextlib import ExitStack
import concourse.bass as bass
import concourse.mybir as mybir
from concourse.bass_utils import run_bass_kernel_spmd

F32 = mybir.dt.float32
BF16 = mybir.dt.bfloat16
I32 = mybir.dt.int32
AF = mybir.ActivationFunctionType
ALU = mybir.AluOpType
AX = mybir.AxisListType


class Res:
    __slots__ = ("name", "writer", "readers", "dsem")

    def __init__(self, name):
        self.name = name
        self.writer = None
        self.readers = {}
        self.dsem = None


class _Sem:
    __slots__ = ("h", "count", "key")

    def __init__(self, h, key):
        self.h = h
        self.count = 0
        self.key = key


class Tracker:
    ENGINES = ("pe", "act", "dve", "pool", "sp")
    SAME_ENGINE_SYNC = ("act", "dve", "pool")

    def __init__(self, nc, stack):
        self.nc = nc
        self.stack = stack
        self.prog = {e: [] for e in self.ENGINES}
        self.esem = {}
        for e in self.ENGINES:
            self.esem[e] = _Sem(stack.enter_context(nc.semaphore("s_" + e)), "s_" + e)
        self.waited = {e: {} for e in self.ENGINES}
        self.nsem = 0
        self.freed = {}

    def _newsem(self, name):
        self.nsem += 1
        return _Sem(self.stack.enter_context(self.nc.semaphore("d%d_%s" % (self.nsem, name))), "d%d" % self.nsem)

    def _waits(self, eng, reads, writes, after=()):
        toks = []
        for r in reads:
            if r.writer is not None:
                toks.append(r.writer)
        for w in list(writes) + list(after):
            if w.writer is not None:
                toks.append(w.writer)
            toks.extend(w.readers.values())
        need = {}
        for (sem, val, src) in toks:
            if src == eng and eng not in self.SAME_ENGINE_SYNC:
                continue
            if self.waited[eng].get(sem.key, 0) >= val:
                continue
            if need.get(sem.key, (None, 0))[1] < val:
                need[sem.key] = (sem, val)
        out = []
        for key, (sem, val) in need.items():
            self.waited[eng][key] = val
            out.append((sem.h, val))
        return out

    def _commit(self, tok, reads, writes):
        for w in writes:
            w.writer = tok
            w.readers = {}
        for r in reads:
            if r not in writes:
                old = r.readers.get(tok[0].key)
                if old is None or old[1] < tok[1]:
                    r.readers[tok[0].key] = tok

    def new_res(self, name):
        r = Res(name)
        r.readers = dict(self.freed)
        return r

    def free(self, rs):
        for r in rs:
            toks = list(r.readers.values())
            if r.writer is not None:
                toks.append(r.writer)
            for tok in toks:
                old = self.freed.get(tok[0].key)
                if old is None or old[1] < tok[1]:
                    self.freed[tok[0].key] = tok

    def op(self, eng, fn, reads=(), writes=(), sig=True):
        waits = self._waits(eng, reads, writes)
        es = self.esem[eng]
        if sig:
            es.count += 1
            tok = (es, es.count, eng)
        else:
            tok = (es, es.count + 1, eng)
        self.prog[eng].append((waits, fn, (es.h, 1) if sig else None))
        self._commit(tok, reads, writes)
        return tok

    def dma(self, eng, out, in_, reads=(), writes=(), accum=(), after=(), **kw):
        waits = self._waits(eng, reads, writes, after)
        prim = writes[0] if writes else reads[0]
        if prim.dsem is None:
            prim.dsem = self._newsem(prim.name)
        ds = prim.dsem
        ds.count += 16
        tok = (ds, ds.count, None)
        self.prog[eng].append((waits, lambda e: e.dma_start(out=out, in_=in_, **kw), (ds.h, 16)))
        self._commit(tok, reads, writes)
        for a in accum:
            old = a.readers.get(ds.key)
            if old is None or old[1] < tok[1]:
                a.readers[ds.key] = tok
        return tok

    def raw(self, eng, fn, reads=(), writes=(), after=(), sem_inc=16):
        waits = self._waits(eng, reads, writes, after)
        prim = writes[0] if writes else reads[0]
        if prim.dsem is None:
            prim.dsem = self._newsem(prim.name)
        ds = prim.dsem
        ds.count += sem_inc
        tok = (ds, ds.count, None)
        self.prog[eng].append((waits, fn, (ds.h, sem_inc)))
        self._commit(tok, reads, writes)
        return tok

    def finish(self, final=()):
        nc = self.nc
        waits = self._waits("sp", [], list(final))
        self.prog["sp"].append((waits, None, None))
        handles = {"pe": "tensor", "act": "scalar", "dve": "vector", "pool": "gpsimd", "sp": "sync"}
        with nc.Block() as block:
            for e in self.ENGINES:
                prog = self.prog[e]

                def body(h, prog=prog):
                    for (waits, fn, inc) in prog:
                        for (sh, val) in waits:
                            h.wait_ge(sh, val)
                        if fn is None:
                            continue
                        inst = fn(h)
                        if inc is not None:
                            inst.then_inc(inc[0], inc[1])

                getattr(block, handles[e])(body)


TOK = 2048
NT = 16
D = 1024
DFF = 2816
TGS = 512
NTG = 4
EPS = 1e-6
PI = float(np.pi)
G_MIX0, G_MEM0, G_MEMKV0, G_FFN0, G_KV, G_MIX1, G_MEM1, G_MEMKV1, G_FFN1, G_FINAL = range(10)
NEG = -30000.0
DEBUG = False
NO_AG = False
FFN_PARTS = 2
FFN_J = 6
FFN_DOWN = True
SKIP = ()
C_ID, C_RT, C_INVF, C_BIG0, C_BIGT, C_PB, C_B2 = 0, 128, 256, 257, 257 + 768, 257 + 1536, 257 + 1536 + 2048
C_W = 257 + 1536 + 4096


class Ctx:
    pass


def build(mode, stage=99):
    nc = bass.Bass("TRN2", target_bir_lowering=False)

    def din(name, shape, dt=F32):
        return nc.dram_tensor(name, list(shape), dt, kind="ExternalInput").ap()

    def dout(name, shape, dt=F32):
        return nc.dram_tensor(name, list(shape), dt, kind="ExternalOutput").ap()

    def dint(name, shape, dt=F32):
        return nc.dram_tensor(name, list(shape), dt, kind="Internal").ap()

    doA = mode in ("A", "F")
    doB = mode in ("B", "F")
    W = {}
    x_d = din("x", [TOK, D])
    cst_d = din("cst", [128, C_W])
    sm_d = din("sm", [128, 104])
    gfin_d = din("gfin", [1, D])
    mem_d = din("mem", [256, D])
    if doA:
        xh_d = din("xh", [128, D])
        pos_d = din("pos", [1, TOK], I32)
        W["conv_in"] = din("conv_w_in", [D, 3 * D])
        W["conv_out"] = din("conv_w_out", [D, D])
        W["w_kv"] = din("w_kv", [D, 2 * D])
    if doB:
        W["moba_q"] = din("moba_w_q", [D, D])
        W["moba_o"] = din("moba_w_o", [D, D])
    layers = ([0] if doA else []) + ([1] if doB else [])
    for l in layers:
        W["mem_q%d" % l] = din("mem_w_q%d" % l, [D, D])
        W["mem_kv%d" % l] = din("mem_w_kv%d" % l, [D, 2 * D])
        W["mem_o%d" % l] = din("mem_w_o%d" % l, [D, D])
        W["gu%d" % l] = din("ffn_w_gu%d" % l, [D, 2 * DFF])
        W["down%d" % l] = din("ffn_w_down%d" % l, [DFF, D])
    if mode == "A":
        kv_own_d = dout("kv_own", [2048, 2048], BF16)
        km_own_d = dout("km_own", [128, 64])
        cs_d = dint("cs_i", [128, 2, TOK])
        cs_out_d = dout("cs", [128, 2, TOK])
        y_d = dout("y", [TOK, D])
    elif mode == "B":
        kv_own_d = din("kv_own", [2048, 2048], BF16)
        kv_prev_d = din("kv_prev", [2048, 2048], BF16)
        km_own_d = din("km_own", [128, 64])
        km_prev_d = din("km_prev", [128, 64])
        cs_d = din("cs", [128, 2, TOK])
        y_d = dout("y", [TOK, D])
    else:
        kv_own_d = dint("kv_own", [2048, 2048], BF16)
        kv_prev_d = dint("kv_prev", [2048, 2048], BF16)
        km_own_d = dint("km_own", [128, 64])
        km_prev_d = dint("km_prev", [128, 64])
        cs_d = dint("cs", [128, 2, TOK])
        csp_d = dint("csp", [128, 2, TOK])
        xp_d = din("xp", [TOK, D])
        xhp_d = din("xhp", [128, D])
        posp_d = din("posp", [1, TOK], I32)
        y_d = dout("y", [TOK, D])

    def kT_view(kv):
        return kv[0:1024, :].rearrange("(h d) t -> h d t", h=8)

    def v_view(kv):
        return kv[1024:2048, :].rearrange("r (two f) -> (r two) f", two=2)

    with ExitStack() as st:
        T = Tracker(nc, st)
        c = Ctx()

        def sb(name, shape, dt, stack=st):
            c.uid = getattr(c, "uid", 0) + 1
            return stack.enter_context(nc.sbuf_tensor("s%d_%s" % (c.uid, name), list(shape), dt))

        r_y = Res("y_d")
        finals = [r_y]
        xres = sb("xres", [128, NT, D], F32)
        r_x = [Res("x%d" % t) for t in range(NT)]
        cst = sb("cst", [128, 257], F32)
        r_cst = Res("cst")
        smt = sb("smt", [128, 104], F32)
        r_sm = Res("sm")
        idb = sb("idb", [128, 128], BF16)
        rtb = sb("rtb", [128, 128], BF16)
        onesb = sb("onesb", [128, 128], BF16)
        r_cb = Res("constbf")
        wbuf = sb("wbuf", [128, 3, 4096], BF16)
        r_w = [Res("w%d" % i) for i in range(3)]
        ss = sb("ss", [128, 32], F32)
        rs_ = sb("rs", [128, 32], F32)
        junk = sb("junk", [128, D], BF16)
        xb = [sb("xb%d" % i, [128, D], BF16) for i in range(3)]
        r_xb = [Res("xb%d" % i) for i in range(3)]
        banks = [st.enter_context(nc.psum_tensor("bank%d" % i, [128, 512], F32)) for i in range(8)]
        r_bank = [Res("bank%d" % i) for i in range(8)]
        c.wi = 0
        c.bi = 0
        c.nrm = 0

        def gcol(gi):
            return smt[:, gi * 8:(gi + 1) * 8]

        def nextbank(lo=0, hi=6):
            b = lo + c.bi % (hi - lo)
            c.bi += 1
            return banks[b], r_bank[b]

        def wslab(src_ap, shape):
            i = c.wi % 3
            c.wi += 1
            n = int(np.prod(shape))
            dst = wbuf[:, i, 0:n]
            if len(shape) == 2:
                dst = dst.rearrange("p (a b) -> p a b", a=shape[0])
            elif len(shape) == 3:
                dst = dst.rearrange("p (a b c) -> p a b c", a=shape[0], b=shape[1])
            T.dma("pool", dst, src_ap, reads=[], writes=[r_w[i]])
            return dst, r_w[i]

        def wsrc(w, r0, nk, c0, ncol):
            return w[r0:r0 + nk * 128, c0:c0 + ncol].rearrange("(kc p) n -> p kc n", p=128)

        T.dma("sp", cst[:], cst_d[:, 0:257], writes=[r_cst])
        T.dma("sp", smt[:], sm_d, writes=[r_sm])
        if mode != "F":
            for t in range(NT):
                T.dma("sp", xres[:, t, :], x_d[t * 128:(t + 1) * 128, :], writes=[r_x[t]])
        T.op("dve", lambda e: e.tensor_copy(idb[:], cst[:, C_ID:C_ID + 128]), reads=[r_cst], writes=[r_cb])
        T.op("dve", lambda e: e.tensor_copy(rtb[:], cst[:, C_RT:C_RT + 128]), reads=[r_cst], writes=[r_cb])
        T.op("dve", lambda e: e.memset(onesb[:], 1.0), reads=[], writes=[r_cb])

        def norm_T(src, r_src, ntiles, gi, hT, r_hT, tiles_per_res=4):
            c.nrm += ntiles

            def batch_a(t0, n):
                blk = 0 if t0 < 8 else 1
                ks = [t0 + i for i in range(n)]
                for i in range(n):
                    k = ks[i]
                    T.op("act", lambda e, k=k, t=t0 + i: e.activation(junk[:], src(t), AF.Square, accum_out=ss[:, k:k + 1]),
                         reads=[r_src[t0 + i]], writes=[r_ss[k], r_junk])
                lo, hi = ks[0], ks[-1] + 1
                T.op("act", lambda e: e.activation(rs_[:, lo:hi], ss[:, lo:hi], AF.Sqrt, bias=EPS, scale=1.0 / D),
                     reads=[r_ss[k] for k in ks], writes=[r_rsb[blk]])
                T.op("dve", lambda e: e.reciprocal(rs_[:, lo:hi], rs_[:, lo:hi]), reads=[r_rsb[blk]], writes=[r_rsb[blk]])
                return {t0 + i: (ks[i], blk) for i in range(n)}

            def stage_b(t, k, blk):
                j = c.xbi % 3
                c.xbi += 1
                T.op("act", lambda e: e.activation(xb[j][:], src(t), AF.Copy, scale=rs_[:, k:k + 1]),
                     reads=[r_src[t], r_rsb[blk]], writes=[r_xb[j]])
                pt = banks[5 + j][:].bitcast(BF16)
                for kc in range(8):
                    T.op("pe", lambda e, kc=kc: e.transpose(pt[:, kc * 128:(kc + 1) * 128], xb[j][:, kc * 128:(kc + 1) * 128], idb[:]),
                         reads=[r_xb[j], r_cb], writes=[r_bank[5 + j]], sig=(kc == 7))
                T.op("dve", lambda e: e.tensor_tensor(
                    hT[:, :, t * 128:(t + 1) * 128], pt.rearrange("p (k n) -> p k n", k=8),
                    gcol(gi).unsqueeze(2).to_broadcast([128, 8, 128]), ALU.mult),
                    reads=[r_bank[5 + j], r_sm], writes=[r_hT[t // tiles_per_res]])

            slots = {}
            if src is xsrc and len(c.pre) == NT:
                slots = dict(c.pre)
            else:
                for t0, n in ((0, 4), (4, 4), (8, 8)) if ntiles == 16 else ((0, ntiles),):
                    slots.update(batch_a(t0, n))
            for t in range(ntiles):
                stage_b(t, *slots[t])

        c.blk = 0
        c.xbi = 0
        r_rsb = [Res("rsb%d" % k) for k in range(4)]
        r_junk = Res("junk")
        r_ss = [Res("ss%d" % k) for k in range(32)]
        r_rs = [Res("rs%d" % k) for k in range(32)]

        def xsrc(t):
            return xres[:, t, :]

        c.pre = {}

        def down_proj(actT, r_act, nk, slabs_for_half, presq=True):
            for half in range(2):
                sl = slabs_for_half(half)
                for t in range(NT):
                    bk, rb = nextbank()
                    ci = 0
                    for (sv, sr, k) in sl:
                        for kk in range(k):
                            T.op("pe", lambda e, sv=sv, kk=kk, ci=ci, bk=bk, t=t: e.matmul(
                                bk[:], actT[:, ci, t * 128:(t + 1) * 128], sv[:, kk, :],
                                start=(ci == 0), stop=(ci == nk - 1)),
                                reads=[r_act[t // 4], sr], writes=[rb], sig=(ci == nk - 1))
                            ci += 1
                    T.op("dve", lambda e, bk=bk, t=t, half=half: e.tensor_tensor(
                        xres[:, t, half * 512:(half + 1) * 512], bk[:], xres[:, t, half * 512:(half + 1) * 512], ALU.add),
                        reads=[rb, r_x[t]], writes=[r_x[t]])
                    if half == 1 and presq:
                        k = 16 + t
                        blk = 2 + t // 8
                        T.op("act", lambda e, k=k, t=t: e.activation(junk[:], xres[:, t, :], AF.Square, accum_out=ss[:, k:k + 1]),
                             reads=[r_x[t]], writes=[r_ss[k], r_junk])
                        c.pre[t] = (k, blk)
                        if t % 8 == 7:
                            lo, hi = 16 + t - 7, 16 + t + 1
                            T.op("act", lambda e, lo=lo, hi=hi: e.activation(rs_[:, lo:hi], ss[:, lo:hi], AF.Sqrt, bias=EPS, scale=1.0 / D),
                                 reads=[r_ss[kk] for kk in range(lo, hi)], writes=[r_rsb[blk]])
                            T.op("dve", lambda e, lo=lo, hi=hi: e.reciprocal(rs_[:, lo:hi], rs_[:, lo:hi]),
                                 reads=[r_rsb[blk]], writes=[r_rsb[blk]])

        def ffn(l, gi, bg_jobs=None):
            c.wi = (c.wi + 2) // 3 * 3
            with ExitStack() as ph:
                hT = sb("hT_f", [128, 8, TOK], BF16, ph)
                r_hT = [T.new_res("hTf%d" % i) for i in range(NTG)]
                actT = sb("actT", [128, 11, TOK], BF16, ph)
                r_act = [T.new_res("act%d" % i) for i in range(NTG)]
                sg = [sb("sg%d" % i, [128, 512], F32, ph) for i in range(2)]
                r_sg = [T.new_res("sg%d" % i) for i in range(2)]
                bg, bg_res = ([], [])
                if bg_jobs:
                    bg, bg_res = cs_bg(ph, bg_jobs)
                norm_T(xsrc, r_x, NT, gi, hT, r_hT)
                wgu = W["gu%d" % l].rearrange("(kc p) (gu f) -> p kc gu f", p=128, gu=2)
                wdn = W["down%d" % l]
                q = 0
                for part in range(FFN_PARTS):
                    for j in range(FFN_J):
                        nch = 2 if j < 5 else 1
                        c0 = part * 1408 + j * 256
                        i_ = c.wi % 3
                        c.wi += 1
                        sv = wbuf[:, i_, 0:16 * nch * 128].rearrange("p (a b c) -> p a b c", a=8, b=2)
                        sr = r_w[i_]
                        for gu_ in range(2):
                            T.dma("pool", sv[:, :, gu_, :], wgu[:, :, gu_, c0:c0 + nch * 128], reads=[], writes=[sr])
                        for tg in range(NTG):
                            for cc in range(nch):
                                ci = j * 2 + cc
                                (gb, rg), (ub, ru) = nextbank(), nextbank()
                                for which, bk, rb in ((0, gb, rg), (1, ub, ru)):
                                    for kc in range(8):
                                        T.op("pe", lambda e, bk=bk, kc=kc, which=which, cc=cc, sv=sv, tg=tg: e.matmul(
                                            bk[:], sv[:, kc, which, cc * 128:(cc + 1) * 128], hT[:, kc, tg * 512:(tg + 1) * 512],
                                            start=(kc == 0), stop=(kc == 7)),
                                            reads=[sr, r_hT[tg]], writes=[rb], sig=(kc == 7))
                                s_ = q % 2
                                q += 1
                                if bg and q % 8 == 4:
                                    bg.pop(0)()
                                T.op("act", lambda e, gb=gb, s_=s_: e.activation(sg[s_][:], gb[:], AF.Silu),
                                     reads=[rg], writes=[r_sg[s_]])
                                T.op("dve", lambda e, ub=ub, s_=s_, ci=ci, tg=tg: e.tensor_tensor(
                                    actT[:, ci, tg * 512:(tg + 1) * 512], ub[:], sg[s_][:], ALU.mult),
                                    reads=[ru, r_sg[s_]], writes=[r_act[tg]])

                    def slabs(half, part=part):
                        r0 = part * 1408
                        a = wslab(wsrc(wdn, r0, 6, half * 512, 512), [6, 512])
                        b = wslab(wsrc(wdn, r0 + 768, 5, half * 512, 512), [5, 512])
                        return [(a[0], a[1], 6), (b[0], b[1], 5)]

                    if FFN_DOWN:
                        down_proj(actT, r_act, 11, slabs, presq=(part == FFN_PARTS - 1))
                while bg:
                    bg.pop(0)()
                T.free(r_hT + r_act + r_sg + bg_res)


        def cs_tables(pos_d, cs_d, r_csd):
            C1 = 6.28125
            C2 = 2 * np.pi - 6.28125
            with ExitStack() as ph:
                posi = sb("posi", [128, 1, TOK], I32, ph)
                r_pos = T.new_res("posi")
                ang = sb("ang", [128, 512], F32, ph)
                uu = sb("uu", [128, 512], F32, ph)
                ki = sb("ki", [128, 512], I32, ph)
                kf = sb("kf", [128, 512], F32, ph)
                mm = sb("mm", [128, 512], F32, ph)
                rr = sb("rr", [128, 2, 512], F32, ph)
                r_t = T.new_res("cs_tmp")
                cso = [sb("cso%d" % i, [128, 2, 512], F32, ph) for i in range(2)]
                r_cso = [T.new_res("cso%d" % i) for i in range(2)]
                T.dma("sp", posi[:], pos_d.partition_broadcast(128), writes=[r_pos])

                def dv(fn, extra=()):
                    T.op("dve", fn, reads=[r_t, r_pos, r_cst] + list(extra), writes=[r_t])

                for ch in range(4):
                    j = ch % 2
                    dv(lambda e, ch=ch: e.tensor_copy(ang[:], posi[:, 0, ch * 512:(ch + 1) * 512]))
                    dv(lambda e: e.tensor_scalar(ang[:], ang[:], cst[:, C_INVF:C_INVF + 1], None, ALU.mult))
                    dv(lambda e: e.tensor_scalar(uu[:], ang[:], float(1.0 / (2 * np.pi)), 0.5, ALU.mult, ALU.add))
                    dv(lambda e: e.tensor_copy(ki[:], uu[:]))
                    dv(lambda e: e.tensor_copy(kf[:], ki[:]))
                    dv(lambda e: e.scalar_tensor_tensor(rr[:, 1, :], kf[:], -C1, ang[:], ALU.mult, ALU.add))
                    dv(lambda e: e.scalar_tensor_tensor(rr[:, 1, :], kf[:], -C2, rr[:, 1, :], ALU.mult, ALU.add))
                    dv(lambda e: e.tensor_scalar(mm[:], rr[:, 1, :], -PI, 2 * PI, ALU.is_lt, ALU.mult))
                    dv(lambda e: e.tensor_tensor(rr[:, 1, :], rr[:, 1, :], mm[:], ALU.add))
                    dv(lambda e: e.tensor_scalar(rr[:, 0, :], rr[:, 1, :], 0.5 * PI, None, ALU.add))
                    dv(lambda e: e.tensor_scalar(mm[:], rr[:, 0, :], PI, -2 * PI, ALU.is_gt, ALU.mult))
                    dv(lambda e: e.tensor_tensor(rr[:, 0, :], rr[:, 0, :], mm[:], ALU.add))
                    dv(lambda e: e.tensor_scalar(rr[:], rr[:], -PI, PI, ALU.max, ALU.min))
                    T.op("act", lambda e, j=j: e.activation(cso[j][:], rr[:], AF.Sin), reads=[r_t], writes=[r_cso[j]])
                    T.dma("sp", cs_d[:, :, ch * 512:(ch + 1) * 512], cso[j][:], reads=[r_cso[j]], accum=[r_csd])
                    if mode == "A":
                        T.dma("sp", cs_out_d[:, :, ch * 512:(ch + 1) * 512], cso[j][:], reads=[r_cso[j]], accum=[r_csd])
                T.free([r_pos, r_t] + r_cso)

        def cs_bg(stack, jobs):
            C1 = 6.28125
            C2 = 2 * np.pi - 6.28125
            posi = sb("bposi", [128, 1, 512], I32, stack)
            r_pos = T.new_res("bposi")
            ang = sb("bang", [128, 512], F32, stack)
            uu = sb("buu", [128, 512], F32, stack)
            ki = sb("bki", [128, 512], I32, stack)
            mm = sb("bmm", [128, 512], F32, stack)
            rr = sb("brr", [128, 2, 512], F32, stack)
            cso = sb("bcso", [128, 2, 512], F32, stack)
            r_t = T.new_res("bcs_tmp")
            r_cso = T.new_res("bcso")
            res_list = [r_pos, r_t, r_cso]

            def dv(fn):
                T.op("dve", fn, reads=[r_t, r_pos, r_cst], writes=[r_t])

            def chunk(pos_d, cs_d, r_csd, ch):
                T.dma("sp", posi[:], pos_d[:, ch * 512:(ch + 1) * 512].partition_broadcast(128), writes=[r_pos], after=[r_t])
                dv(lambda e: e.tensor_copy(ang[:], posi[:, 0, :]))
                dv(lambda e: e.tensor_scalar(ang[:], ang[:], cst[:, C_INVF:C_INVF + 1], None, ALU.mult))
                dv(lambda e: e.tensor_scalar(uu[:], ang[:], float(1.0 / (2 * np.pi)), 0.5, ALU.mult, ALU.add))
                dv(lambda e: e.tensor_copy(ki[:], uu[:]))
                dv(lambda e: e.tensor_copy(uu[:], ki[:]))
                dv(lambda e: e.scalar_tensor_tensor(rr[:, 1, :], uu[:], -C1, ang[:], ALU.mult, ALU.add))
                dv(lambda e: e.scalar_tensor_tensor(rr[:, 1, :], uu[:], -C2, rr[:, 1, :], ALU.mult, ALU.add))
                dv(lambda e: e.tensor_scalar(mm[:], rr[:, 1, :], -PI, 2 * PI, ALU.is_lt, ALU.mult))
                dv(lambda e: e.tensor_tensor(rr[:, 1, :], rr[:, 1, :], mm[:], ALU.add))
                dv(lambda e: e.tensor_scalar(rr[:, 0, :], rr[:, 1, :], 0.5 * PI, None, ALU.add))
                dv(lambda e: e.tensor_scalar(mm[:], rr[:, 0, :], PI, -2 * PI, ALU.is_gt, ALU.mult))
                dv(lambda e: e.tensor_tensor(rr[:, 0, :], rr[:, 0, :], mm[:], ALU.add))
                dv(lambda e: e.tensor_scalar(rr[:], rr[:], -PI, PI, ALU.max, ALU.min))
                T.op("act", lambda e: e.activation(cso[:], rr[:], AF.Sin), reads=[r_t], writes=[r_cso])
                T.dma("sp", cs_d[:, :, ch * 512:(ch + 1) * 512], cso[:], reads=[r_cso], accum=[r_csd])

            out = []
            for (pos_d, cs_d, r_csd) in jobs:
                for ch in range(4):
                    out.append(lambda pos_d=pos_d, cs_d=cs_d, r_csd=r_csd, ch=ch: chunk(pos_d, cs_d, r_csd, ch))
            return out, res_list

        r_csd = Res("cs_d")

        def rope1(ps, r_ps, tmp, r_tmp):
            kb0 = tmp[0]
            T.op("act", lambda e: e.activation(kb0[:], ps[:], AF.Copy), reads=[r_ps], writes=[r_tmp[0]])

        def rope2(ps, r_ps, cs_t, r_cs, out_fn, tmp, r_tmp):
            kb0, t1, t2 = tmp
            rb_, rr = nextbank()
            T.op("pe", lambda e: e.matmul(rb_[:], rtb[:], kb0[:], start=True, stop=True),
                 reads=[r_cb, r_tmp[0]], writes=[rr])
            T.op("dve", lambda e: e.tensor_tensor(t1[:], ps[:], cs_t[:, 0, :], ALU.mult), reads=[r_ps, r_cs, r_tmp[0]], writes=[r_tmp[1]])
            T.op("dve", lambda e: e.tensor_tensor(t2[:], rb_[:], cs_t[:, 1, :], ALU.mult), reads=[rr, r_cs], writes=[r_tmp[2]])
            out_fn(t1, t2)

        c.pending = None
        c.pending2 = None

        def defer(fn):
            if c.pending is not None:
                c.pending()
            c.pending = fn

        def flush():
            if c.pending is not None:
                c.pending()
            c.pending = None

        def conv_phase(xh_d):
            c.wi = (c.wi + 2) // 3 * 3
            with ExitStack() as ph:
                hT = sb("hT_c", [128, 8, TOK], BF16, ph)
                r_hT = [T.new_res("hTc%d" % i) for i in range(NTG)]
                hTh = sb("hTh", [128, 8, 128], BF16, ph)
                r_hTh = [T.new_res("hTh")]
                xh = sb("xh", [128, D], F32, ph)
                r_xh = [T.new_res("xh")]
                yT = sb("yT", [128, 8, TOK], BF16, ph)
                r_yT = [T.new_res("yT%d" % i) for i in range(NTG)]
                zb = [sb("zb%d" % i, [128, 2 + TOK], F32, ph) for i in range(2)]
                r_z = [[T.new_res("z%d_%d" % (i, g)) for g in range(NTG + 1)] for i in range(2)]
                csb = [sb("csb%d" % i, [128, 512], F32, ph) for i in range(2)]
                r_csb = [T.new_res("csb%d" % i) for i in range(2)]
                acc = [sb("acc%d" % i, [128, 512], F32, ph) for i in range(2)]
                r_acc = [T.new_res("acc%d" % i) for i in range(2)]
                T.dma("sp", xh[:], xh_d, writes=r_xh)
                norm_T(lambda t: xh[:], r_xh, 1, G_MIX0, hTh, r_hTh)
                norm_T(xsrc, r_x, NT, G_MIX0, hT, r_hT)
                win = W["conv_in"].rearrange("(kc p) (part fc j) -> p kc part fc j", p=128, part=3, fc=8)
                q = 0
                for fc in range(8):
                    i_ = c.wi % 3
                    c.wi += 1
                    sv = wbuf[:, i_, 0:3072].rearrange("p (a b c) -> p a b c", a=8, b=3)
                    sr = r_w[i_]
                    for part_ in range(3):
                        T.dma("pool", sv[:, :, part_, :], win[:, :, part_, fc, :], reads=[], writes=[sr])
                    z = zb[fc % 2]
                    rz = r_z[fc % 2]
                    hb, rh = nextbank()
                    for which in (1, 2):
                        for kc in range(8):
                            T.op("pe", lambda e, kc=kc, which=which, sv=sv, hb=hb: e.matmul(
                                hb[:, (which - 1) * 2:(which - 1) * 2 + 2], sv[:, kc, which, :], hTh[:, kc, 126:128],
                                start=(kc == 0), stop=(kc == 7)),
                                reads=[sr, r_hTh[0]], writes=[rh], sig=(kc == 7 and which == 2))
                    s_ = q % 2
                    q += 1
                    T.op("act", lambda e, hb=hb, s_=s_: e.activation(csb[s_][:, 0:2], hb[:, 0:2], AF.Copy),
                         reads=[rh], writes=[r_csb[s_]])
                    T.op("dve", lambda e, hb=hb, s_=s_, z=z: e.tensor_tensor(z[:, 0:2], hb[:, 2:4], csb[s_][:, 0:2], ALU.mult),
                         reads=[rh, r_csb[s_]], writes=[rz[NTG]])
                    for tg in range(NTG):
                        bks = [nextbank() for _ in range(3)]
                        for which in range(3):
                            bk, rb = bks[which]
                            for kc in range(8):
                                T.op("pe", lambda e, bk=bk, kc=kc, which=which, sv=sv, tg=tg: e.matmul(
                                    bk[:], sv[:, kc, which, :], hT[:, kc, tg * 512:(tg + 1) * 512],
                                    start=(kc == 0), stop=(kc == 7)),
                                    reads=[sr, r_hT[tg]], writes=[rb], sig=(kc == 7))
                        (bb, rbb), (cb_, rcb), (ub, rub) = bks
                        s_ = q % 2
                        q += 1
                        lo = 2 + tg * 512
                        T.op("act", lambda e, cb_=cb_, s_=s_: e.activation(csb[s_][:], cb_[:], AF.Copy),
                             reads=[rcb], writes=[r_csb[s_]])
                        T.op("dve", lambda e, ub=ub, s_=s_, z=z, lo=lo: e.tensor_tensor(z[:, lo:lo + 512], ub[:], csb[s_][:], ALU.mult),
                             reads=[rub, r_csb[s_]], writes=[rz[tg]])
                        prev = rz[tg - 1] if tg > 0 else rz[NTG]
                        T.op("act", lambda e, s_=s_, z=z, lo=lo, fc=fc: e.activation(
                            acc[s_][:], z[:, lo:lo + 512], AF.Copy, scale=smt[:, 80 + 16 + fc:80 + 16 + fc + 1]),
                            reads=[rz[tg], r_sm], writes=[r_acc[s_]])
                        for jj in (1, 0):
                            sh = 2 - jj
                            T.op("dve", lambda e, s_=s_, z=z, lo=lo, fc=fc, jj=jj, sh=sh: e.scalar_tensor_tensor(
                                acc[s_][:], z[:, lo - sh:lo - sh + 512], smt[:, 80 + jj * 8 + fc:80 + jj * 8 + fc + 1], acc[s_][:],
                                ALU.mult, ALU.add),
                                reads=[rz[tg], prev, r_sm, r_acc[s_]], writes=[r_acc[s_]])
                        T.op("dve", lambda e, bb=bb, s_=s_, fc=fc, tg=tg: e.tensor_tensor(
                            yT[:, fc, tg * 512:(tg + 1) * 512], bb[:], acc[s_][:], ALU.mult),
                            reads=[rbb, r_acc[s_]], writes=[r_yT[tg]])

                def slabs(half):
                    a = wslab(wsrc(W["conv_out"], 0, 8, half * 512, 512), [8, 512])
                    return [(a[0], a[1], 8)]

                if stage == 1 and DEBUG:
                    dbg1 = dout("dbg1", [128, 8, TOK], BF16)
                    dbg2 = dout("dbg2", [128, 8, TOK], BF16)
                    r_dbg = Res("dbg")
                    T.dma("sp", dbg1, hT[:], reads=r_hT, accum=[r_dbg])
                    T.dma("sp", dbg2, yT[:], reads=r_yT, accum=[r_dbg])
                    finals.append(r_dbg)
                down_proj(yT, r_yT, 8, slabs)
                T.free(r_hT + r_hTh + r_xh + r_yT + r_z[0] + r_z[1] + r_csb + r_acc)

        def memattn(l, gi_x, gi_m):
            c.wi = (c.wi + 2) // 3 * 3
            with ExitStack() as ph:
                hT = sb("hT_m", [128, 8, TOK], BF16, ph)
                r_hT = [T.new_res("hTm%d" % i) for i in range(NTG)]
                oT = sb("oT_m", [128, 8, TOK], BF16, ph)
                r_oT = [T.new_res("oTm%d" % i) for i in range(NTG)]
                memx = sb("memx", [128, 2, D], F32, ph)
                r_memx = [T.new_res("memx%d" % i) for i in range(2)]
                memT = sb("memT", [128, 8, 256], BF16, ph)
                r_memT = [T.new_res("memT")]
                KmT = sb("KmT", [128, 8, 256], BF16, ph)
                r_KmT = T.new_res("KmT")
                Vm = sb("Vm", [128, 2, D], BF16, ph)
                r_Vm = T.new_res("Vm")
                qT = [sb("qT%d" % i, [128, 2, 512], BF16, ph) for i in range(2)]
                r_qT = [T.new_res("qT%d" % i) for i in range(2)]
                eT = [sb("eT%d" % i, [128, 2, 512], BF16, ph) for i in range(2)]
                r_eT = [T.new_res("eT%d" % i) for i in range(2)]
                lns = [sb("lns%d" % i, [128, 512], F32, ph) for i in range(2)]
                r_lns = [T.new_res("lns%d" % i) for i in range(2)]
                for mt in range(2):
                    T.dma("sp", memx[:, mt, :], mem_d[mt * 128:(mt + 1) * 128, :], writes=[r_memx[mt]])
                norm_T(lambda t: memx[:, t, :], r_memx, 2, gi_m, memT, r_memT)
                wkv = W["mem_kv%d" % l]
                for j in range(2):
                    sv, sr = wslab(wsrc(wkv, 0, 8, j * 512, 512), [8, 512])
                    for cc in range(4):
                        bk, rb = nextbank()
                        for kc in range(8):
                            T.op("pe", lambda e, bk=bk, kc=kc, cc=cc, sv=sv: e.matmul(
                                bk[:, 0:256], sv[:, kc, cc * 128:(cc + 1) * 128], memT[:, kc, :], start=(kc == 0), stop=(kc == 7)),
                                reads=[sr, r_memT[0]], writes=[rb], sig=(kc == 7))
                        T.op("act", lambda e, bk=bk, j=j, cc=cc: e.activation(KmT[:, j * 4 + cc, :], bk[:, 0:256], AF.Copy),
                             reads=[rb], writes=[r_KmT])
                for j in range(2):
                    sv, sr = wslab(wsrc(wkv, 0, 8, 1024 + j * 512, 512), [8, 512])
                    for mt in range(2):
                        bk, rb = nextbank()
                        for kc in range(8):
                            T.op("pe", lambda e, bk=bk, kc=kc, mt=mt, sv=sv: e.matmul(
                                bk[:], memT[:, kc, mt * 128:(mt + 1) * 128], sv[:, kc, :], start=(kc == 0), stop=(kc == 7)),
                                reads=[sr, r_memT[0]], writes=[rb], sig=(kc == 7))
                        T.op("act", lambda e, bk=bk, j=j, mt=mt: e.activation(Vm[:, mt, j * 512:(j + 1) * 512], bk[:], AF.Copy),
                             reads=[rb], writes=[r_Vm])
                norm_T(xsrc, r_x, NT, gi_x, hT, r_hT)
                wq = W["mem_q%d" % l]
                SCM = 1.0 / 16.0
                slab_of = {}

                def stA(h, tg, s_):
                    if tg == 0:
                        slab_of[h] = wslab(wsrc(wq, 0, 8, h * 256, 256), [8, 256])
                    sv, sr = slab_of[h]
                    for dc in range(2):
                        bk, rb = nextbank()
                        for kc in range(8):
                            T.op("pe", lambda e, bk=bk, kc=kc, dc=dc: e.matmul(
                                bk[:], sv[:, kc, dc * 128:(dc + 1) * 128], hT[:, kc, tg * 512:(tg + 1) * 512],
                                start=(kc == 0), stop=(kc == 7)),
                                reads=[sr, r_hT[tg]], writes=[rb], sig=(kc == 7))
                        T.op("act", lambda e, bk=bk, dc=dc: e.activation(qT[s_][:, dc, :], bk[:], AF.Copy),
                             reads=[rb], writes=[r_qT[s_]])

                def stB(h, tg, s_):
                    for mc in range(2):
                        bk, rb = nextbank()
                        for dc in range(2):
                            T.op("pe", lambda e, bk=bk, dc=dc, mc=mc: e.matmul(
                                bk[:], KmT[:, h * 2 + dc, mc * 128:(mc + 1) * 128], qT[s_][:, dc, :],
                                start=(dc == 0), stop=(dc == 1)),
                                reads=[r_KmT, r_qT[s_]], writes=[rb], sig=(dc == 1))
                        T.op("act", lambda e, bk=bk, mc=mc: e.activation(eT[s_][:, mc, :], bk[:], AF.Exp, scale=SCM),
                             reads=[rb], writes=[r_eT[s_]])

                def stC(h, tg, s_):
                    sbk, rsb = nextbank()
                    for mc in range(2):
                        T.op("pe", lambda e, mc=mc: e.matmul(
                            sbk[:], onesb[:], eT[s_][:, mc, :], start=(mc == 0), stop=(mc == 1)),
                            reads=[r_cb, r_eT[s_]], writes=[rsb], sig=(mc == 1))
                    T.op("act", lambda e: e.activation(lns[s_][:], sbk[:], AF.Ln), reads=[rsb], writes=[r_lns[s_]])
                    T.op("act", lambda e: e.activation(lns[s_][:], lns[s_][:], AF.Exp, scale=-1.0),
                         reads=[r_lns[s_]], writes=[r_lns[s_]])
                    for dc in range(2):
                        bk, rb = nextbank()
                        for mc in range(2):
                            T.op("pe", lambda e, bk=bk, dc=dc, mc=mc: e.matmul(
                                bk[:], Vm[:, mc, h * 256 + dc * 128:h * 256 + (dc + 1) * 128], eT[s_][:, mc, :],
                                start=(mc == 0), stop=(mc == 1)),
                                reads=[r_Vm, r_eT[s_]], writes=[rb], sig=(mc == 1))
                        T.op("dve", lambda e, bk=bk, dc=dc: e.tensor_tensor(
                            oT[:, h * 2 + dc, tg * 512:(tg + 1) * 512], bk[:], lns[s_][:], ALU.mult),
                            reads=[rb, r_lns[s_]], writes=[r_oT[tg]])

                its = [(h, tg, i % 2) for i, (h, tg) in enumerate((h, tg) for h in range(4) for tg in range(NTG))]
                n_it = len(its)
                for i in range(n_it + 2):
                    if i < n_it:
                        stA(*its[i])
                    if 0 <= i - 1 < n_it:
                        stB(*its[i - 1])
                    if 0 <= i - 2 < n_it:
                        stC(*its[i - 2])

                def slabs(half):
                    a = wslab(wsrc(W["mem_o%d" % l], 0, 8, half * 512, 512), [8, 512])
                    return [(a[0], a[1], 8)]

                down_proj(oT, r_oT, 8, slabs)
                T.free(r_hT + r_oT + r_memx + r_memT + [r_KmT, r_Vm] + r_qT + r_eT + r_lns)

        r_kvd = Res("kv_own_d")
        r_kmd = Res("km_own_d")
        r_kvdp = Res("kv_prev_d")
        r_kmdp = Res("km_prev_d")
        r_csdp = Res("csp_d")

        def kv_phase(cs_d, r_csd, kv_own_d, km_own_d, r_kvd, r_kmd, after_norm=None):
            c.wi = (c.wi + 2) // 3 * 3
            with ExitStack() as ph:
                hT = sb("hT_k", [128, 8, TOK], BF16, ph)
                r_hT = [T.new_res("hTk%d" % i) for i in range(NTG)]
                cs_t = [sb("cs_t%d" % i, [128, 2, 512], F32, ph) for i in range(3)]
                r_cs = [T.new_res("cst%d" % i) for i in range(3)]
                tmp = [[sb("rk0_%d" % i, [128, 512], BF16, ph), sb("rt1_%d" % i, [128, 512], F32, ph),
                        sb("rt2_%d" % i, [128, 512], F32, ph)] for i in range(2)]
                r_tmp = [[T.new_res("rtmp%d_%d" % (i, k)) for k in range(3)] for i in range(2)]
                NKB = 8
                kb = [sb("kb%d" % i, [128, 512], BF16, ph) for i in range(NKB)]
                r_kb = [T.new_res("kb%d" % i) for i in range(NKB)]
                ksum = sb("ksum", [128, 64], F32, ph)
                r_ksum = T.new_res("ksum")
                norm_T(xsrc, r_x, NT, G_KV, hT, r_hT)
                if after_norm is not None:
                    after_norm()
                kT_d = kT_view(kv_own_d)
                v_d = v_view(kv_own_d)
                q = 0
                ci = 0

                def cs_load(i):
                    tg_ = i % NTG
                    T.dma("sp", cs_t[i % 3][:], cs_d[:, :, tg_ * 512:(tg_ + 1) * 512], writes=[r_cs[i % 3]], after=[r_csd])

                cs_load(0)
                for j in range(0 if "kvK" in SKIP else 2):
                    sv, sr = wslab(wsrc(W["w_kv"], 0, 8, j * 512, 512), [8, 512])
                    for tg in range(NTG):
                        cj = ci % 3
                        ci += 1
                        if ci < 2 * NTG:
                            cs_load(ci)
                        for hh in range(4):
                            h = j * 4 + hh
                            bk, rb = nextbank()
                            for kc in range(8):
                                T.op("pe", lambda e, bk=bk, kc=kc, hh=hh, sv=sv, tg=tg: e.matmul(
                                    bk[:], sv[:, kc, hh * 128:(hh + 1) * 128], hT[:, kc, tg * 512:(tg + 1) * 512],
                                    start=(kc == 0), stop=(kc == 7)),
                                    reads=[sr, r_hT[tg]], writes=[rb], sig=(kc == 7))
                            s_ = q % 2
                            o_ = q % NKB
                            q += 1

                            def fin(t1, t2, s_=s_, o_=o_, h=h, tg=tg):
                                T.op("dve", lambda e: e.tensor_tensor(t1[:], t1[:], t2[:], ALU.add),
                                     reads=[r_tmp[s_][1], r_tmp[s_][2]], writes=[r_tmp[s_][1]])
                                for blk in range(2):
                                    T.op("act", lambda e, blk=blk: e.activation(
                                        kb[o_][:, blk * 256:(blk + 1) * 256], t1[:, blk * 256:(blk + 1) * 256], AF.Copy,
                                        accum_out=ksum[:, h * 8 + tg * 2 + blk:h * 8 + tg * 2 + blk + 1]),
                                        reads=[r_tmp[s_][1]], writes=[r_kb[o_], r_ksum])

                            rope1(bk, rb, tmp[s_], r_tmp[s_])

                            def tail(bk=bk, rb=rb, cj=cj, s_=s_, o_=o_, h=h, tg=tg, fin=fin):
                                rope2(bk, rb, cs_t[cj], r_cs[cj], fin, tmp[s_], r_tmp[s_])
                                T.dma("sp", kT_d[h, :, tg * 512:(tg + 1) * 512], kb[o_][:], reads=[r_kb[o_]], accum=[r_kvd])

                            defer(tail)
                flush()
                for j in range(0 if "kvV" in SKIP else 2):
                    sv, sr = wslab(wsrc(W["w_kv"], 0, 8, 1024 + j * 512, 512), [8, 512])
                    for t in range(NT):
                        bk, rb = nextbank()
                        for kc in range(8):
                            T.op("pe", lambda e, bk=bk, kc=kc, sv=sv, t=t: e.matmul(
                                bk[:], hT[:, kc, t * 128:(t + 1) * 128], sv[:, kc, :], start=(kc == 0), stop=(kc == 7)),
                                reads=[sr, r_hT[t // 4]], writes=[rb], sig=(kc == 7))
                        o_ = q % NKB
                        q += 1
                        T.op("act", lambda e, bk=bk, o_=o_: e.activation(kb[o_][:], bk[:], AF.Copy), reads=[rb], writes=[r_kb[o_]])
                        T.dma("sp", v_d[t * 128:(t + 1) * 128, j * 512:(j + 1) * 512], kb[o_][:], reads=[r_kb[o_]], accum=[r_kvd])
                T.op("dve", lambda e: e.tensor_scalar(ksum[:], ksum[:], 1.0 / 256.0, None, ALU.mult), reads=[r_ksum], writes=[r_ksum])
                T.dma("sp", km_own_d, ksum[:], reads=[r_ksum], accum=[r_kmd])
                T.free(r_hT + r_cs + r_tmp[0] + r_tmp[1] + r_kb + [r_ksum])

        def moba_phase():
            c.wi = (c.wi + 2) // 3 * 3
            SC = 1.0 / float(np.sqrt(128.0))
            with ExitStack() as ph:
                qT = sb("qT_a", [128, 8, TOK], BF16, ph)
                r_qT = [T.new_res("qTa%d" % i) for i in range(NTG)]
                nmT = sb("nmT", [128, TOK], BF16, ph)
                r_nmT = [T.new_res("nmT%d" % i) for i in range(NTG)]
                with ExitStack() as phg:
                    gate = sb("gate", [128, NT, 8, 16], F32, phg)
                    r_gate = T.new_res("gate")
                    with ExitStack() as ph2:
                        hT = sb("hT_q", [128, 8, TOK], BF16, ph2)
                        r_hT = [T.new_res("hTq%d" % i) for i in range(NTG)]
                        kmT = sb("kmT", [128, 8, 16], F32, ph2)
                        r_kmT = T.new_res("kmT")
                        qf = [sb("qf%d" % i, [128, 512], F32, ph2) for i in range(2)]
                        r_qf = [T.new_res("qf%d" % i) for i in range(2)]
                        cs_t = [sb("cs_q%d" % i, [128, 2, 512], F32, ph2) for i in range(3)]
                        r_cs = [T.new_res("csq%d" % i) for i in range(3)]
                        tmp = [[sb("qk0_%d" % i, [128, 512], BF16, ph2), sb("qt1_%d" % i, [128, 512], F32, ph2),
                                sb("qt2_%d" % i, [128, 512], F32, ph2)] for i in range(2)]
                        r_tmp = [[T.new_res("qtmp%d_%d" % (i, k)) for k in range(3)] for i in range(2)]
                        norm_T(xsrc, r_x, NT, G_MIX1, hT, r_hT)
                        T.dma("sp", kmT[:, :, 0:8], km_prev_d.rearrange("p (h b) -> p h b", h=8), writes=[r_kmT], after=[r_kmdp])
                        T.dma("sp", kmT[:, :, 8:16], km_own_d.rearrange("p (h b) -> p h b", h=8), writes=[r_kmT], after=[r_kmd])
                        q = 0
                        ci = 0

                        def cs_load(i):
                            tg_ = i % NTG
                            T.dma("sp", cs_t[i % 3][:], cs_d[:, :, tg_ * 512:(tg_ + 1) * 512], writes=[r_cs[i % 3]], after=[r_csd])

                        cs_load(0)
                        for j in range(2):
                            sv, sr = wslab(wsrc(W["moba_q"], 0, 8, j * 512, 512), [8, 512])
                            for tg in range(NTG):
                                cj = ci % 3
                                ci += 1
                                if ci < 2 * NTG:
                                    cs_load(ci)
                                for hh in range(4):
                                    h = j * 4 + hh
                                    bk, rb = nextbank()
                                    for kc in range(8):
                                        T.op("pe", lambda e, bk=bk, kc=kc, hh=hh, sv=sv, tg=tg: e.matmul(
                                            bk[:], sv[:, kc, hh * 128:(hh + 1) * 128], hT[:, kc, tg * 512:(tg + 1) * 512],
                                            start=(kc == 0), stop=(kc == 7)),
                                            reads=[sr, r_hT[tg]], writes=[rb], sig=(kc == 7))
                                    s_ = q % 2
                                    q += 1

                                    def fin(t1, t2, s_=s_):
                                        T.op("dve", lambda e: e.tensor_tensor(qf[s_][:], t1[:], t2[:], ALU.add),
                                             reads=[r_tmp[s_][1], r_tmp[s_][2]], writes=[r_qf[s_]])

                                    rope1(bk, rb, tmp[s_], r_tmp[s_])

                                    def tail(bk=bk, rb=rb, cj=cj, s_=s_, h=h, tg=tg, fin=fin):
                                        if c.pending2 is not None:
                                            c.pending2()
                                        rope2(bk, rb, cs_t[cj], r_cs[cj], fin, tmp[s_], r_tmp[s_])
                                        T.op("act", lambda e: e.activation(qT[:, h, tg * 512:(tg + 1) * 512], qf[s_][:], AF.Copy),
                                             reads=[r_qf[s_]], writes=[r_qT[tg]])
                                        c.pending2 = lambda: gates(s_, h, tg)

                                    def gates(s_, h, tg):
                                        gb, rg = nextbank()
                                        for tt in range(4):
                                            T.op("pe", lambda e, tt=tt: e.matmul(
                                                gb[:, tt * 16:(tt + 1) * 16], qf[s_][:, tt * 128:(tt + 1) * 128], kmT[:, h, :],
                                                start=True, stop=True),
                                                reads=[r_qf[s_], r_kmT], writes=[rg], sig=(tt == 3))
                                        T.op("act", lambda e: e.activation(
                                            gate[:, tg * 4:(tg + 1) * 4, h, :], gb[:, 0:64].rearrange("p (t b) -> p t b", t=4), AF.Copy),
                                            reads=[rg], writes=[r_gate])

                                    defer(tail)
                        flush()
                        if c.pending2 is not None:
                            c.pending2()
                        c.pending2 = None
                        T.free(r_hT + [r_kmT] + r_qf + r_cs + r_tmp[0] + r_tmp[1])
                    with ExitStack() as ph2:
                        g2 = sb("g2", [128, NT * 8, 16], F32, ph2)
                        ee = sb("ee", [128, NT * 8, 16], F32, ph2)
                        mx = sb("mx", [128, NT * 8], F32, ph2)
                        nmb = sb("nmb", [128, NT, 128], BF16, ph2)
                        pbt = sb("pbt", [128, NT * 8, 16], F32, ph2)
                        b2t = sb("b2t", [128, NT * 8, 16], F32, ph2)
                        r_tk = T.new_res("topk")
                        r_pb = T.new_res("pbt")
                        T.dma("sp", pbt[:], cst_d[:, C_PB:C_PB + 2048].rearrange("p (g b) -> p g b", b=16), writes=[r_pb])
                        T.dma("sp", b2t[:], cst_d[:, C_B2:C_B2 + 2048].rearrange("p (g b) -> p g b", b=16), writes=[r_pb])
                        gm3 = gate[:].rearrange("p t h b -> p (t h) b")
                        mx_bc = mx[:].unsqueeze(2).to_broadcast([128, NT * 8, 16])

                        def dv(fn):
                            T.op("dve", fn, reads=[r_tk, r_pb, r_gate], writes=[r_tk, r_gate])

                        dv(lambda e: e.tensor_tensor(gm3, gm3, pbt[:], ALU.add))
                        dv(lambda e: e.tensor_reduce(mx[:], gm3, AX.X, ALU.max))
                        dv(lambda e: e.tensor_tensor(ee[:], gm3, mx_bc, ALU.is_ge))
                        dv(lambda e: e.scalar_tensor_tensor(g2[:], ee[:], -1e30, gm3, ALU.mult, ALU.add))
                        dv(lambda e: e.tensor_reduce(mx[:], g2[:], AX.X, ALU.max))
                        dv(lambda e: e.tensor_tensor(ee[:], g2[:], mx_bc, ALU.is_ge))
                        dv(lambda e: e.scalar_tensor_tensor(g2[:], ee[:], -1e30, g2[:], ALU.mult, ALU.add))
                        dv(lambda e: e.tensor_reduce(mx[:], g2[:], AX.X, ALU.max))
                        dv(lambda e: e.tensor_tensor(ee[:], gm3, mx_bc, ALU.is_ge))
                        dv(lambda e: e.scalar_tensor_tensor(ee[:], ee[:], -NEG, b2t[:], ALU.mult, ALU.add))
                        dv(lambda e: e.tensor_scalar(nmb[:].rearrange("p t (h b) -> p (t h) b", h=8), ee[:], 0.0, None, ALU.min))
                        for t in range(NT):
                            j = t % 2
                            pt = banks[6 + j][:].bitcast(BF16)
                            T.op("pe", lambda e, t=t, pt=pt: e.transpose(pt[:, 0:128], nmb[:, t, :], idb[:]),
                                 reads=[r_tk, r_cb], writes=[r_bank[6 + j]])
                            T.op("act", lambda e, t=t, pt=pt: e.activation(nmT[:, t * 128:(t + 1) * 128], pt[:, 0:128], AF.Copy),
                                 reads=[r_bank[6 + j]], writes=[r_nmT[t // 4]])
                        T.free([r_tk, r_pb])
                    T.free([r_gate])
                with ExitStack() as ph3:
                    oT = sb("oT_a", [128, 8, TOK], BF16, ph3)
                    r_oT = [T.new_res("oTa%d" % i) for i in range(NTG)]
                    kTs = [sb("kTs%d" % i, [128, 2, TOK], BF16, ph3) for i in range(2)]
                    r_kTs = [T.new_res("kTs%d" % i) for i in range(2)]
                    Vs = [sb("Vs%d" % i, [128, 32, 128], BF16, ph3) for i in range(2)]
                    r_Vs = [T.new_res("Vs%d" % i) for i in range(2)]
                    pTb = [sb("pTb%d" % i, [128, 512], BF16, ph3) for i in range(3)]
                    r_pT = [T.new_res("pTb%d" % i) for i in range(3)]
                    lns = [sb("lna0", [128, 512], F32, ph3)] * 2
                    r_lns = [T.new_res("lna0")] * 2
                    mk0 = sb("mk0", [128, 768], BF16, ph3)
                    mkt = sb("mkt", [128, 768], BF16, ph3)
                    r_mk = T.new_res("mk")
                    ones32 = sb("ones32", [128, 128], F32, ph3)
                    T.op("dve", lambda e: e.memset(ones32[:], 1.0), writes=[r_mk])
                    T.dma("pool", mk0[:], cst_d[:, C_BIG0:C_BIG0 + 768], writes=[r_mk])
                    T.dma("pool", mkt[:], cst_d[:, C_BIGT:C_BIGT + 768], writes=[r_mk])
                    smask = [mk0[:, 256:768], mkt[:, 256:768], mk0[:, 0:512], mkt[:, 0:512]]
                    kTp, kTo = kT_view(kv_prev_d), kT_view(kv_own_d)
                    vp, vo = v_view(kv_prev_d), v_view(kv_own_d)
                    def load_head(h):
                        hb = h % 2
                        T.dma("sp", kTs[hb][:, 0, :], kTp[h], writes=[r_kTs[hb]], after=[r_kvdp])
                        T.dma("sp", kTs[hb][:, 1, :], kTo[h], writes=[r_kTs[hb]], after=[r_kvd])
                        T.dma("sp", Vs[hb][:, 0:16, :], vp[:, h * 128:(h + 1) * 128].rearrange("(c p) f -> p c f", p=128),
                              writes=[r_Vs[hb]], after=[r_kvdp])
                        T.dma("sp", Vs[hb][:, 16:32, :], vo[:, h * 128:(h + 1) * 128].rearrange("(c p) f -> p c f", p=128),
                              writes=[r_Vs[hb]], after=[r_kvd])

                    items = []
                    ai = 0
                    for h in range(8):
                        for g in range(NTG):
                            a_ = ai % 2
                            ai += 1
                            chunks = [(0, cc_, h * 16 + cc_ // 2, None) for cc_ in range(16)]
                            chunks += [(1, cc_, h * 16 + 8 + cc_ // 2, (cc_ - 4 * g) if cc_ >= 4 * g else None)
                                       for cc_ in range(4 * g + 4)]
                            n = len(chunks)
                            for i_, (half, cc_, row, dj) in enumerate(chunks):
                                items.append(dict(h=h, g=g, a_=a_, half=half, cc_=cc_, row=row, dj=dj, i_=i_, n=n,
                                                  p_=len(items) % 3, last_of_head=(g == NTG - 1 and i_ == n - 1)))

                    def emit_S(it):
                        h, g, half, cc_, row, dj, p_ = it["h"], it["g"], it["half"], it["cc_"], it["row"], it["dj"], it["p_"]
                        hb = h % 2
                        stb, rst = banks[p_], r_bank[p_]
                        T.op("pe", lambda e: e.matmul(
                            stb[:], kTs[hb][:, half, cc_ * 128:(cc_ + 1) * 128], qT[:, h, g * 512:(g + 1) * 512],
                            start=True, stop=False),
                            reads=[r_kTs[hb], r_qT[g]], writes=[rst], sig=False)
                        T.op("pe", lambda e: e.matmul(
                            stb[:], idb[:, row:row + 1].to_broadcast([128, 128]), nmT[:, g * 512:(g + 1) * 512],
                            start=False, stop=(dj is None)),
                            reads=[r_cb, r_nmT[g]], writes=[rst], sig=(dj is None))
                        if dj is not None:
                            T.op("pe", lambda e: e.matmul(stb[:], idb[:], smask[dj], start=False, stop=True),
                                 reads=[r_cb, r_mk], writes=[rst])
                        T.op("act", lambda e: e.activation(pTb[p_][:], stb[:], AF.Exp, scale=SC),
                             reads=[rst], writes=[r_pT[p_]])

                    def emit_PV(it):
                        h, g, half, cc_, p_, i_, n, a_ = it["h"], it["g"], it["half"], it["cc_"], it["p_"], it["i_"], it["n"], it["a_"]
                        hb = h % 2
                        ob, rob = banks[3 + a_], r_bank[3 + a_]
                        sbk, rsb = banks[5 + a_], r_bank[5 + a_]
                        vi = cc_ if half == 0 else 16 + cc_
                        T.op("pe", lambda e: e.matmul(
                            ob[:], Vs[hb][:, vi, :], pTb[p_][:], start=(i_ == 0), stop=(i_ == n - 1)),
                            reads=[r_Vs[hb], r_pT[p_]], writes=[rob], sig=True)
                        if i_ == 0:
                            T.op("dve", lambda e: e.tensor_copy(sbk[:], pTb[p_][:]), reads=[r_pT[p_]], writes=[rsb])
                        else:
                            T.op("dve", lambda e: e.tensor_tensor(sbk[:], sbk[:], pTb[p_][:], ALU.add),
                                 reads=[r_pT[p_], rsb], writes=[rsb])
                        if i_ == n - 1:
                            T.op("dve", lambda e: e.tensor_copy(lns[a_][:], sbk[:]), reads=[rsb], writes=[r_lns[a_]])
                            T.op("pe", lambda e: e.matmul(banks[7][:], ones32[:], lns[a_][:], start=True, stop=True),
                                 reads=[r_mk, r_lns[a_]], writes=[r_bank[7]])
                            T.op("act", lambda e: e.activation(lns[a_][:], banks[7][:], AF.Ln), reads=[r_bank[7]], writes=[r_lns[a_]])
                            T.op("act", lambda e: e.activation(lns[a_][:], lns[a_][:], AF.Exp, scale=-1.0),
                                 reads=[r_lns[a_]], writes=[r_lns[a_]])
                            T.op("dve", lambda e: e.tensor_tensor(
                                oT[:, h, g * 512:(g + 1) * 512], ob[:], lns[a_][:], ALU.mult),
                                reads=[rob, r_lns[a_]], writes=[r_oT[g]])
                        if it["last_of_head"] and h + 2 < 8:
                            load_head(h + 2)

                    LA = 2
                    load_head(0)
                    load_head(1)
                    for idx in range(len(items) + LA):
                        if idx < len(items):
                            emit_S(items[idx])
                        if idx - LA >= 0:
                            emit_PV(items[idx - LA])

                    def slabs(half):
                        a = wslab(wsrc(W["moba_o"], 0, 8, half * 512, 512), [8, 512])
                        return [(a[0], a[1], 8)]

                    down_proj(oT, r_oT, 8, slabs)
                    T.free(r_oT + r_kTs + r_Vs + r_pT + r_lns + [r_mk])
                T.free(r_qT + r_nmT)

        def final_phase():
            with ExitStack() as ph:
                gf = sb("gf", [128, 1, D], F32, ph)
                r_gf = T.new_res("gf")
                yb = [sb("yb%d" % i, [128, D], F32, ph) for i in range(2)]
                r_yb = [T.new_res("yb%d" % i) for i in range(2)]
                T.dma("sp", gf[:], gfin_d.partition_broadcast(128), writes=[r_gf])
                slots = dict(c.pre) if len(c.pre) == NT else {}
                for t0 in ((0, 8) if not slots else ()):
                    blk = 0 if t0 < 8 else 1
                    ks = [t0 + i for i in range(8)]
                    for i in range(8):
                        T.op("act", lambda e, t=t0 + i, k=ks[i]: e.activation(junk[:], xres[:, t, :], AF.Square, accum_out=ss[:, k:k + 1]),
                             reads=[r_x[t0 + i]], writes=[r_ss[ks[i]], r_junk])
                        slots[t0 + i] = (ks[i], blk)
                    lo, hi = ks[0], ks[-1] + 1
                    T.op("act", lambda e, lo=lo, hi=hi: e.activation(rs_[:, lo:hi], ss[:, lo:hi], AF.Sqrt, bias=EPS, scale=1.0 / D),
                         reads=[r_ss[k] for k in ks], writes=[r_rsb[blk]])
                    T.op("dve", lambda e, lo=lo, hi=hi: e.reciprocal(rs_[:, lo:hi], rs_[:, lo:hi]), reads=[r_rsb[blk]], writes=[r_rsb[blk]])
                for t in range(NT):
                    k, blk = slots[t]
                    j = t % 2
                    T.op("dve", lambda e, t=t, k=k, j=j: e.scalar_tensor_tensor(
                        yb[j][:], xres[:, t, :], rs_[:, k:k + 1], gf[:, 0, :], ALU.mult, ALU.mult),
                        reads=[r_x[t], r_rsb[blk], r_gf], writes=[r_yb[j]])
                    T.dma("sp", y_d[t * 128:(t + 1) * 128, :], yb[j][:], reads=[r_yb[j]], accum=[r_y])
                c.nrm += NT
                T.free([r_gf] + r_yb)

        if mode == "F":
            for t in range(NT):
                T.dma("sp", xres[:, t, :], xp_d[t * 128:(t + 1) * 128, :], writes=[r_x[t]])
            conv_phase(xhp_d)
            memattn(0, G_MEM0, G_MEMKV0)
            ffn(0, G_FFN0, bg_jobs=[(posp_d, csp_d, r_csdp), (pos_d, cs_d, r_csd)])
            def reload_x():
                c.pre = {}
                for t in range(NT):
                    T.dma("sp", xres[:, t, :], x_d[t * 128:(t + 1) * 128, :], writes=[r_x[t]])

            kv_phase(csp_d, r_csdp, kv_prev_d, km_prev_d, r_kvdp, r_kmdp, after_norm=reload_x)
        if doA:
            if mode != "F":
                cs_tables(pos_d, cs_d, r_csd)
            if stage >= 1 and "conv" not in SKIP:
                conv_phase(xh_d)
            if stage >= 2 and "mem" not in SKIP:
                memattn(0, G_MEM0, G_MEMKV0)
            if stage >= 3 and "ffn" not in SKIP:
                ffn(0, G_FFN0)
            if stage >= 4:
                kv_phase(cs_d, r_csd, kv_own_d, km_own_d, r_kvd, r_kmd)
                finals += [r_kvd, r_kmd, r_csd]
        if doB:
            if stage >= 5:
                moba_phase()
            if stage >= 6:
                memattn(1, G_MEM1, G_MEMKV1)
            if stage >= 7:
                ffn(1, G_FFN1)
            final_phase()
        if mode == "A":
            for t in range(NT):
                T.dma("sp", y_d[t * 128:(t + 1) * 128, :], xres[:, t, :], reads=[r_x[t]], accum=[r_y])
        T.finish(final=finals)
    return nc


def _consts(second_half):
    cst = np.zeros((128, C_W), np.float32)
    cst[:, C_ID:C_ID + 128] = np.eye(128, dtype=np.float32)
    rt = np.zeros((128, 128), np.float32)
    for d_ in range(16):
        rt[d_ + 16, d_] = -1.0
        rt[d_, d_ + 16] = 1.0
    cst[:, C_RT:C_RT + 128] = rt
    invf = np.float32(500000.0) ** (-(np.arange(0, 32, 2, dtype=np.float32)) / np.float32(32))
    cst[0:16, C_INVF] = invf
    cst[16:32, C_INVF] = invf
    k = np.arange(128)[:, None]
    qq = np.arange(128)[None, :]
    tri = np.where(k <= qq, 0.0, NEG).astype(np.float32)
    neg = np.full((128, 128), NEG, np.float32)
    cst[:, C_BIG0 + 256:C_BIG0 + 384] = tri
    cst[:, C_BIGT + 256:C_BIGT + 384] = neg
    cst[:, C_BIGT + 384:C_BIGT + 512] = tri
    pb = np.zeros((16, 16), np.float32)
    b2 = np.zeros((16, 16), np.float32)
    for t in range(16):
        qb = t // 2
        for col in range(16):
            if col < 8:
                past, own = bool(second_half), False
            else:
                past, own = (col - 8) < qb, (col - 8) == qb
            pb[t, col] = 0.0 if past else -1e30
            b2[t, col] = NEG if past else (0.0 if own else 2 * NEG)
    cst[:, C_PB:C_PB + 2048] = np.repeat(pb[:, None, :], 8, axis=1).reshape(1, 2048)
    cst[:, C_B2:C_B2 + 2048] = np.repeat(b2[:, None, :], 8, axis=1).reshape(1, 2048)
    return cst


def _small(inp):
    gains = np.stack([inp["norm_mix"][0], inp["norm_mem"][0], inp["norm_memkv"][0], inp["norm_ffn"][0], inp["kv_norm"],
                      inp["norm_mix"][1], inp["norm_mem"][1], inp["norm_memkv"][1], inp["norm_ffn"][1], inp["norm_final"]])
    sm = np.zeros((128, 104), np.float32)
    sm[:, 0:80] = gains.reshape(10, 8, 128).transpose(2, 0, 1).reshape(128, 80)
    sm[:, 80:104] = inp["conv_w"][0].reshape(3, 8, 128).transpose(2, 0, 1).reshape(128, 24)
    return np.ascontiguousarray(sm)


def _core_inputs(inp, core, mode, x_override=None):
    b, half = core // 2, core % 2
    x = inp["x"] if x_override is None else x_override
    m = {
        "x": np.ascontiguousarray(x[b, half * TOK:(half + 1) * TOK]),
        "cst": _consts(half == 1),
        "sm": _small(inp),
        "gfin": np.ascontiguousarray(inp["norm_final"].reshape(1, D)),
        "mem": np.ascontiguousarray(inp["mem"][b]),
    }
    if mode in ("A", "F"):
        m["xh"] = (np.ascontiguousarray(inp["x"][b, TOK - 128:TOK]) if half == 1 else np.zeros((128, D), np.float32))
        m["pos"] = np.ascontiguousarray(inp["positions"][b, half * TOK:(half + 1) * TOK].reshape(1, TOK).astype(np.int32))
        m["conv_w_in"] = inp["conv_w_in"][0]
        m["conv_w_out"] = inp["conv_w_out"][0]
        m["w_kv"] = inp["w_kv"]
    if mode == "F":
        if half == 1:
            m["xp"] = np.ascontiguousarray(inp["x"][b, 0:TOK])
            m["posp"] = np.ascontiguousarray(inp["positions"][b, 0:TOK].reshape(1, TOK).astype(np.int32))
        else:
            m["xp"] = np.zeros((TOK, D), np.float32)
            m["posp"] = np.zeros((1, TOK), np.int32)
        m["xhp"] = np.zeros((128, D), np.float32)
    if mode in ("B", "F"):
        m["moba_w_q"] = inp["moba_w_q"][0]
        m["moba_w_o"] = inp["moba_w_o"][0]
    for l in ([0] if mode in ("A", "F") else []) + ([1] if mode in ("B", "F") else []):
        m["mem_w_q%d" % l] = inp["mem_w_q"][l]
        m["mem_w_kv%d" % l] = inp["mem_w_kv"][l]
        m["mem_w_o%d" % l] = inp["mem_w_o"][l]
        m["ffn_w_gu%d" % l] = inp["ffn_w_gu"][l]
        m["ffn_w_down%d" % l] = inp["ffn_w_down"][l]
    return m


def kernel(**inputs):
    inp = {k: np.asarray(v) for k, v in inputs.items()}
    n = 8
    nc = build("F")
    maps = [_core_inputs(inp, c, "F") for c in range(n)]
    res = run_bass_kernel_spmd(nc, maps, core_ids=list(range(n))).results
    y = np.stack([res[c]["y"] for c in range(n)]).reshape(4, 4096, D)
    return y.astype(np.float32)
```

```python
import numpy as np
from contextlib import ExitStack
import concourse.bass as bass
import concourse.mybir as mybir
from concourse.bass_utils import run_bass_kernel_spmd

F32 = mybir.dt.float32
BF16 = mybir.dt.bfloat16
I32 = mybir.dt.int32
AF = mybir.ActivationFunctionType
ALU = mybir.AluOpType
AX = mybir.AxisListType


class Res:
    __slots__ = ("name", "writer", "readers", "dsem")

    def __init__(self, name):
        self.name = name
        self.writer = None
        self.readers = {}
        self.dsem = None


class _Sem:
    __slots__ = ("h", "count", "key")

    def __init__(self, h, key):
        self.h = h
        self.count = 0
        self.key = key


class Tracker:
    ENGINES = ("pe", "act", "dve", "pool", "sp")
    SAME_ENGINE_SYNC = ("act", "dve", "pool")

    def __init__(self, nc, stack):
        self.nc = nc
        self.stack = stack
        self.prog = {e: [] for e in self.ENGINES}
        self.esem = {}
        for e in self.ENGINES:
            self.esem[e] = _Sem(stack.enter_context(nc.semaphore("s_" + e)), "s_" + e)
        self.waited = {e: {} for e in self.ENGINES}
        self.nsem = 0
        self.freed = {}

    def _newsem(self, name):
        self.nsem += 1
        return _Sem(self.stack.enter_context(self.nc.semaphore("d%d_%s" % (self.nsem, name))), "d%d" % self.nsem)

    def _waits(self, eng, reads, writes, after=()):
        toks = []
        for r in reads:
            if r.writer is not None:
                toks.append(r.writer)
        for w in list(writes) + list(after):
            if w.writer is not None:
                toks.append(w.writer)
            toks.extend(w.readers.values())
        need = {}
        for (sem, val, src) in toks:
            if src == eng and eng not in self.SAME_ENGINE_SYNC:
                continue
            if self.waited[eng].get(sem.key, 0) >= val:
                continue
            if need.get(sem.key, (None, 0))[1] < val:
                need[sem.key] = (sem, val)
        out = []
        for key, (sem, val) in need.items():
            self.waited[eng][key] = val
            out.append((sem.h, val))
        return out

    def _commit(self, tok, reads, writes):
        for w in writes:
            w.writer = tok
            w.readers = {}
        for r in reads:
            if r not in writes:
                old = r.readers.get(tok[0].key)
                if old is None or old[1] < tok[1]:
                    r.readers[tok[0].key] = tok

    def new_res(self, name):
        r = Res(name)
        r.readers = dict(self.freed)
        return r

    def free(self, rs):
        for r in rs:
            toks = list(r.readers.values())
            if r.writer is not None:
                toks.append(r.writer)
            for tok in toks:
                old = self.freed.get(tok[0].key)
                if old is None or old[1] < tok[1]:
                    self.freed[tok[0].key] = tok

    def op(self, eng, fn, reads=(), writes=(), sig=True):
        waits = self._waits(eng, reads, writes)
        es = self.esem[eng]
        if sig:
            es.count += 1
            tok = (es, es.count, eng)
        else:
            tok = (es, es.count + 1, eng)
        self.prog[eng].append((waits, fn, (es.h, 1) if sig else None))
        self._commit(tok, reads, writes)
        return tok

    def dma(self, eng, out, in_, reads=(), writes=(), accum=(), after=(), **kw):
        waits = self._waits(eng, reads, writes, after)
        prim = writes[0] if writes else reads[0]
        if prim.dsem is None:
            prim.dsem = self._newsem(prim.name)
        ds = prim.dsem
        ds.count += 16
        tok = (ds, ds.count, None)
        self.prog[eng].append((waits, lambda e: e.dma_start(out=out, in_=in_, **kw), (ds.h, 16)))
        self._commit(tok, reads, writes)
        for a in accum:
            old = a.readers.get(ds.key)
            if old is None or old[1] < tok[1]:
                a.readers[ds.key] = tok
        return tok

    def raw(self, eng, fn, reads=(), writes=(), after=(), sem_inc=16):
        waits = self._waits(eng, reads, writes, after)
        prim = writes[0] if writes else reads[0]
        if prim.dsem is None:
            prim.dsem = self._newsem(prim.name)
        ds = prim.dsem
        ds.count += sem_inc
        tok = (ds, ds.count, None)
        self.prog[eng].append((waits, fn, (ds.h, sem_inc)))
        self._commit(tok, reads, writes)
        return tok

    def finish(self, final=()):
        nc = self.nc
        waits = self._waits("sp", [], list(final))
        self.prog["sp"].append((waits, None, None))
        handles = {"pe": "tensor", "act": "scalar", "dve": "vector", "pool": "gpsimd", "sp": "sync"}
        with nc.Block() as block:
            for e in self.ENGINES:
                prog = self.prog[e]

                def body(h, prog=prog):
                    for (waits, fn, inc) in prog:
                        for (sh, val) in waits:
                            h.wait_ge(sh, val)
                        if fn is None:
                            continue
                        inst = fn(h)
                        if inc is not None:
                            inst.then_inc(inc[0], inc[1])

                getattr(block, handles[e])(body)


TOK = 2048
NT = 16
D = 1024
DFF = 2816
TGS = 512
NTG = 4
EPS = 1e-6
PI = float(np.pi)
G_MIX0, G_MEM0, G_MEMKV0, G_FFN0, G_KV, G_MIX1, G_MEM1, G_MEMKV1, G_FFN1, G_FINAL = range(10)
NEG = -30000.0
DEBUG = False
NO_AG = False
FFN_PARTS = 2
FFN_J = 6
FFN_DOWN = True
SKIP = ()
C_ID, C_RT, C_INVF, C_BIG0, C_BIGT, C_PB, C_B2 = 0, 128, 256, 257, 257 + 768, 257 + 1536, 257 + 1536 + 2048
C_W = 257 + 1536 + 4096


class Ctx:
    pass


def build(mode, stage=99):
    nc = bass.Bass("TRN2", target_bir_lowering=False)

    def din(name, shape, dt=F32):
        return nc.dram_tensor(name, list(shape), dt, kind="ExternalInput").ap()

    def dout(name, shape, dt=F32):
        return nc.dram_tensor(name, list(shape), dt, kind="ExternalOutput").ap()

    def dint(name, shape, dt=F32):
        return nc.dram_tensor(name, list(shape), dt, kind="Internal").ap()

    doA = mode in ("A", "F")
    doB = mode in ("B", "F")
    W = {}
    x_d = din("x", [TOK, D])
    cst_d = din("cst", [128, C_W])
    sm_d = din("sm", [128, 104])
    gfin_d = din("gfin", [1, D])
    mem_d = din("mem", [256, D])
    if doA:
        xh_d = din("xh", [128, D])
        pos_d = din("pos", [1, TOK], I32)
        W["conv_in"] = din("conv_w_in", [D, 3 * D])
        W["conv_out"] = din("conv_w_out", [D, D])
        W["w_kv"] = din("w_kv", [D, 2 * D])
    if doB:
        W["moba_q"] = din("moba_w_q", [D, D])
        W["moba_o"] = din("moba_w_o", [D, D])
    layers = ([0] if doA else []) + ([1] if doB else [])
    for l in layers:
        W["mem_q%d" % l] = din("mem_w_q%d" % l, [D, D])
        W["mem_kv%d" % l] = din("mem_w_kv%d" % l, [D, 2 * D])
        W["mem_o%d" % l] = din("mem_w_o%d" % l, [D, D])
        W["gu%d" % l] = din("ffn_w_gu%d" % l, [D, 2 * DFF])
        W["down%d" % l] = din("ffn_w_down%d" % l, [DFF, D])
    if mode == "A":
        kv_own_d = dout("kv_own", [2048, 2048], BF16)
        km_own_d = dout("km_own", [128, 64])
        cs_d = dint("cs_i", [128, 2, TOK])
        cs_out_d = dout("cs", [128, 2, TOK])
        y_d = dout("y", [TOK, D])
    elif mode == "B":
        kv_own_d = din("kv_own", [2048, 2048], BF16)
        kv_prev_d = din("kv_prev", [2048, 2048], BF16)
        km_own_d = din("km_own", [128, 64])
        km_prev_d = din("km_prev", [128, 64])
        cs_d = din("cs", [128, 2, TOK])
        y_d = dout("y", [TOK, D])
    else:
        kv_own_d = dint("kv_own", [2048, 2048], BF16)
        kv_prev_d = dint("kv_prev", [2048, 2048], BF16)
        km_own_d = dint("km_own", [128, 64])
        km_prev_d = dint("km_prev", [128, 64])
        cs_d = dint("cs", [128, 2, TOK])
        csp_d = dint("csp", [128, 2, TOK])
        xp_d = din("xp", [TOK, D])
        xhp_d = din("xhp", [128, D])
        posp_d = din("posp", [1, TOK], I32)
        y_d = dout("y", [TOK, D])

    def kT_view(kv):
        return kv[0:1024, :].rearrange("(h d) t -> h d t", h=8)

    def v_view(kv):
        return kv[1024:2048, :].rearrange("r (two f) -> (r two) f", two=2)

    with ExitStack() as st:
        T = Tracker(nc, st)
        c = Ctx()

        def sb(name, shape, dt, stack=st):
            c.uid = getattr(c, "uid", 0) + 1
            return stack.enter_context(nc.sbuf_tensor("s%d_%s" % (c.uid, name), list(shape), dt))

        r_y = Res("y_d")
        finals = [r_y]
        xres = sb("xres", [128, NT, D], F32)
        r_x = [Res("x%d" % t) for t in range(NT)]
        cst = sb("cst", [128, 257], F32)
        r_cst = Res("cst")
        smt = sb("smt", [128, 104], F32)
        r_sm = Res("sm")
        idb = sb("idb", [128, 128], BF16)
        rtb = sb("rtb", [128, 128], BF16)
        onesb = sb("onesb", [128, 128], BF16)
        r_cb = Res("constbf")
        wbuf = sb("wbuf", [128, 3, 4096], BF16)
        r_w = [Res("w%d" % i) for i in range(3)]
        ss = sb("ss", [128, 32], F32)
        rs_ = sb("rs", [128, 32], F32)
        junk = sb("junk", [128, D], BF16)
        xb = [sb("xb%d" % i, [128, D], BF16) for i in range(3)]
        r_xb = [Res("xb%d" % i) for i in range(3)]
        banks = [st.enter_context(nc.psum_tensor("bank%d" % i, [128, 512], F32)) for i in range(8)]
        r_bank = [Res("bank%d" % i) for i in range(8)]
        c.wi = 0
        c.bi = 0
        c.nrm = 0

        def gcol(gi):
            return smt[:, gi * 8:(gi + 1) * 8]

        def nextbank(lo=0, hi=6):
            b = lo + c.bi % (hi - lo)
            c.bi += 1
            return banks[b], r_bank[b]

        def wslab(src_ap, shape):
            i = c.wi % 3
            c.wi += 1
            n = int(np.prod(shape))
            dst = wbuf[:, i, 0:n]
            if len(shape) == 2:
                dst = dst.rearrange("p (a b) -> p a b", a=shape[0])
            elif len(shape) == 3:
                dst = dst.rearrange("p (a b c) -> p a b c", a=shape[0], b=shape[1])
            T.dma("pool", dst, src_ap, reads=[], writes=[r_w[i]])
            return dst, r_w[i]

        def wsrc(w, r0, nk, c0, ncol):
            return w[r0:r0 + nk * 128, c0:c0 + ncol].rearrange("(kc p) n -> p kc n", p=128)

        T.dma("sp", cst[:], cst_d[:, 0:257], writes=[r_cst])
        T.dma("sp", smt[:], sm_d, writes=[r_sm])
        if mode != "F":
            for t in range(NT):
                T.dma("sp", xres[:, t, :], x_d[t * 128:(t + 1) * 128, :], writes=[r_x[t]])
        T.op("dve", lambda e: e.tensor_copy(idb[:], cst[:, C_ID:C_ID + 128]), reads=[r_cst], writes=[r_cb])
        T.op("dve", lambda e: e.tensor_copy(rtb[:], cst[:, C_RT:C_RT + 128]), reads=[r_cst], writes=[r_cb])
        T.op("dve", lambda e: e.memset(onesb[:], 1.0), reads=[], writes=[r_cb])

        def norm_T(src, r_src, ntiles, gi, hT, r_hT, tiles_per_res=4):
            c.nrm += ntiles

            def batch_a(t0, n):
                blk = 0 if t0 < 8 else 1
                ks = [t0 + i for i in range(n)]
                for i in range(n):
                    k = ks[i]
                    T.op("act", lambda e, k=k, t=t0 + i: e.activation(junk[:], src(t), AF.Square, accum_out=ss[:, k:k + 1]),
                         reads=[r_src[t0 + i]], writes=[r_ss[k], r_junk])
                lo, hi = ks[0], ks[-1] + 1
                T.op("act", lambda e: e.activation(rs_[:, lo:hi], ss[:, lo:hi], AF.Sqrt, bias=EPS, scale=1.0 / D),
                     reads=[r_ss[k] for k in ks], writes=[r_rsb[blk]])
                T.op("dve", lambda e: e.reciprocal(rs_[:, lo:hi], rs_[:, lo:hi]), reads=[r_rsb[blk]], writes=[r_rsb[blk]])
                return {t0 + i: (ks[i], blk) for i in range(n)}

            def stage_b(t, k, blk):
                j = c.xbi % 3
                c.xbi += 1
                T.op("act", lambda e: e.activation(xb[j][:], src(t), AF.Copy, scale=rs_[:, k:k + 1]),
                     reads=[r_src[t], r_rsb[blk]], writes=[r_xb[j]])
                pt = banks[5 + j][:].bitcast(BF16)
                for kc in range(8):
                    T.op("pe", lambda e, kc=kc: e.transpose(pt[:, kc * 128:(kc + 1) * 128], xb[j][:, kc * 128:(kc + 1) * 128], idb[:]),
                         reads=[r_xb[j], r_cb], writes=[r_bank[5 + j]], sig=(kc == 7))
                T.op("dve", lambda e: e.tensor_tensor(
                    hT[:, :, t * 128:(t + 1) * 128], pt.rearrange("p (k n) -> p k n", k=8),
                    gcol(gi).unsqueeze(2).to_broadcast([128, 8, 128]), ALU.mult),
                    reads=[r_bank[5 + j], r_sm], writes=[r_hT[t // tiles_per_res]])

            slots = {}
            if src is xsrc and len(c.pre) == NT:
                slots = dict(c.pre)
            else:
                for t0, n in ((0, 4), (4, 4), (8, 8)) if ntiles == 16 else ((0, ntiles),):
                    slots.update(batch_a(t0, n))
            for t in range(ntiles):
                stage_b(t, *slots[t])

        c.blk = 0
        c.xbi = 0
        r_rsb = [Res("rsb%d" % k) for k in range(4)]
        r_junk = Res("junk")
        r_ss = [Res("ss%d" % k) for k in range(32)]
        r_rs = [Res("rs%d" % k) for k in range(32)]

        def xsrc(t):
            return xres[:, t, :]

        c.pre = {}

        def down_proj(actT, r_act, nk, slabs_for_half, presq=True):
            for half in range(2):
                sl = slabs_for_half(half)
                for t in range(NT):
                    bk, rb = nextbank()
                    ci = 0
                    for (sv, sr, k) in sl:
                        for kk in range(k):
                            T.op("pe", lambda e, sv=sv, kk=kk, ci=ci, bk=bk, t=t: e.matmul(
                                bk[:], actT[:, ci, t * 128:(t + 1) * 128], sv[:, kk, :],
                                start=(ci == 0), stop=(ci == nk - 1)),
                                reads=[r_act[t // 4], sr], writes=[rb], sig=(ci == nk - 1))
                            ci += 1
                    T.op("dve", lambda e, bk=bk, t=t, half=half: e.tensor_tensor(
                        xres[:, t, half * 512:(half + 1) * 512], bk[:], xres[:, t, half * 512:(half + 1) * 512], ALU.add),
                        reads=[rb, r_x[t]], writes=[r_x[t]])
                    if half == 1 and presq:
                        k = 16 + t
                        blk = 2 + t // 8
                        T.op("act", lambda e, k=k, t=t: e.activation(junk[:], xres[:, t, :], AF.Square, accum_out=ss[:, k:k + 1]),
                             reads=[r_x[t]], writes=[r_ss[k], r_junk])
                        c.pre[t] = (k, blk)
                        if t % 8 == 7:
                            lo, hi = 16 + t - 7, 16 + t + 1
                            T.op("act", lambda e, lo=lo, hi=hi: e.activation(rs_[:, lo:hi], ss[:, lo:hi], AF.Sqrt, bias=EPS, scale=1.0 / D),
                                 reads=[r_ss[kk] for kk in range(lo, hi)], writes=[r_rsb[blk]])
                            T.op("dve", lambda e, lo=lo, hi=hi: e.reciprocal(rs_[:, lo:hi], rs_[:, lo:hi]),
                                 reads=[r_rsb[blk]], writes=[r_rsb[blk]])

        def ffn(l, gi, bg_jobs=None):
            c.wi = (c.wi + 2) // 3 * 3
            with ExitStack() as ph:
                hT = sb("hT_f", [128, 8, TOK], BF16, ph)
                r_hT = [T.new_res("hTf%d" % i) for i in range(NTG)]
                actT = sb("actT", [128, 11, TOK], BF16, ph)
                r_act = [T.new_res("act%d" % i) for i in range(NTG)]
                sg = [sb("sg%d" % i, [128, 512], F32, ph) for i in range(2)]
                r_sg = [T.new_res("sg%d" % i) for i in range(2)]
                bg, bg_res = ([], [])
                if bg_jobs:
                    bg, bg_res = cs_bg(ph, bg_jobs)
                norm_T(xsrc, r_x, NT, gi, hT, r_hT)
                wgu = W["gu%d" % l].rearrange("(kc p) (gu f) -> p kc gu f", p=128, gu=2)
                wdn = W["down%d" % l]
                q = 0
                for part in range(FFN_PARTS):
                    for j in range(FFN_J):
                        nch = 2 if j < 5 else 1
                        c0 = part * 1408 + j * 256
                        i_ = c.wi % 3
                        c.wi += 1
                        sv = wbuf[:, i_, 0:16 * nch * 128].rearrange("p (a b c) -> p a b c", a=8, b=2)
                        sr = r_w[i_]
                        for gu_ in range(2):
                            T.dma("pool", sv[:, :, gu_, :], wgu[:, :, gu_, c0:c0 + nch * 128], reads=[], writes=[sr])
                        for tg in range(NTG):
                            for cc in range(nch):
                                ci = j * 2 + cc
                                (gb, rg), (ub, ru) = nextbank(), nextbank()
                                for which, bk, rb in ((0, gb, rg), (1, ub, ru)):
                                    for kc in range(8):
                                        T.op("pe", lambda e, bk=bk, kc=kc, which=which, cc=cc, sv=sv, tg=tg: e.matmul(
                                            bk[:], sv[:, kc, which, cc * 128:(cc + 1) * 128], hT[:, kc, tg * 512:(tg + 1) * 512],
                                            start=(kc == 0), stop=(kc == 7)),
                                            reads=[sr, r_hT[tg]], writes=[rb], sig=(kc == 7))
                                s_ = q % 2
                                q += 1
                                if bg and q % 8 == 4:
                                    bg.pop(0)()
                                T.op("act", lambda e, gb=gb, s_=s_: e.activation(sg[s_][:], gb[:], AF.Silu),
                                     reads=[rg], writes=[r_sg[s_]])
                                T.op("dve", lambda e, ub=ub, s_=s_, ci=ci, tg=tg: e.tensor_tensor(
                                    actT[:, ci, tg * 512:(tg + 1) * 512], ub[:], sg[s_][:], ALU.mult),
                                    reads=[ru, r_sg[s_]], writes=[r_act[tg]])

                    def slabs(half, part=part):
                        r0 = part * 1408
                        a = wslab(wsrc(wdn, r0, 6, half * 512, 512), [6, 512])
                        b = wslab(wsrc(wdn, r0 + 768, 5, half * 512, 512), [5, 512])
                        return [(a[0], a[1], 6), (b[0], b[1], 5)]

                    if FFN_DOWN:
                        down_proj(actT, r_act, 11, slabs, presq=(part == FFN_PARTS - 1))
                while bg:
                    bg.pop(0)()
                T.free(r_hT + r_act + r_sg + bg_res)


        def cs_tables(pos_d, cs_d, r_csd):
            C1 = 6.28125
            C2 = 2 * np.pi - 6.28125
            with ExitStack() as ph:
                posi = sb("posi", [128, 1, TOK], I32, ph)
                r_pos = T.new_res("posi")
                ang = sb("ang", [128, 512], F32, ph)
                uu = sb("uu", [128, 512], F32, ph)
                ki = sb("ki", [128, 512], I32, ph)
                kf = sb("kf", [128, 512], F32, ph)
                mm = sb("mm", [128, 512], F32, ph)
                rr = sb("rr", [128, 2, 512], F32, ph)
                r_t = T.new_res("cs_tmp")
                cso = [sb("cso%d" % i, [128, 2, 512], F32, ph) for i in range(2)]
                r_cso = [T.new_res("cso%d" % i) for i in range(2)]
                T.dma("sp", posi[:], pos_d.partition_broadcast(128), writes=[r_pos])

                def dv(fn, extra=()):
                    T.op("dve", fn, reads=[r_t, r_pos, r_cst] + list(extra), writes=[r_t])

                for ch in range(4):
                    j = ch % 2
                    dv(lambda e, ch=ch: e.tensor_copy(ang[:], posi[:, 0, ch * 512:(ch + 1) * 512]))
                    dv(lambda e: e.tensor_scalar(ang[:], ang[:], cst[:, C_INVF:C_INVF + 1], None, ALU.mult))
                    dv(lambda e: e.tensor_scalar(uu[:], ang[:], float(1.0 / (2 * np.pi)), 0.5, ALU.mult, ALU.add))
                    dv(lambda e: e.tensor_copy(ki[:], uu[:]))
                    dv(lambda e: e.tensor_copy(kf[:], ki[:]))
                    dv(lambda e: e.scalar_tensor_tensor(rr[:, 1, :], kf[:], -C1, ang[:], ALU.mult, ALU.add))
                    dv(lambda e: e.scalar_tensor_tensor(rr[:, 1, :], kf[:], -C2, rr[:, 1, :], ALU.mult, ALU.add))
                    dv(lambda e: e.tensor_scalar(mm[:], rr[:, 1, :], -PI, 2 * PI, ALU.is_lt, ALU.mult))
                    dv(lambda e: e.tensor_tensor(rr[:, 1, :], rr[:, 1, :], mm[:], ALU.add))
                    dv(lambda e: e.tensor_scalar(rr[:, 0, :], rr[:, 1, :], 0.5 * PI, None, ALU.add))
                    dv(lambda e: e.tensor_scalar(mm[:], rr[:, 0, :], PI, -2 * PI, ALU.is_gt, ALU.mult))
                    dv(lambda e: e.tensor_tensor(rr[:, 0, :], rr[:, 0, :], mm[:], ALU.add))
                    dv(lambda e: e.tensor_scalar(rr[:], rr[:], -PI, PI, ALU.max, ALU.min))
                    T.op("act", lambda e, j=j: e.activation(cso[j][:], rr[:], AF.Sin), reads=[r_t], writes=[r_cso[j]])
                    T.dma("sp", cs_d[:, :, ch * 512:(ch + 1) * 512], cso[j][:], reads=[r_cso[j]], accum=[r_csd])
                    if mode == "A":
                        T.dma("sp", cs_out_d[:, :, ch * 512:(ch + 1) * 512], cso[j][:], reads=[r_cso[j]], accum=[r_csd])
                T.free([r_pos, r_t] + r_cso)

        def cs_bg(stack, jobs):
            C1 = 6.28125
            C2 = 2 * np.pi - 6.28125
            posi = sb("bposi", [128, 1, 512], I32, stack)
            r_pos = T.new_res("bposi")
            ang = sb("bang", [128, 512], F32, stack)
            uu = sb("buu", [128, 512], F32, stack)
            ki = sb("bki", [128, 512], I32, stack)
            mm = sb("bmm", [128, 512], F32, stack)
            rr = sb("brr", [128, 2, 512], F32, stack)
            cso = sb("bcso", [128, 2, 512], F32, stack)
            r_t = T.new_res("bcs_tmp")
            r_cso = T.new_res("bcso")
            res_list = [r_pos, r_t, r_cso]

            def dv(fn):
                T.op("dve", fn, reads=[r_t, r_pos, r_cst], writes=[r_t])

            def chunk(pos_d, cs_d, r_csd, ch):
                T.dma("sp", posi[:], pos_d[:, ch * 512:(ch + 1) * 512].partition_broadcast(128), writes=[r_pos], after=[r_t])
                dv(lambda e: e.tensor_copy(ang[:], posi[:, 0, :]))
                dv(lambda e: e.tensor_scalar(ang[:], ang[:], cst[:, C_INVF:C_INVF + 1], None, ALU.mult))
                dv(lambda e: e.tensor_scalar(uu[:], ang[:], float(1.0 / (2 * np.pi)), 0.5, ALU.mult, ALU.add))
                dv(lambda e: e.tensor_copy(ki[:], uu[:]))
                dv(lambda e: e.tensor_copy(uu[:], ki[:]))
                dv(lambda e: e.scalar_tensor_tensor(rr[:, 1, :], uu[:], -C1, ang[:], ALU.mult, ALU.add))
                dv(lambda e: e.scalar_tensor_tensor(rr[:, 1, :], uu[:], -C2, rr[:, 1, :], ALU.mult, ALU.add))
                dv(lambda e: e.tensor_scalar(mm[:], rr[:, 1, :], -PI, 2 * PI, ALU.is_lt, ALU.mult))
                dv(lambda e: e.tensor_tensor(rr[:, 1, :], rr[:, 1, :], mm[:], ALU.add))
                dv(lambda e: e.tensor_scalar(rr[:, 0, :], rr[:, 1, :], 0.5 * PI, None, ALU.add))
                dv(lambda e: e.tensor_scalar(mm[:], rr[:, 0, :], PI, -2 * PI, ALU.is_gt, ALU.mult))
                dv(lambda e: e.tensor_tensor(rr[:, 0, :], rr[:, 0, :], mm[:], ALU.add))
                dv(lambda e: e.tensor_scalar(rr[:], rr[:], -PI, PI, ALU.max, ALU.min))
                T.op("act", lambda e: e.activation(cso[:], rr[:], AF.Sin), reads=[r_t], writes=[r_cso])
                T.dma("sp", cs_d[:, :, ch * 512:(ch + 1) * 512], cso[:], reads=[r_cso], accum=[r_csd])

            out = []
            for (pos_d, cs_d, r_csd) in jobs:
                for ch in range(4):
                    out.append(lambda pos_d=pos_d, cs_d=cs_d, r_csd=r_csd, ch=ch: chunk(pos_d, cs_d, r_csd, ch))
            return out, res_list

        r_csd = Res("cs_d")

        def rope1(ps, r_ps, tmp, r_tmp):
            kb0 = tmp[0]
            T.op("act", lambda e: e.activation(kb0[:], ps[:], AF.Copy), reads=[r_ps], writes=[r_tmp[0]])

        def rope2(ps, r_ps, cs_t, r_cs, out_fn, tmp, r_tmp):
            kb0, t1, t2 = tmp
            rb_, rr = nextbank()
            T.op("pe", lambda e: e.matmul(rb_[:], rtb[:], kb0[:], start=True, stop=True),
                 reads=[r_cb, r_tmp[0]], writes=[rr])
            T.op("dve", lambda e: e.tensor_tensor(t1[:], ps[:], cs_t[:, 0, :], ALU.mult), reads=[r_ps, r_cs, r_tmp[0]], writes=[r_tmp[1]])
            T.op("dve", lambda e: e.tensor_tensor(t2[:], rb_[:], cs_t[:, 1, :], ALU.mult), reads=[rr, r_cs], writes=[r_tmp[2]])
            out_fn(t1, t2)

        c.pending = None
        c.pending2 = None

        def defer(fn):
            if c.pending is not None:
                c.pending()
            c.pending = fn

        def flush():
            if c.pending is not None:
                c.pending()
            c.pending = None

        def conv_phase(xh_d):
            c.wi = (c.wi + 2) // 3 * 3
            with ExitStack() as ph:
                hT = sb("hT_c", [128, 8, TOK], BF16, ph)
                r_hT = [T.new_res("hTc%d" % i) for i in range(NTG)]
                hTh = sb("hTh", [128, 8, 128], BF16, ph)
                r_hTh = [T.new_res("hTh")]
                xh = sb("xh", [128, D], F32, ph)
                r_xh = [T.new_res("xh")]
                yT = sb("yT", [128, 8, TOK], BF16, ph)
                r_yT = [T.new_res("yT%d" % i) for i in range(NTG)]
                zb = [sb("zb%d" % i, [128, 2 + TOK], F32, ph) for i in range(2)]
                r_z = [[T.new_res("z%d_%d" % (i, g)) for g in range(NTG + 1)] for i in range(2)]
                csb = [sb("csb%d" % i, [128, 512], F32, ph) for i in range(2)]
                r_csb = [T.new_res("csb%d" % i) for i in range(2)]
                acc = [sb("acc%d" % i, [128, 512], F32, ph) for i in range(2)]
                r_acc = [T.new_res("acc%d" % i) for i in range(2)]
                T.dma("sp", xh[:], xh_d, writes=r_xh)
                norm_T(lambda t: xh[:], r_xh, 1, G_MIX0, hTh, r_hTh)
                norm_T(xsrc, r_x, NT, G_MIX0, hT, r_hT)
                win = W["conv_in"].rearrange("(kc p) (part fc j) -> p kc part fc j", p=128, part=3, fc=8)
                q = 0
                for fc in range(8):
                    i_ = c.wi % 3
                    c.wi += 1
                    sv = wbuf[:, i_, 0:3072].rearrange("p (a b c) -> p a b c", a=8, b=3)
                    sr = r_w[i_]
                    for part_ in range(3):
                        T.dma("pool", sv[:, :, part_, :], win[:, :, part_, fc, :], reads=[], writes=[sr])
                    z = zb[fc % 2]
                    rz = r_z[fc % 2]
                    hb, rh = nextbank()
                    for which in (1, 2):
                        for kc in range(8):
                            T.op("pe", lambda e, kc=kc, which=which, sv=sv, hb=hb: e.matmul(
                                hb[:, (which - 1) * 2:(which - 1) * 2 + 2], sv[:, kc, which, :], hTh[:, kc, 126:128],
                                start=(kc == 0), stop=(kc == 7)),
                                reads=[sr, r_hTh[0]], writes=[rh], sig=(kc == 7 and which == 2))
                    s_ = q % 2
                    q += 1
                    T.op("act", lambda e, hb=hb, s_=s_: e.activation(csb[s_][:, 0:2], hb[:, 0:2], AF.Copy),
                         reads=[rh], writes=[r_csb[s_]])
                    T.op("dve", lambda e, hb=hb, s_=s_, z=z: e.tensor_tensor(z[:, 0:2], hb[:, 2:4], csb[s_][:, 0:2], ALU.mult),
                         reads=[rh, r_csb[s_]], writes=[rz[NTG]])
                    for tg in range(NTG):
                        bks = [nextbank() for _ in range(3)]
                        for which in range(3):
                            bk, rb = bks[which]
                            for kc in range(8):
                                T.op("pe", lambda e, bk=bk, kc=kc, which=which, sv=sv, tg=tg: e.matmul(
                                    bk[:], sv[:, kc, which, :], hT[:, kc, tg * 512:(tg + 1) * 512],
                                    start=(kc == 0), stop=(kc == 7)),
                                    reads=[sr, r_hT[tg]], writes=[rb], sig=(kc == 7))
                        (bb, rbb), (cb_, rcb), (ub, rub) = bks
                        s_ = q % 2
                        q += 1
                        lo = 2 + tg * 512
                        T.op("act", lambda e, cb_=cb_, s_=s_: e.activation(csb[s_][:], cb_[:], AF.Copy),
                             reads=[rcb], writes=[r_csb[s_]])
                        T.op("dve", lambda e, ub=ub, s_=s_, z=z, lo=lo: e.tensor_tensor(z[:, lo:lo + 512], ub[:], csb[s_][:], ALU.mult),
                             reads=[rub, r_csb[s_]], writes=[rz[tg]])
                        prev = rz[tg - 1] if tg > 0 else rz[NTG]
                        T.op("act", lambda e, s_=s_, z=z, lo=lo, fc=fc: e.activation(
                            acc[s_][:], z[:, lo:lo + 512], AF.Copy, scale=smt[:, 80 + 16 + fc:80 + 16 + fc + 1]),
                            reads=[rz[tg], r_sm], writes=[r_acc[s_]])
                        for jj in (1, 0):
                            sh = 2 - jj
                            T.op("dve", lambda e, s_=s_, z=z, lo=lo, fc=fc, jj=jj, sh=sh: e.scalar_tensor_tensor(
                                acc[s_][:], z[:, lo - sh:lo - sh + 512], smt[:, 80 + jj * 8 + fc:80 + jj * 8 + fc + 1], acc[s_][:],
                                ALU.mult, ALU.add),
                                reads=[rz[tg], prev, r_sm, r_acc[s_]], writes=[r_acc[s_]])
                        T.op("dve", lambda e, bb=bb, s_=s_, fc=fc, tg=tg: e.tensor_tensor(
                            yT[:, fc, tg * 512:(tg + 1) * 512], bb[:], acc[s_][:], ALU.mult),
                            reads=[rbb, r_acc[s_]], writes=[r_yT[tg]])

                def slabs(half):
                    a = wslab(wsrc(W["conv_out"], 0, 8, half * 512, 512), [8, 512])
                    return [(a[0], a[1], 8)]

                if stage == 1 and DEBUG:
                    dbg1 = dout("dbg1", [128, 8, TOK], BF16)
                    dbg2 = dout("dbg2", [128, 8, TOK], BF16)
                    r_dbg = Res("dbg")
                    T.dma("sp", dbg1, hT[:], reads=r_hT, accum=[r_dbg])
                    T.dma("sp", dbg2, yT[:], reads=r_yT, accum=[r_dbg])
                    finals.append(r_dbg)
                down_proj(yT, r_yT, 8, slabs)
                T.free(r_hT + r_hTh + r_xh + r_yT + r_z[0] + r_z[1] + r_csb + r_acc)

        def memattn(l, gi_x, gi_m):
            c.wi = (c.wi + 2) // 3 * 3
            with ExitStack() as ph:
                hT = sb("hT_m", [128, 8, TOK], BF16, ph)
                r_hT = [T.new_res("hTm%d" % i) for i in range(NTG)]
                oT = sb("oT_m", [128, 8, TOK], BF16, ph)
                r_oT = [T.new_res("oTm%d" % i) for i in range(NTG)]
                memx = sb("memx", [128, 2, D], F32, ph)
                r_memx = [T.new_res("memx%d" % i) for i in range(2)]
                memT = sb("memT", [128, 8, 256], BF16, ph)
                r_memT = [T.new_res("memT")]
                KmT = sb("KmT", [128, 8, 256], BF16, ph)
                r_KmT = T.new_res("KmT")
                Vm = sb("Vm", [128, 2, D], BF16, ph)
                r_Vm = T.new_res("Vm")
                qT = [sb("qT%d" % i, [128, 2, 512], BF16, ph) for i in range(2)]
                r_qT = [T.new_res("qT%d" % i) for i in range(2)]
                eT = [sb("eT%d" % i, [128, 2, 512], BF16, ph) for i in range(2)]
                r_eT = [T.new_res("eT%d" % i) for i in range(2)]
                lns = [sb("lns%d" % i, [128, 512], F32, ph) for i in range(2)]
                r_lns = [T.new_res("lns%d" % i) for i in range(2)]
                for mt in range(2):
                    T.dma("sp", memx[:, mt, :], mem_d[mt * 128:(mt + 1) * 128, :], writes=[r_memx[mt]])
                norm_T(lambda t: memx[:, t, :], r_memx, 2, gi_m, memT, r_memT)
                wkv = W["mem_kv%d" % l]
                for j in range(2):
                    sv, sr = wslab(wsrc(wkv, 0, 8, j * 512, 512), [8, 512])
                    for cc in range(4):
                        bk, rb = nextbank()
                        for kc in range(8):
                            T.op("pe", lambda e, bk=bk, kc=kc, cc=cc, sv=sv: e.matmul(
                                bk[:, 0:256], sv[:, kc, cc * 128:(cc + 1) * 128], memT[:, kc, :], start=(kc == 0), stop=(kc == 7)),
                                reads=[sr, r_memT[0]], writes=[rb], sig=(kc == 7))
                        T.op("act", lambda e, bk=bk, j=j, cc=cc: e.activation(KmT[:, j * 4 + cc, :], bk[:, 0:256], AF.Copy),
                             reads=[rb], writes=[r_KmT])
                for j in range(2):
                    sv, sr = wslab(wsrc(wkv, 0, 8, 1024 + j * 512, 512), [8, 512])
                    for mt in range(2):
                        bk, rb = nextbank()
                        for kc in range(8):
                            T.op("pe", lambda e, bk=bk, kc=kc, mt=mt, sv=sv: e.matmul(
                                bk[:], memT[:, kc, mt * 128:(mt + 1) * 128], sv[:, kc, :], start=(kc == 0), stop=(kc == 7)),
                                reads=[sr, r_memT[0]], writes=[rb], sig=(kc == 7))
                        T.op("act", lambda e, bk=bk, j=j, mt=mt: e.activation(Vm[:, mt, j * 512:(j + 1) * 512], bk[:], AF.Copy),
                             reads=[rb], writes=[r_Vm])
                norm_T(xsrc, r_x, NT, gi_x, hT, r_hT)
                wq = W["mem_q%d" % l]
                SCM = 1.0 / 16.0
                slab_of = {}

                def stA(h, tg, s_):
                    if tg == 0:
                        slab_of[h] = wslab(wsrc(wq, 0, 8, h * 256, 256), [8, 256])
                    sv, sr = slab_of[h]
                    for dc in range(2):
                        bk, rb = nextbank()
                        for kc in range(8):
                            T.op("pe", lambda e, bk=bk, kc=kc, dc=dc: e.matmul(
                                bk[:], sv[:, kc, dc * 128:(dc + 1) * 128], hT[:, kc, tg * 512:(tg + 1) * 512],
                                start=(kc == 0), stop=(kc == 7)),
                                reads=[sr, r_hT[tg]], writes=[rb], sig=(kc == 7))
                        T.op("act", lambda e, bk=bk, dc=dc: e.activation(qT[s_][:, dc, :], bk[:], AF.Copy),
                             reads=[rb], writes=[r_qT[s_]])

                def stB(h, tg, s_):
                    for mc in range(2):
                        bk, rb = nextbank()
                        for dc in range(2):
                            T.op("pe", lambda e, bk=bk, dc=dc, mc=mc: e.matmul(
                                bk[:], KmT[:, h * 2 + dc, mc * 128:(mc + 1) * 128], qT[s_][:, dc, :],
                                start=(dc == 0), stop=(dc == 1)),
                                reads=[r_KmT, r_qT[s_]], writes=[rb], sig=(dc == 1))
                        T.op("act", lambda e, bk=bk, mc=mc: e.activation(eT[s_][:, mc, :], bk[:], AF.Exp, scale=SCM),
                             reads=[rb], writes=[r_eT[s_]])

                def stC(h, tg, s_):
                    sbk, rsb = nextbank()
                    for mc in range(2):
                        T.op("pe", lambda e, mc=mc: e.matmul(
                            sbk[:], onesb[:], eT[s_][:, mc, :], start=(mc == 0), stop=(mc == 1)),
                            reads=[r_cb, r_eT[s_]], writes=[rsb], sig=(mc == 1))
                    T.op("act", lambda e: e.activation(lns[s_][:], sbk[:], AF.Ln), reads=[rsb], writes=[r_lns[s_]])
                    T.op("act", lambda e: e.activation(lns[s_][:], lns[s_][:], AF.Exp, scale=-1.0),
                         reads=[r_lns[s_]], writes=[r_lns[s_]])
                    for dc in range(2):
                        bk, rb = nextbank()
                        for mc in range(2):
                            T.op("pe", lambda e, bk=bk, dc=dc, mc=mc: e.matmul(
                                bk[:], Vm[:, mc, h * 256 + dc * 128:h * 256 + (dc + 1) * 128], eT[s_][:, mc, :],
                                start=(mc == 0), stop=(mc == 1)),
                                reads=[r_Vm, r_eT[s_]], writes=[rb], sig=(mc == 1))
                        T.op("dve", lambda e, bk=bk, dc=dc: e.tensor_tensor(
                            oT[:, h * 2 + dc, tg * 512:(tg + 1) * 512], bk[:], lns[s_][:], ALU.mult),
                            reads=[rb, r_lns[s_]], writes=[r_oT[tg]])

                its = [(h, tg, i % 2) for i, (h, tg) in enumerate((h, tg) for h in range(4) for tg in range(NTG))]
                n_it = len(its)
                for i in range(n_it + 2):
                    if i < n_it:
                        stA(*its[i])
                    if 0 <= i - 1 < n_it:
                        stB(*its[i - 1])
                    if 0 <= i - 2 < n_it:
                        stC(*its[i - 2])

                def slabs(half):
                    a = wslab(wsrc(W["mem_o%d" % l], 0, 8, half * 512, 512), [8, 512])
                    return [(a[0], a[1], 8)]

                down_proj(oT, r_oT, 8, slabs)
                T.free(r_hT + r_oT + r_memx + r_memT + [r_KmT, r_Vm] + r_qT + r_eT + r_lns)

        r_kvd = Res("kv_own_d")
        r_kmd = Res("km_own_d")
        r_kvdp = Res("kv_prev_d")
        r_kmdp = Res("km_prev_d")
        r_csdp = Res("csp_d")

        def kv_phase(cs_d, r_csd, kv_own_d, km_own_d, r_kvd, r_kmd, after_norm=None):
            c.wi = (c.wi + 2) // 3 * 3
            with ExitStack() as ph:
                hT = sb("hT_k", [128, 8, TOK], BF16, ph)
                r_hT = [T.new_res("hTk%d" % i) for i in range(NTG)]
                cs_t = [sb("cs_t%d" % i, [128, 2, 512], F32, ph) for i in range(3)]
                r_cs = [T.new_res("cst%d" % i) for i in range(3)]
                tmp = [[sb("rk0_%d" % i, [128, 512], BF16, ph), sb("rt1_%d" % i, [128, 512], F32, ph),
                        sb("rt2_%d" % i, [128, 512], F32, ph)] for i in range(2)]
                r_tmp = [[T.new_res("rtmp%d_%d" % (i, k)) for k in range(3)] for i in range(2)]
                NKB = 8
                kb = [sb("kb%d" % i, [128, 512], BF16, ph) for i in range(NKB)]
                r_kb = [T.new_res("kb%d" % i) for i in range(NKB)]
                ksum = sb("ksum", [128, 64], F32, ph)
                r_ksum = T.new_res("ksum")
                norm_T(xsrc, r_x, NT, G_KV, hT, r_hT)
                if after_norm is not None:
                    after_norm()
                kT_d = kT_view(kv_own_d)
                v_d = v_view(kv_own_d)
                q = 0
                ci = 0

                def cs_load(i):
                    tg_ = i % NTG
                    T.dma("sp", cs_t[i % 3][:], cs_d[:, :, tg_ * 512:(tg_ + 1) * 512], writes=[r_cs[i % 3]], after=[r_csd])

                cs_load(0)
                for j in range(0 if "kvK" in SKIP else 2):
                    sv, sr = wslab(wsrc(W["w_kv"], 0, 8, j * 512, 512), [8, 512])
                    for tg in range(NTG):
                        cj = ci % 3
                        ci += 1
                        if ci < 2 * NTG:
                            cs_load(ci)
                        for hh in range(4):
                            h = j * 4 + hh
                            bk, rb = nextbank()
                            for kc in range(8):
                                T.op("pe", lambda e, bk=bk, kc=kc, hh=hh, sv=sv, tg=tg: e.matmul(
                                    bk[:], sv[:, kc, hh * 128:(hh + 1) * 128], hT[:, kc, tg * 512:(tg + 1) * 512],
                                    start=(kc == 0), stop=(kc == 7)),
                                    reads=[sr, r_hT[tg]], writes=[rb], sig=(kc == 7))
                            s_ = q % 2
                            o_ = q % NKB
                            q += 1

                            def fin(t1, t2, s_=s_, o_=o_, h=h, tg=tg):
                                T.op("dve", lambda e: e.tensor_tensor(t1[:], t1[:], t2[:], ALU.add),
                                     reads=[r_tmp[s_][1], r_tmp[s_][2]], writes=[r_tmp[s_][1]])
                                for blk in range(2):
                                    T.op("act", lambda e, blk=blk: e.activation(
                                        kb[o_][:, blk * 256:(blk + 1) * 256], t1[:, blk * 256:(blk + 1) * 256], AF.Copy,
                                        accum_out=ksum[:, h * 8 + tg * 2 + blk:h * 8 + tg * 2 + blk + 1]),
                                        reads=[r_tmp[s_][1]], writes=[r_kb[o_], r_ksum])

                            rope1(bk, rb, tmp[s_], r_tmp[s_])

                            def tail(bk=bk, rb=rb, cj=cj, s_=s_, o_=o_, h=h, tg=tg, fin=fin):
                                rope2(bk, rb, cs_t[cj], r_cs[cj], fin, tmp[s_], r_tmp[s_])
                                T.dma("sp", kT_d[h, :, tg * 512:(tg + 1) * 512], kb[o_][:], reads=[r_kb[o_]], accum=[r_kvd])

                            defer(tail)
                flush()
                for j in range(0 if "kvV" in SKIP else 2):
                    sv, sr = wslab(wsrc(W["w_kv"], 0, 8, 1024 + j * 512, 512), [8, 512])
                    for t in range(NT):
                        bk, rb = nextbank()
                        for kc in range(8):
                            T.op("pe", lambda e, bk=bk, kc=kc, sv=sv, t=t: e.matmul(
                                bk[:], hT[:, kc, t * 128:(t + 1) * 128], sv[:, kc, :], start=(kc == 0), stop=(kc == 7)),
                                reads=[sr, r_hT[t // 4]], writes=[rb], sig=(kc == 7))
                        o_ = q % NKB
                        q += 1
                        T.op("act", lambda e, bk=bk, o_=o_: e.activation(kb[o_][:], bk[:], AF.Copy), reads=[rb], writes=[r_kb[o_]])
                        T.dma("sp", v_d[t * 128:(t + 1) * 128, j * 512:(j + 1) * 512], kb[o_][:], reads=[r_kb[o_]], accum=[r_kvd])
                T.op("dve", lambda e: e.tensor_scalar(ksum[:], ksum[:], 1.0 / 256.0, None, ALU.mult), reads=[r_ksum], writes=[r_ksum])
                T.dma("sp", km_own_d, ksum[:], reads=[r_ksum], accum=[r_kmd])
                T.free(r_hT + r_cs + r_tmp[0] + r_tmp[1] + r_kb + [r_ksum])

        def moba_phase():
            c.wi = (c.wi + 2) // 3 * 3
            SC = 1.0 / float(np.sqrt(128.0))
            with ExitStack() as ph:
                qT = sb("qT_a", [128, 8, TOK], BF16, ph)
                r_qT = [T.new_res("qTa%d" % i) for i in range(NTG)]
                nmT = sb("nmT", [128, TOK], BF16, ph)
                r_nmT = [T.new_res("nmT%d" % i) for i in range(NTG)]
                with ExitStack() as phg:
                    gate = sb("gate", [128, NT, 8, 16], F32, phg)
                    r_gate = T.new_res("gate")
                    with ExitStack() as ph2:
                        hT = sb("hT_q", [128, 8, TOK], BF16, ph2)
                        r_hT = [T.new_res("hTq%d" % i) for i in range(NTG)]
                        kmT = sb("kmT", [128, 8, 16], F32, ph2)
                        r_kmT = T.new_res("kmT")
                        qf = [sb("qf%d" % i, [128, 512], F32, ph2) for i in range(2)]
                        r_qf = [T.new_res("qf%d" % i) for i in range(2)]
                        cs_t = [sb("cs_q%d" % i, [128, 2, 512], F32, ph2) for i in range(3)]
                        r_cs = [T.new_res("csq%d" % i) for i in range(3)]
                        tmp = [[sb("qk0_%d" % i, [128, 512], BF16, ph2), sb("qt1_%d" % i, [128, 512], F32, ph2),
                                sb("qt2_%d" % i, [128, 512], F32, ph2)] for i in range(2)]
                        r_tmp = [[T.new_res("qtmp%d_%d" % (i, k)) for k in range(3)] for i in range(2)]
                        norm_T(xsrc, r_x, NT, G_MIX1, hT, r_hT)
                        T.dma("sp", kmT[:, :, 0:8], km_prev_d.rearrange("p (h b) -> p h b", h=8), writes=[r_kmT], after=[r_kmdp])
                        T.dma("sp", kmT[:, :, 8:16], km_own_d.rearrange("p (h b) -> p h b", h=8), writes=[r_kmT], after=[r_kmd])
                        q = 0
                        ci = 0

                        def cs_load(i):
                            tg_ = i % NTG
                            T.dma("sp", cs_t[i % 3][:], cs_d[:, :, tg_ * 512:(tg_ + 1) * 512], writes=[r_cs[i % 3]], after=[r_csd])

                        cs_load(0)
                        for j in range(2):
                            sv, sr = wslab(wsrc(W["moba_q"], 0, 8, j * 512, 512), [8, 512])
                            for tg in range(NTG):
                                cj = ci % 3
                                ci += 1
                                if ci < 2 * NTG:
                                    cs_load(ci)
                                for hh in range(4):
                                    h = j * 4 + hh
                                    bk, rb = nextbank()
                                    for kc in range(8):
                                        T.op("pe", lambda e, bk=bk, kc=kc, hh=hh, sv=sv, tg=tg: e.matmul(
                                            bk[:], sv[:, kc, hh * 128:(hh + 1) * 128], hT[:, kc, tg * 512:(tg + 1) * 512],
                                            start=(kc == 0), stop=(kc == 7)),
                                            reads=[sr, r_hT[tg]], writes=[rb], sig=(kc == 7))
                                    s_ = q % 2
                                    q += 1

                                    def fin(t1, t2, s_=s_):
                                        T.op("dve", lambda e: e.tensor_tensor(qf[s_][:], t1[:], t2[:], ALU.add),
                                             reads=[r_tmp[s_][1], r_tmp[s_][2]], writes=[r_qf[s_]])

                                    rope1(bk, rb, tmp[s_], r_tmp[s_])

                                    def tail(bk=bk, rb=rb, cj=cj, s_=s_, h=h, tg=tg, fin=fin):
                                        if c.pending2 is not None:
                                            c.pending2()
                                        rope2(bk, rb, cs_t[cj], r_cs[cj], fin, tmp[s_], r_tmp[s_])
                                        T.op("act", lambda e: e.activation(qT[:, h, tg * 512:(tg + 1) * 512], qf[s_][:], AF.Copy),
                                             reads=[r_qf[s_]], writes=[r_qT[tg]])
                                        c.pending2 = lambda: gates(s_, h, tg)

                                    def gates(s_, h, tg):
                                        gb, rg = nextbank()
                                        for tt in range(4):
                                            T.op("pe", lambda e, tt=tt: e.matmul(
                                                gb[:, tt * 16:(tt + 1) * 16], qf[s_][:, tt * 128:(tt + 1) * 128], kmT[:, h, :],
                                                start=True, stop=True),
                                                reads=[r_qf[s_], r_kmT], writes=[rg], sig=(tt == 3))
                                        T.op("act", lambda e: e.activation(
                                            gate[:, tg * 4:(tg + 1) * 4, h, :], gb[:, 0:64].rearrange("p (t b) -> p t b", t=4), AF.Copy),
                                            reads=[rg], writes=[r_gate])

                                    defer(tail)
                        flush()
                        if c.pending2 is not None:
                            c.pending2()
                        c.pending2 = None
                        T.free(r_hT + [r_kmT] + r_qf + r_cs + r_tmp[0] + r_tmp[1])
                    with ExitStack() as ph2:
                        g2 = sb("g2", [128, NT * 8, 16], F32, ph2)
                        ee = sb("ee", [128, NT * 8, 16], F32, ph2)
                        mx = sb("mx", [128, NT * 8], F32, ph2)
                        nmb = sb("nmb", [128, NT, 128], BF16, ph2)
                        pbt = sb("pbt", [128, NT * 8, 16], F32, ph2)
                        b2t = sb("b2t", [128, NT * 8, 16], F32, ph2)
                        r_tk = T.new_res("topk")
                        r_pb = T.new_res("pbt")
                        T.dma("sp", pbt[:], cst_d[:, C_PB:C_PB + 2048].rearrange("p (g b) -> p g b", b=16), writes=[r_pb])
                        T.dma("sp", b2t[:], cst_d[:, C_B2:C_B2 + 2048].rearrange("p (g b) -> p g b", b=16), writes=[r_pb])
                        gm3 = gate[:].rearrange("p t h b -> p (t h) b")
                        mx_bc = mx[:].unsqueeze(2).to_broadcast([128, NT * 8, 16])

                        def dv(fn):
                            T.op("dve", fn, reads=[r_tk, r_pb, r_gate], writes=[r_tk, r_gate])

                        dv(lambda e: e.tensor_tensor(gm3, gm3, pbt[:], ALU.add))
                        dv(lambda e: e.tensor_reduce(mx[:], gm3, AX.X, ALU.max))
                        dv(lambda e: e.tensor_tensor(ee[:], gm3, mx_bc, ALU.is_ge))
                        dv(lambda e: e.scalar_tensor_tensor(g2[:], ee[:], -1e30, gm3, ALU.mult, ALU.add))
                        dv(lambda e: e.tensor_reduce(mx[:], g2[:], AX.X, ALU.max))
                        dv(lambda e: e.tensor_tensor(ee[:], g2[:], mx_bc, ALU.is_ge))
                        dv(lambda e: e.scalar_tensor_tensor(g2[:], ee[:], -1e30, g2[:], ALU.mult, ALU.add))
                        dv(lambda e: e.tensor_reduce(mx[:], g2[:], AX.X, ALU.max))
                        dv(lambda e: e.tensor_tensor(ee[:], gm3, mx_bc, ALU.is_ge))
                        dv(lambda e: e.scalar_tensor_tensor(ee[:], ee[:], -NEG, b2t[:], ALU.mult, ALU.add))
                        dv(lambda e: e.tensor_scalar(nmb[:].rearrange("p t (h b) -> p (t h) b", h=8), ee[:], 0.0, None, ALU.min))
                        for t in range(NT):
                            j = t % 2
                            pt = banks[6 + j][:].bitcast(BF16)
                            T.op("pe", lambda e, t=t, pt=pt: e.transpose(pt[:, 0:128], nmb[:, t, :], idb[:]),
                                 reads=[r_tk, r_cb], writes=[r_bank[6 + j]])
                            T.op("act", lambda e, t=t, pt=pt: e.activation(nmT[:, t * 128:(t + 1) * 128], pt[:, 0:128], AF.Copy),
                                 reads=[r_bank[6 + j]], writes=[r_nmT[t // 4]])
                        T.free([r_tk, r_pb])
                    T.free([r_gate])
                with ExitStack() as ph3:
                    oT = sb("oT_a", [128, 8, TOK], BF16, ph3)
                    r_oT = [T.new_res("oTa%d" % i) for i in range(NTG)]
                    kTs = [sb("kTs%d" % i, [128, 2, TOK], BF16, ph3) for i in range(2)]
                    r_kTs = [T.new_res("kTs%d" % i) for i in range(2)]
                    Vs = [sb("Vs%d" % i, [128, 32, 128], BF16, ph3) for i in range(2)]
                    r_Vs = [T.new_res("Vs%d" % i) for i in range(2)]
                    pTb = [sb("pTb%d" % i, [128, 512], BF16, ph3) for i in range(3)]
                    r_pT = [T.new_res("pTb%d" % i) for i in range(3)]
                    lns = [sb("lna0", [128, 512], F32, ph3)] * 2
                    r_lns = [T.new_res("lna0")] * 2
                    mk0 = sb("mk0", [128, 768], BF16, ph3)
                    mkt = sb("mkt", [128, 768], BF16, ph3)
                    r_mk = T.new_res("mk")
                    ones32 = sb("ones32", [128, 128], F32, ph3)
                    T.op("dve", lambda e: e.memset(ones32[:], 1.0), writes=[r_mk])
                    T.dma("pool", mk0[:], cst_d[:, C_BIG0:C_BIG0 + 768], writes=[r_mk])
                    T.dma("pool", mkt[:], cst_d[:, C_BIGT:C_BIGT + 768], writes=[r_mk])
                    smask = [mk0[:, 256:768], mkt[:, 256:768], mk0[:, 0:512], mkt[:, 0:512]]
                    kTp, kTo = kT_view(kv_prev_d), kT_view(kv_own_d)
                    vp, vo = v_view(kv_prev_d), v_view(kv_own_d)
                    def load_head(h):
                        hb = h % 2
                        T.dma("sp", kTs[hb][:, 0, :], kTp[h], writes=[r_kTs[hb]], after=[r_kvdp])
                        T.dma("sp", kTs[hb][:, 1, :], kTo[h], writes=[r_kTs[hb]], after=[r_kvd])
                        T.dma("sp", Vs[hb][:, 0:16, :], vp[:, h * 128:(h + 1) * 128].rearrange("(c p) f -> p c f", p=128),
                              writes=[r_Vs[hb]], after=[r_kvdp])
                        T.dma("sp", Vs[hb][:, 16:32, :], vo[:, h * 128:(h + 1) * 128].rearrange("(c p) f -> p c f", p=128),
                              writes=[r_Vs[hb]], after=[r_kvd])

                    items = []
                    ai = 0
                    for h in range(8):
                        for g in range(NTG):
                            a_ = ai % 2
                            ai += 1
                            chunks = [(0, cc_, h * 16 + cc_ // 2, None) for cc_ in range(16)]
                            chunks += [(1, cc_, h * 16 + 8 + cc_ // 2, (cc_ - 4 * g) if cc_ >= 4 * g else None)
                                       for cc_ in range(4 * g + 4)]
                            n = len(chunks)
                            for i_, (half, cc_, row, dj) in enumerate(chunks):
                                items.append(dict(h=h, g=g, a_=a_, half=half, cc_=cc_, row=row, dj=dj, i_=i_, n=n,
                                                  p_=len(items) % 3, last_of_head=(g == NTG - 1 and i_ == n - 1)))

                    def emit_S(it):
                        h, g, half, cc_, row, dj, p_ = it["h"], it["g"], it["half"], it["cc_"], it["row"], it["dj"], it["p_"]
                        hb = h % 2
                        stb, rst = banks[p_], r_bank[p_]
                        T.op("pe", lambda e: e.matmul(
                            stb[:], kTs[hb][:, half, cc_ * 128:(cc_ + 1) * 128], qT[:, h, g * 512:(g + 1) * 512],
                            start=True, stop=False),
                            reads=[r_kTs[hb], r_qT[g]], writes=[rst], sig=False)
                        T.op("pe", lambda e: e.matmul(
                            stb[:], idb[:, row:row + 1].to_broadcast([128, 128]), nmT[:, g * 512:(g + 1) * 512],
                            start=False, stop=(dj is None)),
                            reads=[r_cb, r_nmT[g]], writes=[rst], sig=(dj is None))
                        if dj is not None:
                            T.op("pe", lambda e: e.matmul(stb[:], idb[:], smask[dj], start=False, stop=True),
                                 reads=[r_cb, r_mk], writes=[rst])
                        T.op("act", lambda e: e.activation(pTb[p_][:], stb[:], AF.Exp, scale=SC),
                             reads=[rst], writes=[r_pT[p_]])

                    def emit_PV(it):
                        h, g, half, cc_, p_, i_, n, a_ = it["h"], it["g"], it["half"], it["cc_"], it["p_"], it["i_"], it["n"], it["a_"]
                        hb = h % 2
                        ob, rob = banks[3 + a_], r_bank[3 + a_]
                        sbk, rsb = banks[5 + a_], r_bank[5 + a_]
                        vi = cc_ if half == 0 else 16 + cc_
                        T.op("pe", lambda e: e.matmul(
                            ob[:], Vs[hb][:, vi, :], pTb[p_][:], start=(i_ == 0), stop=(i_ == n - 1)),
                            reads=[r_Vs[hb], r_pT[p_]], writes=[rob], sig=True)
                        if i_ % 2 == 0:
                            T.op("pe", lambda e: e.matmul(
                                sbk[:], onesb[:], pTb[p_][:], start=(i_ == 0), stop=False),
                                reads=[r_cb, r_pT[p_]], writes=[rsb], sig=True)
                        elif i_ == 1:
                            T.op("dve", lambda e: e.tensor_copy(banks[7][:], pTb[p_][:]), reads=[r_pT[p_]], writes=[r_bank[7]])
                        else:
                            T.op("dve", lambda e: e.tensor_tensor(banks[7][:], banks[7][:], pTb[p_][:], ALU.add),
                                 reads=[r_pT[p_], r_bank[7]], writes=[r_bank[7]])
                        if i_ == n - 1:
                            T.op("dve", lambda e: e.tensor_copy(lns[a_][:], banks[7][:]), reads=[r_bank[7]], writes=[r_lns[a_]])
                            T.op("pe", lambda e: e.matmul(sbk[:], ones32[:], lns[a_][:], start=False, stop=True),
                                 reads=[r_mk, r_lns[a_]], writes=[rsb])
                            T.op("act", lambda e: e.activation(lns[a_][:], sbk[:], AF.Ln), reads=[rsb], writes=[r_lns[a_]])
                            T.op("act", lambda e: e.activation(lns[a_][:], lns[a_][:], AF.Exp, scale=-1.0),
                                 reads=[r_lns[a_]], writes=[r_lns[a_]])
                            T.op("dve", lambda e: e.tensor_tensor(
                                oT[:, h, g * 512:(g + 1) * 512], ob[:], lns[a_][:], ALU.mult),
                                reads=[rob, r_lns[a_]], writes=[r_oT[g]])
                        if it["last_of_head"] and h + 2 < 8:
                            load_head(h + 2)

                    LA = 2
                    load_head(0)
                    load_head(1)
                    for idx in range(len(items) + LA):
                        if idx < len(items):
                            emit_S(items[idx])
                        if idx - LA >= 0:
                            emit_PV(items[idx - LA])

                    def slabs(half):
                        a = wslab(wsrc(W["moba_o"], 0, 8, half * 512, 512), [8, 512])
                        return [(a[0], a[1], 8)]

                    down_proj(oT, r_oT, 8, slabs)
                    T.free(r_oT + r_kTs + r_Vs + r_pT + r_lns + [r_mk])
                T.free(r_qT + r_nmT)

        def final_phase():
            with ExitStack() as ph:
                gf = sb("gf", [128, 1, D], F32, ph)
                r_gf = T.new_res("gf")
                yb = [sb("yb%d" % i, [128, D], F32, ph) for i in range(2)]
                r_yb = [T.new_res("yb%d" % i) for i in range(2)]
                T.dma("sp", gf[:], gfin_d.partition_broadcast(128), writes=[r_gf])
                slots = dict(c.pre) if len(c.pre) == NT else {}
                for t0 in ((0, 8) if not slots else ()):
                    blk = 0 if t0 < 8 else 1
                    ks = [t0 + i for i in range(8)]
                    for i in range(8):
                        T.op("act", lambda e, t=t0 + i, k=ks[i]: e.activation(junk[:], xres[:, t, :], AF.Square, accum_out=ss[:, k:k + 1]),
                             reads=[r_x[t0 + i]], writes=[r_ss[ks[i]], r_junk])
                        slots[t0 + i] = (ks[i], blk)
                    lo, hi = ks[0], ks[-1] + 1
                    T.op("act", lambda e, lo=lo, hi=hi: e.activation(rs_[:, lo:hi], ss[:, lo:hi], AF.Sqrt, bias=EPS, scale=1.0 / D),
                         reads=[r_ss[k] for k in ks], writes=[r_rsb[blk]])
                    T.op("dve", lambda e, lo=lo, hi=hi: e.reciprocal(rs_[:, lo:hi], rs_[:, lo:hi]), reads=[r_rsb[blk]], writes=[r_rsb[blk]])
                for t in range(NT):
                    k, blk = slots[t]
                    j = t % 2
                    T.op("dve", lambda e, t=t, k=k, j=j: e.scalar_tensor_tensor(
                        yb[j][:], xres[:, t, :], rs_[:, k:k + 1], gf[:, 0, :], ALU.mult, ALU.mult),
                        reads=[r_x[t], r_rsb[blk], r_gf], writes=[r_yb[j]])
                    T.dma("sp", y_d[t * 128:(t + 1) * 128, :], yb[j][:], reads=[r_yb[j]], accum=[r_y])
                c.nrm += NT
                T.free([r_gf] + r_yb)

        if mode == "F":
            for t in range(NT):
                T.dma("sp", xres[:, t, :], xp_d[t * 128:(t + 1) * 128, :], writes=[r_x[t]])
            conv_phase(xhp_d)
            memattn(0, G_MEM0, G_MEMKV0)
            ffn(0, G_FFN0, bg_jobs=[(posp_d, csp_d, r_csdp), (pos_d, cs_d, r_csd)])
            def reload_x():
                c.pre = {}
                for t in range(NT):
                    T.dma("sp", xres[:, t, :], x_d[t * 128:(t + 1) * 128, :], writes=[r_x[t]])

            kv_phase(csp_d, r_csdp, kv_prev_d, km_prev_d, r_kvdp, r_kmdp, after_norm=reload_x)
        if doA:
            if mode != "F":
                cs_tables(pos_d, cs_d, r_csd)
            if stage >= 1 and "conv" not in SKIP:
                conv_phase(xh_d)
            if stage >= 2 and "mem" not in SKIP:
                memattn(0, G_MEM0, G_MEMKV0)
            if stage >= 3 and "ffn" not in SKIP:
                ffn(0, G_FFN0)
            if stage >= 4:
                kv_phase(cs_d, r_csd, kv_own_d, km_own_d, r_kvd, r_kmd)
                finals += [r_kvd, r_kmd, r_csd]
        if doB:
            if stage >= 5:
                moba_phase()
            if stage >= 6:
                memattn(1, G_MEM1, G_MEMKV1)
            if stage >= 7:
                ffn(1, G_FFN1)
            final_phase()
        if mode == "A":
            for t in range(NT):
                T.dma("sp", y_d[t * 128:(t + 1) * 128, :], xres[:, t, :], reads=[r_x[t]], accum=[r_y])
        T.finish(final=finals)
    return nc


def _consts(second_half):
    cst = np.zeros((128, C_W), np.float32)
    cst[:, C_ID:C_ID + 128] = np.eye(128, dtype=np.float32)
    rt = np.zeros((128, 128), np.float32)
    for d_ in range(16):
        rt[d_ + 16, d_] = -1.0
        rt[d_, d_ + 16] = 1.0
    cst[:, C_RT:C_RT + 128] = rt
    invf = np.float32(500000.0) ** (-(np.arange(0, 32, 2, dtype=np.float32)) / np.float32(32))
    cst[0:16, C_INVF] = invf
    cst[16:32, C_INVF] = invf
    k = np.arange(128)[:, None]
    qq = np.arange(128)[None, :]
    tri = np.where(k <= qq, 0.0, NEG).astype(np.float32)
    neg = np.full((128, 128), NEG, np.float32)
    cst[:, C_BIG0 + 256:C_BIG0 + 384] = tri
    cst[:, C_BIGT + 256:C_BIGT + 384] = neg
    cst[:, C_BIGT + 384:C_BIGT + 512] = tri
    pb = np.zeros((16, 16), np.float32)
    b2 = np.zeros((16, 16), np.float32)
    for t in range(16):
        qb = t // 2
        for col in range(16):
            if col < 8:
                past, own = bool(second_half), False
            else:
                past, own = (col - 8) < qb, (col - 8) == qb
            pb[t, col] = 0.0 if past else -1e30
            b2[t, col] = NEG if past else (0.0 if own else 2 * NEG)
    cst[:, C_PB:C_PB + 2048] = np.repeat(pb[:, None, :], 8, axis=1).reshape(1, 2048)
    cst[:, C_B2:C_B2 + 2048] = np.repeat(b2[:, None, :], 8, axis=1).reshape(1, 2048)
    return cst


def _small(inp):
    gains = np.stack([inp["norm_mix"][0], inp["norm_mem"][0], inp["norm_memkv"][0], inp["norm_ffn"][0], inp["kv_norm"],
                      inp["norm_mix"][1], inp["norm_mem"][1], inp["norm_memkv"][1], inp["norm_ffn"][1], inp["norm_final"]])
    sm = np.zeros((128, 104), np.float32)
    sm[:, 0:80] = gains.reshape(10, 8, 128).transpose(2, 0, 1).reshape(128, 80)
    sm[:, 80:104] = inp["conv_w"][0].reshape(3, 8, 128).transpose(2, 0, 1).reshape(128, 24)
    return np.ascontiguousarray(sm)


def _core_inputs(inp, core, mode, x_override=None):
    b, half = core // 2, core % 2
    x = inp["x"] if x_override is None else x_override
    m = {
        "x": np.ascontiguousarray(x[b, half * TOK:(half + 1) * TOK]),
        "cst": _consts(half == 1),
        "sm": _small(inp),
        "gfin": np.ascontiguousarray(inp["norm_final"].reshape(1, D)),
        "mem": np.ascontiguousarray(inp["mem"][b]),
    }
    if mode in ("A", "F"):
        m["xh"] = (np.ascontiguousarray(inp["x"][b, TOK - 128:TOK]) if half == 1 else np.zeros((128, D), np.float32))
        m["pos"] = np.ascontiguousarray(inp["positions"][b, half * TOK:(half + 1) * TOK].reshape(1, TOK).astype(np.int32))
        m["conv_w_in"] = inp["conv_w_in"][0]
        m["conv_w_out"] = inp["conv_w_out"][0]
        m["w_kv"] = inp["w_kv"]
    if mode == "F":
        if half == 1:
            m["xp"] = np.ascontiguousarray(inp["x"][b, 0:TOK])
            m["posp"] = np.ascontiguousarray(inp["positions"][b, 0:TOK].reshape(1, TOK).astype(np.int32))
        else:
            m["xp"] = np.zeros((TOK, D), np.float32)
            m["posp"] = np.zeros((1, TOK), np.int32)
        m["xhp"] = np.zeros((128, D), np.float32)
    if mode in ("B", "F"):
        m["moba_w_q"] = inp["moba_w_q"][0]
        m["moba_w_o"] = inp["moba_w_o"][0]
    for l in ([0] if mode in ("A", "F") else []) + ([1] if mode in ("B", "F") else []):
        m["mem_w_q%d" % l] = inp["mem_w_q"][l]
        m["mem_w_kv%d" % l] = inp["mem_w_kv"][l]
        m["mem_w_o%d" % l] = inp["mem_w_o"][l]
        m["ffn_w_gu%d" % l] = inp["ffn_w_gu"][l]
        m["ffn_w_down%d" % l] = inp["ffn_w_down"][l]
    return m


def kernel(**inputs):
    inp = {k: np.asarray(v) for k, v in inputs.items()}
    n = 8
    nc = build("F")
    maps = [_core_inputs(inp, c, "F") for c in range(n)]
    res = run_bass_kernel_spmd(nc, maps, core_ids=list(range(n))).results
    y = np.stack([res[c]["y"] for c in range(n)]).reshape(4, 4096, D)
    return y.astype(np.float32)
```

```python
import numpy as np
from contextlib import ExitStack
import concourse.bass as bass
import concourse.mybir as mybir
from concourse.bass_utils import run_bass_kernel_spmd

F32 = mybir.dt.float32
BF16 = mybir.dt.bfloat16
I32 = mybir.dt.int32
AF = mybir.ActivationFunctionType
ALU = mybir.AluOpType
AX = mybir.AxisListType


class Res:
    __slots__ = ("name", "writer", "readers", "dsem")

    def __init__(self, name):
        self.name = name
        self.writer = None
        self.readers = {}
        self.dsem = None


class _Sem:
    __slots__ = ("h", "count", "key")

    def __init__(self, h, key):
        self.h = h
        self.count = 0
        self.key = key


class Tracker:
    ENGINES = ("pe", "act", "dve", "pool", "sp")
    SAME_ENGINE_SYNC = ("act", "dve", "pool")

    def __init__(self, nc, stack):
        self.nc = nc
        self.stack = stack
        self.prog = {e: [] for e in self.ENGINES}
        self.esem = {}
        for e in self.ENGINES:
            self.esem[e] = _Sem(stack.enter_context(nc.semaphore("s_" + e)), "s_" + e)
        self.waited = {e: {} for e in self.ENGINES}
        self.nsem = 0
        self.freed = {}

    def _newsem(self, name):
        self.nsem += 1
        return _Sem(self.stack.enter_context(self.nc.semaphore("d%d_%s" % (self.nsem, name))), "d%d" % self.nsem)

    def _waits(self, eng, reads, writes, after=()):
        toks = []
        for r in reads:
            if r.writer is not None:
                toks.append(r.writer)
        for w in list(writes) + list(after):
            if w.writer is not None:
                toks.append(w.writer)
            toks.extend(w.readers.values())
        need = {}
        for (sem, val, src) in toks:
            if src == eng and eng not in self.SAME_ENGINE_SYNC:
                continue
            if self.waited[eng].get(sem.key, 0) >= val:
                continue
            if need.get(sem.key, (None, 0))[1] < val:
                need[sem.key] = (sem, val)
        out = []
        for key, (sem, val) in need.items():
            self.waited[eng][key] = val
            out.append((sem.h, val))
        return out

    def _commit(self, tok, reads, writes):
        for w in writes:
            w.writer = tok
            w.readers = {}
        for r in reads:
            if r not in writes:
                old = r.readers.get(tok[0].key)
                if old is None or old[1] < tok[1]:
                    r.readers[tok[0].key] = tok

    def new_res(self, name):
        r = Res(name)
        r.readers = dict(self.freed)
        return r

    def free(self, rs):
        for r in rs:
            toks = list(r.readers.values())
            if r.writer is not None:
                toks.append(r.writer)
            for tok in toks:
                old = self.freed.get(tok[0].key)
                if old is None or old[1] < tok[1]:
                    self.freed[tok[0].key] = tok

    def op(self, eng, fn, reads=(), writes=(), sig=True):
        waits = self._waits(eng, reads, writes)
        es = self.esem[eng]
        if sig:
            es.count += 1
            tok = (es, es.count, eng)
        else:
            tok = (es, es.count + 1, eng)
        self.prog[eng].append((waits, fn, (es.h, 1) if sig else None))
        self._commit(tok, reads, writes)
        return tok

    def dma(self, eng, out, in_, reads=(), writes=(), accum=(), after=(), **kw):
        waits = self._waits(eng, reads, writes, after)
        prim = writes[0] if writes else reads[0]
        if prim.dsem is None:
            prim.dsem = self._newsem(prim.name)
        ds = prim.dsem
        ds.count += 16
        tok = (ds, ds.count, None)
        self.prog[eng].append((waits, lambda e: e.dma_start(out=out, in_=in_, **kw), (ds.h, 16)))
        self._commit(tok, reads, writes)
        for a in accum:
            old = a.readers.get(ds.key)
            if old is None or old[1] < tok[1]:
                a.readers[ds.key] = tok
        return tok

    def raw(self, eng, fn, reads=(), writes=(), after=(), sem_inc=16):
        waits = self._waits(eng, reads, writes, after)
        prim = writes[0] if writes else reads[0]
        if prim.dsem is None:
            prim.dsem = self._newsem(prim.name)
        ds = prim.dsem
        ds.count += sem_inc
        tok = (ds, ds.count, None)
        self.prog[eng].append((waits, fn, (ds.h, sem_inc)))
        self._commit(tok, reads, writes)
        return tok

    def finish(self, final=()):
        nc = self.nc
        waits = self._waits("sp", [], list(final))
        self.prog["sp"].append((waits, None, None))
        handles = {"pe": "tensor", "act": "scalar", "dve": "vector", "pool": "gpsimd", "sp": "sync"}
        with nc.Block() as block:
            for e in self.ENGINES:
                prog = self.prog[e]

                def body(h, prog=prog):
                    for (waits, fn, inc) in prog:
                        for (sh, val) in waits:
                            h.wait_ge(sh, val)
                        if fn is None:
                            continue
                        inst = fn(h)
                        if inc is not None:
                            inst.then_inc(inc[0], inc[1])

                getattr(block, handles[e])(body)


TOK = 2048
NT = 16
D = 1024
DFF = 2816
TGS = 512
NTG = 4
EPS = 1e-6
PI = float(np.pi)
G_MIX0, G_MEM0, G_MEMKV0, G_FFN0, G_KV, G_MIX1, G_MEM1, G_MEMKV1, G_FFN1, G_FINAL = range(10)
NEG = -30000.0
DEBUG = False
NO_AG = False
FFN_PARTS = 2
FFN_J = 6
FFN_DOWN = True
SKIP = ()
C_ID, C_RT, C_INVF, C_BIG0, C_BIGT, C_PB, C_B2 = 0, 128, 256, 257, 257 + 768, 257 + 1536, 257 + 1536 + 2048
C_W = 257 + 1536 + 4096


class Ctx:
    pass


def build(mode, stage=99):
    nc = bass.Bass("TRN2", target_bir_lowering=False)

    def din(name, shape, dt=F32):
        return nc.dram_tensor(name, list(shape), dt, kind="ExternalInput").ap()

    def dout(name, shape, dt=F32):
        return nc.dram_tensor(name, list(shape), dt, kind="ExternalOutput").ap()

    def dint(name, shape, dt=F32):
        return nc.dram_tensor(name, list(shape), dt, kind="Internal").ap()

    doA = mode in ("A", "F")
    doB = mode in ("B", "F")
    W = {}
    x_d = din("x", [TOK, D])
    cst_d = din("cst", [128, C_W])
    sm_d = din("sm", [128, 104])
    gfin_d = din("gfin", [1, D])
    mem_d = din("mem", [256, D])
    if doA:
        xh_d = din("xh", [128, D])
        pos_d = din("pos", [1, TOK], I32)
        W["conv_in"] = din("conv_w_in", [D, 3 * D])
        W["conv_out"] = din("conv_w_out", [D, D])
        W["w_kv"] = din("w_kv", [D, 2 * D])
    if doB:
        W["moba_q"] = din("moba_w_q", [D, D])
        W["moba_o"] = din("moba_w_o", [D, D])
    layers = ([0] if doA else []) + ([1] if doB else [])
    for l in layers:
        W["mem_q%d" % l] = din("mem_w_q%d" % l, [D, D])
        W["mem_kv%d" % l] = din("mem_w_kv%d" % l, [D, 2 * D])
        W["mem_o%d" % l] = din("mem_w_o%d" % l, [D, D])
        W["gu%d" % l] = din("ffn_w_gu%d" % l, [D, 2 * DFF])
        W["down%d" % l] = din("ffn_w_down%d" % l, [DFF, D])
    if mode == "A":
        kv_own_d = dout("kv_own", [2048, 2048], BF16)
        km_own_d = dout("km_own", [128, 64])
        cs_d = dint("cs_i", [128, 2, TOK])
        cs_out_d = dout("cs", [128, 2, TOK])
        y_d = dout("y", [TOK, D])
    elif mode == "B":
        kv_own_d = din("kv_own", [2048, 2048], BF16)
        kv_prev_d = din("kv_prev", [2048, 2048], BF16)
        km_own_d = din("km_own", [128, 64])
        km_prev_d = din("km_prev", [128, 64])
        cs_d = din("cs", [128, 2, TOK])
        y_d = dout("y", [TOK, D])
    else:
        kv_own_d = dint("kv_own", [2048, 2048], BF16)
        kv_prev_d = dint("kv_prev", [2048, 2048], BF16)
        km_own_d = dint("km_own", [128, 64])
        km_prev_d = dint("km_prev", [128, 64])
        cs_d = dint("cs", [128, 2, TOK])
        csp_d = dint("csp", [128, 2, TOK])
        xp_d = din("xp", [TOK, D])
        xhp_d = din("xhp", [128, D])
        posp_d = din("posp", [1, TOK], I32)
        y_d = dout("y", [TOK, D])

    def kT_view(kv):
        return kv[0:1024, :].rearrange("(h d) t -> h d t", h=8)

    def v_view(kv):
        return kv[1024:2048, :].rearrange("r (two f) -> (r two) f", two=2)

    with ExitStack() as st:
        T = Tracker(nc, st)
        c = Ctx()

        def sb(name, shape, dt, stack=st):
            c.uid = getattr(c, "uid", 0) + 1
            return stack.enter_context(nc.sbuf_tensor("s%d_%s" % (c.uid, name), list(shape), dt))

        r_y = Res("y_d")
        finals = [r_y]
        xres = sb("xres", [128, NT, D], F32)
        r_x = [Res("x%d" % t) for t in range(NT)]
        cst = sb("cst", [128, 257], F32)
        r_cst = Res("cst")
        smt = sb("smt", [128, 104], F32)
        r_sm = Res("sm")
        idb = sb("idb", [128, 128], BF16)
        rtb = sb("rtb", [128, 128], BF16)
        onesb = sb("onesb", [128, 128], BF16)
        r_cb = Res("constbf")
        wbuf = sb("wbuf", [128, 3, 4096], BF16)
        r_w = [Res("w%d" % i) for i in range(3)]
        ss = sb("ss", [128, 32], F32)
        rs_ = sb("rs", [128, 32], F32)
        junk = sb("junk", [128, D], BF16)
        xb = [sb("xb%d" % i, [128, D], BF16) for i in range(3)]
        r_xb = [Res("xb%d" % i) for i in range(3)]
        banks = [st.enter_context(nc.psum_tensor("bank%d" % i, [128, 512], F32)) for i in range(8)]
        r_bank = [Res("bank%d" % i) for i in range(8)]
        c.wi = 0
        c.bi = 0
        c.nrm = 0

        def gcol(gi):
            return smt[:, gi * 8:(gi + 1) * 8]

        def nextbank(lo=0, hi=6):
            b = lo + c.bi % (hi - lo)
            c.bi += 1
            return banks[b], r_bank[b]

        def wslab(src_ap, shape):
            i = c.wi % 3
            c.wi += 1
            n = int(np.prod(shape))
            dst = wbuf[:, i, 0:n]
            if len(shape) == 2:
                dst = dst.rearrange("p (a b) -> p a b", a=shape[0])
            elif len(shape) == 3:
                dst = dst.rearrange("p (a b c) -> p a b c", a=shape[0], b=shape[1])
            T.dma("pool", dst, src_ap, reads=[], writes=[r_w[i]])
            return dst, r_w[i]

        def wsrc(w, r0, nk, c0, ncol):
            return w[r0:r0 + nk * 128, c0:c0 + ncol].rearrange("(kc p) n -> p kc n", p=128)

        T.dma("sp", cst[:], cst_d[:, 0:257], writes=[r_cst])
        T.dma("sp", smt[:], sm_d, writes=[r_sm])
        if mode != "F":
            for t in range(NT):
                T.dma("sp", xres[:, t, :], x_d[t * 128:(t + 1) * 128, :], writes=[r_x[t]])
        T.op("dve", lambda e: e.tensor_copy(idb[:], cst[:, C_ID:C_ID + 128]), reads=[r_cst], writes=[r_cb])
        T.op("dve", lambda e: e.tensor_copy(rtb[:], cst[:, C_RT:C_RT + 128]), reads=[r_cst], writes=[r_cb])
        T.op("dve", lambda e: e.memset(onesb[:], 1.0), reads=[], writes=[r_cb])

        def norm_T(src, r_src, ntiles, gi, hT, r_hT, tiles_per_res=4):
            c.nrm += ntiles

            def batch_a(t0, n):
                blk = 0 if t0 < 8 else 1
                ks = [t0 + i for i in range(n)]
                for i in range(n):
                    k = ks[i]
                    T.op("act", lambda e, k=k, t=t0 + i: e.activation(junk[:], src(t), AF.Square, accum_out=ss[:, k:k + 1]),
                         reads=[r_src[t0 + i]], writes=[r_ss[k], r_junk])
                lo, hi = ks[0], ks[-1] + 1
                T.op("act", lambda e: e.activation(rs_[:, lo:hi], ss[:, lo:hi], AF.Sqrt, bias=EPS, scale=1.0 / D),
                     reads=[r_ss[k] for k in ks], writes=[r_rsb[blk]])
                T.op("dve", lambda e: e.reciprocal(rs_[:, lo:hi], rs_[:, lo:hi]), reads=[r_rsb[blk]], writes=[r_rsb[blk]])
                return {t0 + i: (ks[i], blk) for i in range(n)}

            def stage_b(t, k, blk):
                j = c.xbi % 3
                c.xbi += 1
                T.op("act", lambda e: e.activation(xb[j][:], src(t), AF.Copy, scale=rs_[:, k:k + 1]),
                     reads=[r_src[t], r_rsb[blk]], writes=[r_xb[j]])
                pt = banks[5 + j][:].bitcast(BF16)
                for kc in range(8):
                    T.op("pe", lambda e, kc=kc: e.transpose(pt[:, kc * 128:(kc + 1) * 128], xb[j][:, kc * 128:(kc + 1) * 128], idb[:]),
                         reads=[r_xb[j], r_cb], writes=[r_bank[5 + j]], sig=(kc == 7))
                T.op("dve", lambda e: e.tensor_tensor(
                    hT[:, :, t * 128:(t + 1) * 128], pt.rearrange("p (k n) -> p k n", k=8),
                    gcol(gi).unsqueeze(2).to_broadcast([128, 8, 128]), ALU.mult),
                    reads=[r_bank[5 + j], r_sm], writes=[r_hT[t // tiles_per_res]])

            slots = {}
            if src is xsrc and len(c.pre) == NT:
                slots = dict(c.pre)
            else:
                for t0, n in ((0, 4), (4, 4), (8, 8)) if ntiles == 16 else ((0, ntiles),):
                    slots.update(batch_a(t0, n))
            for t in range(ntiles):
                stage_b(t, *slots[t])

        c.blk = 0
        c.xbi = 0
        r_rsb = [Res("rsb%d" % k) for k in range(4)]
        r_junk = Res("junk")
        r_ss = [Res("ss%d" % k) for k in range(32)]
        r_rs = [Res("rs%d" % k) for k in range(32)]

        def xsrc(t):
            return xres[:, t, :]

        c.pre = {}

        def down_proj(actT, r_act, nk, slabs_for_half, presq=True):
            for half in range(2):
                sl = slabs_for_half(half)
                for t in range(NT):
                    bk, rb = nextbank()
                    ci = 0
                    for (sv, sr, k) in sl:
                        for kk in range(k):
                            T.op("pe", lambda e, sv=sv, kk=kk, ci=ci, bk=bk, t=t: e.matmul(
                                bk[:], actT[:, ci, t * 128:(t + 1) * 128], sv[:, kk, :],
                                start=(ci == 0), stop=(ci == nk - 1)),
                                reads=[r_act[t // 4], sr], writes=[rb], sig=(ci == nk - 1))
                            ci += 1
                    T.op("dve", lambda e, bk=bk, t=t, half=half: e.tensor_tensor(
                        xres[:, t, half * 512:(half + 1) * 512], bk[:], xres[:, t, half * 512:(half + 1) * 512], ALU.add),
                        reads=[rb, r_x[t]], writes=[r_x[t]])
                    if half == 1 and presq:
                        k = 16 + t
                        blk = 2 + t // 8
                        T.op("act", lambda e, k=k, t=t: e.activation(junk[:], xres[:, t, :], AF.Square, accum_out=ss[:, k:k + 1]),
                             reads=[r_x[t]], writes=[r_ss[k], r_junk])
                        c.pre[t] = (k, blk)
                        if t % 8 == 7:
                            lo, hi = 16 + t - 7, 16 + t + 1
                            T.op("act", lambda e, lo=lo, hi=hi: e.activation(rs_[:, lo:hi], ss[:, lo:hi], AF.Sqrt, bias=EPS, scale=1.0 / D),
                                 reads=[r_ss[kk] for kk in range(lo, hi)], writes=[r_rsb[blk]])
                            T.op("dve", lambda e, lo=lo, hi=hi: e.reciprocal(rs_[:, lo:hi], rs_[:, lo:hi]),
                                 reads=[r_rsb[blk]], writes=[r_rsb[blk]])

        def ffn(l, gi, bg_jobs=None):
            c.wi = (c.wi + 2) // 3 * 3
            with ExitStack() as ph:
                hT = sb("hT_f", [128, 8, TOK], BF16, ph)
                r_hT = [T.new_res("hTf%d" % i) for i in range(NTG)]
                actT = sb("actT", [128, 11, TOK], BF16, ph)
                r_act = [T.new_res("act%d" % i) for i in range(NTG)]
                sg = [sb("sg%d" % i, [128, 512], F32, ph) for i in range(2)]
                r_sg = [T.new_res("sg%d" % i) for i in range(2)]
                bg, bg_res = ([], [])
                if bg_jobs:
                    bg, bg_res = cs_bg(ph, bg_jobs)
                norm_T(xsrc, r_x, NT, gi, hT, r_hT)
                wgu = W["gu%d" % l].rearrange("(kc p) (gu f) -> p kc gu f", p=128, gu=2)
                wdn = W["down%d" % l]
                q = 0
                for part in range(FFN_PARTS):
                    for j in range(FFN_J):
                        nch = 2 if j < 5 else 1
                        c0 = part * 1408 + j * 256
                        i_ = c.wi % 3
                        c.wi += 1
                        sv = wbuf[:, i_, 0:16 * nch * 128].rearrange("p (a b c) -> p a b c", a=8, b=2)
                        sr = r_w[i_]
                        for gu_ in range(2):
                            T.dma("pool", sv[:, :, gu_, :], wgu[:, :, gu_, c0:c0 + nch * 128], reads=[], writes=[sr])
                        for tg in range(NTG):
                            for cc in range(nch):
                                ci = j * 2 + cc
                                (gb, rg), (ub, ru) = nextbank(), nextbank()
                                for which, bk, rb in ((0, gb, rg), (1, ub, ru)):
                                    for kc in range(8):
                                        T.op("pe", lambda e, bk=bk, kc=kc, which=which, cc=cc, sv=sv, tg=tg: e.matmul(
                                            bk[:], sv[:, kc, which, cc * 128:(cc + 1) * 128], hT[:, kc, tg * 512:(tg + 1) * 512],
                                            start=(kc == 0), stop=(kc == 7)),
                                            reads=[sr, r_hT[tg]], writes=[rb], sig=(kc == 7))
                                s_ = q % 2
                                q += 1
                                if bg and q % 8 == 4:
                                    bg.pop(0)()
                                T.op("act", lambda e, gb=gb, s_=s_: e.activation(sg[s_][:], gb[:], AF.Silu),
                                     reads=[rg], writes=[r_sg[s_]])
                                T.op("dve", lambda e, ub=ub, s_=s_, ci=ci, tg=tg: e.tensor_tensor(
                                    actT[:, ci, tg * 512:(tg + 1) * 512], ub[:], sg[s_][:], ALU.mult),
                                    reads=[ru, r_sg[s_]], writes=[r_act[tg]])

                    def slabs(half, part=part):
                        r0 = part * 1408
                        a = wslab(wsrc(wdn, r0, 6, half * 512, 512), [6, 512])
                        b = wslab(wsrc(wdn, r0 + 768, 5, half * 512, 512), [5, 512])
                        return [(a[0], a[1], 6), (b[0], b[1], 5)]

                    if FFN_DOWN:
                        down_proj(actT, r_act, 11, slabs, presq=(part == FFN_PARTS - 1))
                while bg:
                    bg.pop(0)()
                T.free(r_hT + r_act + r_sg + bg_res)


        def cs_tables(pos_d, cs_d, r_csd):
            C1 = 6.28125
            C2 = 2 * np.pi - 6.28125
            with ExitStack() as ph:
                posi = sb("posi", [128, 1, TOK], I32, ph)
                r_pos = T.new_res("posi")
                ang = sb("ang", [128, 512], F32, ph)
                uu = sb("uu", [128, 512], F32, ph)
                ki = sb("ki", [128, 512], I32, ph)
                kf = sb("kf", [128, 512], F32, ph)
                mm = sb("mm", [128, 512], F32, ph)
                rr = sb("rr", [128, 2, 512], F32, ph)
                r_t = T.new_res("cs_tmp")
                cso = [sb("cso%d" % i, [128, 2, 512], F32, ph) for i in range(2)]
                r_cso = [T.new_res("cso%d" % i) for i in range(2)]
                T.dma("sp", posi[:], pos_d.partition_broadcast(128), writes=[r_pos])

                def dv(fn, extra=()):
                    T.op("dve", fn, reads=[r_t, r_pos, r_cst] + list(extra), writes=[r_t])

                for ch in range(4):
                    j = ch % 2
                    dv(lambda e, ch=ch: e.tensor_copy(ang[:], posi[:, 0, ch * 512:(ch + 1) * 512]))
                    dv(lambda e: e.tensor_scalar(ang[:], ang[:], cst[:, C_INVF:C_INVF + 1], None, ALU.mult))
                    dv(lambda e: e.tensor_scalar(uu[:], ang[:], float(1.0 / (2 * np.pi)), 0.5, ALU.mult, ALU.add))
                    dv(lambda e: e.tensor_copy(ki[:], uu[:]))
                    dv(lambda e: e.tensor_copy(kf[:], ki[:]))
                    dv(lambda e: e.scalar_tensor_tensor(rr[:, 1, :], kf[:], -C1, ang[:], ALU.mult, ALU.add))
                    dv(lambda e: e.scalar_tensor_tensor(rr[:, 1, :], kf[:], -C2, rr[:, 1, :], ALU.mult, ALU.add))
                    dv(lambda e: e.tensor_scalar(mm[:], rr[:, 1, :], -PI, 2 * PI, ALU.is_lt, ALU.mult))
                    dv(lambda e: e.tensor_tensor(rr[:, 1, :], rr[:, 1, :], mm[:], ALU.add))
                    dv(lambda e: e.tensor_scalar(rr[:, 0, :], rr[:, 1, :], 0.5 * PI, None, ALU.add))
                    dv(lambda e: e.tensor_scalar(mm[:], rr[:, 0, :], PI, -2 * PI, ALU.is_gt, ALU.mult))
                    dv(lambda e: e.tensor_tensor(rr[:, 0, :], rr[:, 0, :], mm[:], ALU.add))
                    dv(lambda e: e.tensor_scalar(rr[:], rr[:], -PI, PI, ALU.max, ALU.min))
                    T.op("act", lambda e, j=j: e.activation(cso[j][:], rr[:], AF.Sin), reads=[r_t], writes=[r_cso[j]])
                    T.dma("sp", cs_d[:, :, ch * 512:(ch + 1) * 512], cso[j][:], reads=[r_cso[j]], accum=[r_csd])
                    if mode == "A":
                        T.dma("sp", cs_out_d[:, :, ch * 512:(ch + 1) * 512], cso[j][:], reads=[r_cso[j]], accum=[r_csd])
                T.free([r_pos, r_t] + r_cso)

        def cs_bg(stack, jobs):
            C1 = 6.28125
            C2 = 2 * np.pi - 6.28125
            posi = sb("bposi", [128, 1, 512], I32, stack)
            r_pos = T.new_res("bposi")
            ang = sb("bang", [128, 512], F32, stack)
            uu = sb("buu", [128, 512], F32, stack)
            ki = sb("bki", [128, 512], I32, stack)
            mm = sb("bmm", [128, 512], F32, stack)
            rr = sb("brr", [128, 2, 512], F32, stack)
            cso = sb("bcso", [128, 2, 512], F32, stack)
            r_t = T.new_res("bcs_tmp")
            r_cso = T.new_res("bcso")
            res_list = [r_pos, r_t, r_cso]

            def dv(fn):
                T.op("dve", fn, reads=[r_t, r_pos, r_cst], writes=[r_t])

            def chunk(pos_d, cs_d, r_csd, ch):
                T.dma("sp", posi[:], pos_d[:, ch * 512:(ch + 1) * 512].partition_broadcast(128), writes=[r_pos], after=[r_t])
                dv(lambda e: e.tensor_copy(ang[:], posi[:, 0, :]))
                dv(lambda e: e.tensor_scalar(ang[:], ang[:], cst[:, C_INVF:C_INVF + 1], None, ALU.mult))
                dv(lambda e: e.tensor_scalar(uu[:], ang[:], float(1.0 / (2 * np.pi)), 0.5, ALU.mult, ALU.add))
                dv(lambda e: e.tensor_copy(ki[:], uu[:]))
                dv(lambda e: e.tensor_copy(uu[:], ki[:]))
                dv(lambda e: e.scalar_tensor_tensor(rr[:, 1, :], uu[:], -C1, ang[:], ALU.mult, ALU.add))
                dv(lambda e: e.scalar_tensor_tensor(rr[:, 1, :], uu[:], -C2, rr[:, 1, :], ALU.mult, ALU.add))
                dv(lambda e: e.tensor_scalar(mm[:], rr[:, 1, :], -PI, 2 * PI, ALU.is_lt, ALU.mult))
                dv(lambda e: e.tensor_tensor(rr[:, 1, :], rr[:, 1, :], mm[:], ALU.add))
                dv(lambda e: e.tensor_scalar(rr[:, 0, :], rr[:, 1, :], 0.5 * PI, None, ALU.add))
                dv(lambda e: e.tensor_scalar(mm[:], rr[:, 0, :], PI, -2 * PI, ALU.is_gt, ALU.mult))
                dv(lambda e: e.tensor_tensor(rr[:, 0, :], rr[:, 0, :], mm[:], ALU.add))
                dv(lambda e: e.tensor_scalar(rr[:], rr[:], -PI, PI, ALU.max, ALU.min))
                T.op("act", lambda e: e.activation(cso[:], rr[:], AF.Sin), reads=[r_t], writes=[r_cso])
                T.dma("sp", cs_d[:, :, ch * 512:(ch + 1) * 512], cso[:], reads=[r_cso], accum=[r_csd])

            out = []
            for (pos_d, cs_d, r_csd) in jobs:
                for ch in range(4):
                    out.append(lambda pos_d=pos_d, cs_d=cs_d, r_csd=r_csd, ch=ch: chunk(pos_d, cs_d, r_csd, ch))
            return out, res_list

        r_csd = Res("cs_d")

        def rope1(ps, r_ps, tmp, r_tmp):
            kb0 = tmp[0]
            T.op("act", lambda e: e.activation(kb0[:], ps[:], AF.Copy), reads=[r_ps], writes=[r_tmp[0]])

        def rope2(ps, r_ps, cs_t, r_cs, out_fn, tmp, r_tmp):
            kb0, t1, t2 = tmp
            rb_, rr = nextbank()
            T.op("pe", lambda e: e.matmul(rb_[:], rtb[:], kb0[:], start=True, stop=True),
                 reads=[r_cb, r_tmp[0]], writes=[rr])
            T.op("dve", lambda e: e.tensor_tensor(t1[:], ps[:], cs_t[:, 0, :], ALU.mult), reads=[r_ps, r_cs, r_tmp[0]], writes=[r_tmp[1]])
            T.op("dve", lambda e: e.tensor_tensor(t2[:], rb_[:], cs_t[:, 1, :], ALU.mult), reads=[rr, r_cs], writes=[r_tmp[2]])
            out_fn(t1, t2)

        c.pending = None
        c.pending2 = None

        def defer(fn):
            if c.pending is not None:
                c.pending()
            c.pending = fn

        def flush():
            if c.pending is not None:
                c.pending()
            c.pending = None

        def conv_phase(xh_d):
            c.wi = (c.wi + 2) // 3 * 3
            with ExitStack() as ph:
                hT = sb("hT_c", [128, 8, TOK], BF16, ph)
                r_hT = [T.new_res("hTc%d" % i) for i in range(NTG)]
                hTh = sb("hTh", [128, 8, 128], BF16, ph)
                r_hTh = [T.new_res("hTh")]
                xh = sb("xh", [128, D], F32, ph)
                r_xh = [T.new_res("xh")]
                yT = sb("yT", [128, 8, TOK], BF16, ph)
                r_yT = [T.new_res("yT%d" % i) for i in range(NTG)]
                zb = [sb("zb%d" % i, [128, 2 + TOK], F32, ph) for i in range(2)]
                r_z = [[T.new_res("z%d_%d" % (i, g)) for g in range(NTG + 1)] for i in range(2)]
                csb = [sb("csb%d" % i, [128, 512], F32, ph) for i in range(2)]
                r_csb = [T.new_res("csb%d" % i) for i in range(2)]
                acc = [sb("acc%d" % i, [128, 512], F32, ph) for i in range(2)]
                r_acc = [T.new_res("acc%d" % i) for i in range(2)]
                T.dma("sp", xh[:], xh_d, writes=r_xh)
                norm_T(lambda t: xh[:], r_xh, 1, G_MIX0, hTh, r_hTh)
                norm_T(xsrc, r_x, NT, G_MIX0, hT, r_hT)
                win = W["conv_in"].rearrange("(kc p) (part fc j) -> p kc part fc j", p=128, part=3, fc=8)
                q = 0
                for fc in range(8):
                    i_ = c.wi % 3
                    c.wi += 1
                    sv = wbuf[:, i_, 0:3072].rearrange("p (a b c) -> p a b c", a=8, b=3)
                    sr = r_w[i_]
                    for part_ in range(3):
                        T.dma("pool", sv[:, :, part_, :], win[:, :, part_, fc, :], reads=[], writes=[sr])
                    z = zb[fc % 2]
                    rz = r_z[fc % 2]
                    hb, rh = nextbank()
                    for which in (1, 2):
                        for kc in range(8):
                            T.op("pe", lambda e, kc=kc, which=which, sv=sv, hb=hb: e.matmul(
                                hb[:, (which - 1) * 2:(which - 1) * 2 + 2], sv[:, kc, which, :], hTh[:, kc, 126:128],
                                start=(kc == 0), stop=(kc == 7)),
                                reads=[sr, r_hTh[0]], writes=[rh], sig=(kc == 7 and which == 2))
                    s_ = q % 2
                    q += 1
                    T.op("act", lambda e, hb=hb, s_=s_: e.activation(csb[s_][:, 0:2], hb[:, 0:2], AF.Copy),
                         reads=[rh], writes=[r_csb[s_]])
                    T.op("dve", lambda e, hb=hb, s_=s_, z=z: e.tensor_tensor(z[:, 0:2], hb[:, 2:4], csb[s_][:, 0:2], ALU.mult),
                         reads=[rh, r_csb[s_]], writes=[rz[NTG]])
                    for tg in range(NTG):
                        bks = [nextbank() for _ in range(3)]
                        for which in range(3):
                            bk, rb = bks[which]
                            for kc in range(8):
                                T.op("pe", lambda e, bk=bk, kc=kc, which=which, sv=sv, tg=tg: e.matmul(
                                    bk[:], sv[:, kc, which, :], hT[:, kc, tg * 512:(tg + 1) * 512],
                                    start=(kc == 0), stop=(kc == 7)),
                                    reads=[sr, r_hT[tg]], writes=[rb], sig=(kc == 7))
                        (bb, rbb), (cb_, rcb), (ub, rub) = bks
                        s_ = q % 2
                        q += 1
                        lo = 2 + tg * 512
                        T.op("act", lambda e, cb_=cb_, s_=s_: e.activation(csb[s_][:], cb_[:], AF.Copy),
                             reads=[rcb], writes=[r_csb[s_]])
                        T.op("dve", lambda e, ub=ub, s_=s_, z=z, lo=lo: e.tensor_tensor(z[:, lo:lo + 512], ub[:], csb[s_][:], ALU.mult),
                             reads=[rub, r_csb[s_]], writes=[rz[tg]])
                        prev = rz[tg - 1] if tg > 0 else rz[NTG]
                        T.op("act", lambda e, s_=s_, z=z, lo=lo, fc=fc: e.activation(
                            acc[s_][:], z[:, lo:lo + 512], AF.Copy, scale=smt[:, 80 + 16 + fc:80 + 16 + fc + 1]),
                            reads=[rz[tg], r_sm], writes=[r_acc[s_]])
                        for jj in (1, 0):
                            sh = 2 - jj
                            T.op("dve", lambda e, s_=s_, z=z, lo=lo, fc=fc, jj=jj, sh=sh: e.scalar_tensor_tensor(
                                acc[s_][:], z[:, lo - sh:lo - sh + 512], smt[:, 80 + jj * 8 + fc:80 + jj * 8 + fc + 1], acc[s_][:],
                                ALU.mult, ALU.add),
                                reads=[rz[tg], prev, r_sm, r_acc[s_]], writes=[r_acc[s_]])
                        T.op("dve", lambda e, bb=bb, s_=s_, fc=fc, tg=tg: e.tensor_tensor(
                            yT[:, fc, tg * 512:(tg + 1) * 512], bb[:], acc[s_][:], ALU.mult),
                            reads=[rbb, r_acc[s_]], writes=[r_yT[tg]])

                def slabs(half):
                    a = wslab(wsrc(W["conv_out"], 0, 8, half * 512, 512), [8, 512])
                    return [(a[0], a[1], 8)]

                if stage == 1 and DEBUG:
                    dbg1 = dout("dbg1", [128, 8, TOK], BF16)
                    dbg2 = dout("dbg2", [128, 8, TOK], BF16)
                    r_dbg = Res("dbg")
                    T.dma("sp", dbg1, hT[:], reads=r_hT, accum=[r_dbg])
                    T.dma("sp", dbg2, yT[:], reads=r_yT, accum=[r_dbg])
                    finals.append(r_dbg)
                down_proj(yT, r_yT, 8, slabs)
                T.free(r_hT + r_hTh + r_xh + r_yT + r_z[0] + r_z[1] + r_csb + r_acc)

        def memattn(l, gi_x, gi_m):
            c.wi = (c.wi + 2) // 3 * 3
            with ExitStack() as ph:
                hT = sb("hT_m", [128, 8, TOK], BF16, ph)
                r_hT = [T.new_res("hTm%d" % i) for i in range(NTG)]
                oT = sb("oT_m", [128, 8, TOK], BF16, ph)
                r_oT = [T.new_res("oTm%d" % i) for i in range(NTG)]
                memx = sb("memx", [128, 2, D], F32, ph)
                r_memx = [T.new_res("memx%d" % i) for i in range(2)]
                memT = sb("memT", [128, 8, 256], BF16, ph)
                r_memT = [T.new_res("memT")]
                KmT = sb("KmT", [128, 8, 256], BF16, ph)
                r_KmT = T.new_res("KmT")
                Vm = sb("Vm", [128, 2, D], BF16, ph)
                r_Vm = T.new_res("Vm")
                qT = [sb("qT%d" % i, [128, 2, 512], BF16, ph) for i in range(2)]
                r_qT = [T.new_res("qT%d" % i) for i in range(2)]
                eT = [sb("eT%d" % i, [128, 2, 512], BF16, ph) for i in range(2)]
                r_eT = [T.new_res("eT%d" % i) for i in range(2)]
                lns = [sb("lns%d" % i, [128, 512], F32, ph) for i in range(2)]
                r_lns = [T.new_res("lns%d" % i) for i in range(2)]
                for mt in range(2):
                    T.dma("sp", memx[:, mt, :], mem_d[mt * 128:(mt + 1) * 128, :], writes=[r_memx[mt]])
                norm_T(lambda t: memx[:, t, :], r_memx, 2, gi_m, memT, r_memT)
                wkv = W["mem_kv%d" % l]
                for j in range(2):
                    sv, sr = wslab(wsrc(wkv, 0, 8, j * 512, 512), [8, 512])
                    for cc in range(4):
                        bk, rb = nextbank()
                        for kc in range(8):
                            T.op("pe", lambda e, bk=bk, kc=kc, cc=cc, sv=sv: e.matmul(
                                bk[:, 0:256], sv[:, kc, cc * 128:(cc + 1) * 128], memT[:, kc, :], start=(kc == 0), stop=(kc == 7)),
                                reads=[sr, r_memT[0]], writes=[rb], sig=(kc == 7))
                        T.op("act", lambda e, bk=bk, j=j, cc=cc: e.activation(KmT[:, j * 4 + cc, :], bk[:, 0:256], AF.Copy),
                             reads=[rb], writes=[r_KmT])
                for j in range(2):
                    sv, sr = wslab(wsrc(wkv, 0, 8, 1024 + j * 512, 512), [8, 512])
                    for mt in range(2):
                        bk, rb = nextbank()
                        for kc in range(8):
                            T.op("pe", lambda e, bk=bk, kc=kc, mt=mt, sv=sv: e.matmul(
                                bk[:], memT[:, kc, mt * 128:(mt + 1) * 128], sv[:, kc, :], start=(kc == 0), stop=(kc == 7)),
                                reads=[sr, r_memT[0]], writes=[rb], sig=(kc == 7))
                        T.op("act", lambda e, bk=bk, j=j, mt=mt: e.activation(Vm[:, mt, j * 512:(j + 1) * 512], bk[:], AF.Copy),
                             reads=[rb], writes=[r_Vm])
                norm_T(xsrc, r_x, NT, gi_x, hT, r_hT)
                wq = W["mem_q%d" % l]
                SCM = 1.0 / 16.0
                slab_of = {}

                def stA(h, tg, s_):
                    if tg == 0:
                        slab_of[h] = wslab(wsrc(wq, 0, 8, h * 256, 256), [8, 256])
                    sv, sr = slab_of[h]
                    for dc in range(2):
                        bk, rb = nextbank()
                        for kc in range(8):
                            T.op("pe", lambda e, bk=bk, kc=kc, dc=dc: e.matmul(
                                bk[:], sv[:, kc, dc * 128:(dc + 1) * 128], hT[:, kc, tg * 512:(tg + 1) * 512],
                                start=(kc == 0), stop=(kc == 7)),
                                reads=[sr, r_hT[tg]], writes=[rb], sig=(kc == 7))
                        T.op("act", lambda e, bk=bk, dc=dc: e.activation(qT[s_][:, dc, :], bk[:], AF.Copy),
                             reads=[rb], writes=[r_qT[s_]])

                def stB(h, tg, s_):
                    for mc in range(2):
                        bk, rb = nextbank()
                        for dc in range(2):
                            T.op("pe", lambda e, bk=bk, dc=dc, mc=mc: e.matmul(
                                bk[:], KmT[:, h * 2 + dc, mc * 128:(mc + 1) * 128], qT[s_][:, dc, :],
                                start=(dc == 0), stop=(dc == 1)),
                                reads=[r_KmT, r_qT[s_]], writes=[rb], sig=(dc == 1))
                        T.op("act", lambda e, bk=bk, mc=mc: e.activation(eT[s_][:, mc, :], bk[:], AF.Exp, scale=SCM),
                             reads=[rb], writes=[r_eT[s_]])

                def stC(h, tg, s_):
                    sbk, rsb = nextbank()
                    for mc in range(2):
                        T.op("pe", lambda e, mc=mc: e.matmul(
                            sbk[:], onesb[:], eT[s_][:, mc, :], start=(mc == 0), stop=(mc == 1)),
                            reads=[r_cb, r_eT[s_]], writes=[rsb], sig=(mc == 1))
                    T.op("act", lambda e: e.activation(lns[s_][:], sbk[:], AF.Ln), reads=[rsb], writes=[r_lns[s_]])
                    T.op("act", lambda e: e.activation(lns[s_][:], lns[s_][:], AF.Exp, scale=-1.0),
                         reads=[r_lns[s_]], writes=[r_lns[s_]])
                    for dc in range(2):
                        bk, rb = nextbank()
                        for mc in range(2):
                            T.op("pe", lambda e, bk=bk, dc=dc, mc=mc: e.matmul(
                                bk[:], Vm[:, mc, h * 256 + dc * 128:h * 256 + (dc + 1) * 128], eT[s_][:, mc, :],
                                start=(mc == 0), stop=(mc == 1)),
                                reads=[r_Vm, r_eT[s_]], writes=[rb], sig=(mc == 1))
                        T.op("dve", lambda e, bk=bk, dc=dc: e.tensor_tensor(
                            oT[:, h * 2 + dc, tg * 512:(tg + 1) * 512], bk[:], lns[s_][:], ALU.mult),
                            reads=[rb, r_lns[s_]], writes=[r_oT[tg]])

                its = [(h, tg, i % 2) for i, (h, tg) in enumerate((h, tg) for h in range(4) for tg in range(NTG))]
                n_it = len(its)
                for i in range(n_it + 2):
                    if i < n_it:
                        stA(*its[i])
                    if 0 <= i - 1 < n_it:
                        stB(*its[i - 1])
                    if 0 <= i - 2 < n_it:
                        stC(*its[i - 2])

                def slabs(half):
                    a = wslab(wsrc(W["mem_o%d" % l], 0, 8, half * 512, 512), [8, 512])
                    return [(a[0], a[1], 8)]

                down_proj(oT, r_oT, 8, slabs)
                T.free(r_hT + r_oT + r_memx + r_memT + [r_KmT, r_Vm] + r_qT + r_eT + r_lns)

        r_kvd = Res("kv_own_d")
        r_kmd = Res("km_own_d")
        r_kvdp = Res("kv_prev_d")
        r_kmdp = Res("km_prev_d")
        r_csdp = Res("csp_d")

        def kv_phase(cs_d, r_csd, kv_own_d, km_own_d, r_kvd, r_kmd, after_norm=None):
            c.wi = (c.wi + 2) // 3 * 3
            with ExitStack() as ph:
                hT = sb("hT_k", [128, 8, TOK], BF16, ph)
                r_hT = [T.new_res("hTk%d" % i) for i in range(NTG)]
                cs_t = [sb("cs_t%d" % i, [128, 2, 512], F32, ph) for i in range(3)]
                r_cs = [T.new_res("cst%d" % i) for i in range(3)]
                tmp = [[sb("rk0_%d" % i, [128, 512], BF16, ph), sb("rt1_%d" % i, [128, 512], F32, ph),
                        sb("rt2_%d" % i, [128, 512], F32, ph)] for i in range(2)]
                r_tmp = [[T.new_res("rtmp%d_%d" % (i, k)) for k in range(3)] for i in range(2)]
                NKB = 8
                kb = [sb("kb%d" % i, [128, 512], BF16, ph) for i in range(NKB)]
                r_kb = [T.new_res("kb%d" % i) for i in range(NKB)]
                ksum = sb("ksum", [128, 64], F32, ph)
                r_ksum = T.new_res("ksum")
                norm_T(xsrc, r_x, NT, G_KV, hT, r_hT)
                if after_norm is not None:
                    after_norm()
                kT_d = kT_view(kv_own_d)
                v_d = v_view(kv_own_d)
                q = 0
                ci = 0

                def cs_load(i):
                    tg_ = i % NTG
                    T.dma("sp", cs_t[i % 3][:], cs_d[:, :, tg_ * 512:(tg_ + 1) * 512], writes=[r_cs[i % 3]], after=[r_csd])

                cs_load(0)
                for j in range(0 if "kvK" in SKIP else 2):
                    sv, sr = wslab(wsrc(W["w_kv"], 0, 8, j * 512, 512), [8, 512])
                    for tg in range(NTG):
                        cj = ci % 3
                        ci += 1
                        if ci < 2 * NTG:
                            cs_load(ci)
                        for hh in range(4):
                            h = j * 4 + hh
                            bk, rb = nextbank()
                            for kc in range(8):
                                T.op("pe", lambda e, bk=bk, kc=kc, hh=hh, sv=sv, tg=tg: e.matmul(
                                    bk[:], sv[:, kc, hh * 128:(hh + 1) * 128], hT[:, kc, tg * 512:(tg + 1) * 512],
                                    start=(kc == 0), stop=(kc == 7)),
                                    reads=[sr, r_hT[tg]], writes=[rb], sig=(kc == 7))
                            s_ = q % 2
                            o_ = q % NKB
                            q += 1

                            def fin(t1, t2, s_=s_, o_=o_, h=h, tg=tg):
                                T.op("dve", lambda e: e.tensor_tensor(t1[:], t1[:], t2[:], ALU.add),
                                     reads=[r_tmp[s_][1], r_tmp[s_][2]], writes=[r_tmp[s_][1]])
                                for blk in range(2):
                                    T.op("act", lambda e, blk=blk: e.activation(
                                        kb[o_][:, blk * 256:(blk + 1) * 256], t1[:, blk * 256:(blk + 1) * 256], AF.Copy,
                                        accum_out=ksum[:, h * 8 + tg * 2 + blk:h * 8 + tg * 2 + blk + 1]),
                                        reads=[r_tmp[s_][1]], writes=[r_kb[o_], r_ksum])

                            rope1(bk, rb, tmp[s_], r_tmp[s_])

                            def tail(bk=bk, rb=rb, cj=cj, s_=s_, o_=o_, h=h, tg=tg, fin=fin):
                                rope2(bk, rb, cs_t[cj], r_cs[cj], fin, tmp[s_], r_tmp[s_])
                                T.dma("sp", kT_d[h, :, tg * 512:(tg + 1) * 512], kb[o_][:], reads=[r_kb[o_]], accum=[r_kvd])

                            defer(tail)
                flush()
                for j in range(0 if "kvV" in SKIP else 2):
                    sv, sr = wslab(wsrc(W["w_kv"], 0, 8, 1024 + j * 512, 512), [8, 512])
                    for t in range(NT):
                        bk, rb = nextbank()
                        for kc in range(8):
                            T.op("pe", lambda e, bk=bk, kc=kc, sv=sv, t=t: e.matmul(
                                bk[:], hT[:, kc, t * 128:(t + 1) * 128], sv[:, kc, :], start=(kc == 0), stop=(kc == 7)),
                                reads=[sr, r_hT[t // 4]], writes=[rb], sig=(kc == 7))
                        o_ = q % NKB
                        q += 1
                        T.op("act", lambda e, bk=bk, o_=o_: e.activation(kb[o_][:], bk[:], AF.Copy), reads=[rb], writes=[r_kb[o_]])
                        T.dma("sp", v_d[t * 128:(t + 1) * 128, j * 512:(j + 1) * 512], kb[o_][:], reads=[r_kb[o_]], accum=[r_kvd])
                T.op("dve", lambda e: e.tensor_scalar(ksum[:], ksum[:], 1.0 / 256.0, None, ALU.mult), reads=[r_ksum], writes=[r_ksum])
                T.dma("sp", km_own_d, ksum[:], reads=[r_ksum], accum=[r_kmd])
                T.free(r_hT + r_cs + r_tmp[0] + r_tmp[1] + r_kb + [r_ksum])

        def moba_phase():
            c.wi = (c.wi + 2) // 3 * 3
            SC = 1.0 / float(np.sqrt(128.0))
            with ExitStack() as ph:
                qT = sb("qT_a", [128, 8, TOK], BF16, ph)
                r_qT = [T.new_res("qTa%d" % i) for i in range(NTG)]
                nmT = sb("nmT", [128, TOK], BF16, ph)
                r_nmT = [T.new_res("nmT%d" % i) for i in range(NTG)]
                with ExitStack() as phg:
                    gate = sb("gate", [128, NT, 8, 16], F32, phg)
                    r_gate = T.new_res("gate")
                    with ExitStack() as ph2:
                        hT = sb("hT_q", [128, 8, TOK], BF16, ph2)
                        r_hT = [T.new_res("hTq%d" % i) for i in range(NTG)]
                        kmT = sb("kmT", [128, 8, 16], F32, ph2)
                        r_kmT = T.new_res("kmT")
                        qf = [sb("qf%d" % i, [128, 512], F32, ph2) for i in range(2)]
                        r_qf = [T.new_res("qf%d" % i) for i in range(2)]
                        cs_t = [sb("cs_q%d" % i, [128, 2, 512], F32, ph2) for i in range(3)]
                        r_cs = [T.new_res("csq%d" % i) for i in range(3)]
                        tmp = [[sb("qk0_%d" % i, [128, 512], BF16, ph2), sb("qt1_%d" % i, [128, 512], F32, ph2),
                                sb("qt2_%d" % i, [128, 512], F32, ph2)] for i in range(2)]
                        r_tmp = [[T.new_res("qtmp%d_%d" % (i, k)) for k in range(3)] for i in range(2)]
                        norm_T(xsrc, r_x, NT, G_MIX1, hT, r_hT)
                        T.dma("sp", kmT[:, :, 0:8], km_prev_d.rearrange("p (h b) -> p h b", h=8), writes=[r_kmT], after=[r_kmdp])
                        T.dma("sp", kmT[:, :, 8:16], km_own_d.rearrange("p (h b) -> p h b", h=8), writes=[r_kmT], after=[r_kmd])
                        q = 0
                        ci = 0

                        def cs_load(i):
                            tg_ = i % NTG
                            T.dma("sp", cs_t[i % 3][:], cs_d[:, :, tg_ * 512:(tg_ + 1) * 512], writes=[r_cs[i % 3]], after=[r_csd])

                        cs_load(0)
                        for j in range(2):
                            sv, sr = wslab(wsrc(W["moba_q"], 0, 8, j * 512, 512), [8, 512])
                            for tg in range(NTG):
                                cj = ci % 3
                                ci += 1
                                if ci < 2 * NTG:
                                    cs_load(ci)
                                for hh in range(4):
                                    h = j * 4 + hh
                                    bk, rb = nextbank()
                                    for kc in range(8):
                                        T.op("pe", lambda e, bk=bk, kc=kc, hh=hh, sv=sv, tg=tg: e.matmul(
                                            bk[:], sv[:, kc, hh * 128:(hh + 1) * 128], hT[:, kc, tg * 512:(tg + 1) * 512],
                                            start=(kc == 0), stop=(kc == 7)),
                                            reads=[sr, r_hT[tg]], writes=[rb], sig=(kc == 7))
                                    s_ = q % 2
                                    q += 1

                                    def fin(t1, t2, s_=s_):
                                        T.op("dve", lambda e: e.tensor_tensor(qf[s_][:], t1[:], t2[:], ALU.add),
                                             reads=[r_tmp[s_][1], r_tmp[s_][2]], writes=[r_qf[s_]])

                                    rope1(bk, rb, tmp[s_], r_tmp[s_])

                                    def tail(bk=bk, rb=rb, cj=cj, s_=s_, h=h, tg=tg, fin=fin):
                                        if c.pending2 is not None:
                                            c.pending2()
                                        rope2(bk, rb, cs_t[cj], r_cs[cj], fin, tmp[s_], r_tmp[s_])
                                        T.op("act", lambda e: e.activation(qT[:, h, tg * 512:(tg + 1) * 512], qf[s_][:], AF.Copy),
                                             reads=[r_qf[s_]], writes=[r_qT[tg]])
                                        c.pending2 = lambda: gates(s_, h, tg)

                                    def gates(s_, h, tg):
                                        gb, rg = nextbank()
                                        for tt in range(4):
                                            T.op("pe", lambda e, tt=tt: e.matmul(
                                                gb[:, tt * 16:(tt + 1) * 16], qf[s_][:, tt * 128:(tt + 1) * 128], kmT[:, h, :],
                                                start=True, stop=True),
                                                reads=[r_qf[s_], r_kmT], writes=[rg], sig=(tt == 3))
                                        T.op("act", lambda e: e.activation(
                                            gate[:, tg * 4:(tg + 1) * 4, h, :], gb[:, 0:64].rearrange("p (t b) -> p t b", t=4), AF.Copy),
                                            reads=[rg], writes=[r_gate])

                                    defer(tail)
                        flush()
                        if c.pending2 is not None:
                            c.pending2()
                        c.pending2 = None
                        T.free(r_hT + [r_kmT] + r_qf + r_cs + r_tmp[0] + r_tmp[1])
                    with ExitStack() as ph2:
                        g2 = sb("g2", [128, NT * 8, 16], F32, ph2)
                        ee = sb("ee", [128, NT * 8, 16], F32, ph2)
                        mx = sb("mx", [128, NT * 8], F32, ph2)
                        nmb = sb("nmb", [128, NT, 128], BF16, ph2)
                        pbt = sb("pbt", [128, NT * 8, 16], F32, ph2)
                        b2t = sb("b2t", [128, NT * 8, 16], F32, ph2)
                        r_tk = T.new_res("topk")
                        r_pb = T.new_res("pbt")
                        T.dma("sp", pbt[:], cst_d[:, C_PB:C_PB + 2048].rearrange("p (g b) -> p g b", b=16), writes=[r_pb])
                        T.dma("sp", b2t[:], cst_d[:, C_B2:C_B2 + 2048].rearrange("p (g b) -> p g b", b=16), writes=[r_pb])
                        gm3 = gate[:].rearrange("p t h b -> p (t h) b")
                        mx_bc = mx[:].unsqueeze(2).to_broadcast([128, NT * 8, 16])

                        def dv(fn):
                            T.op("dve", fn, reads=[r_tk, r_pb, r_gate], writes=[r_tk, r_gate])

                        dv(lambda e: e.tensor_tensor(gm3, gm3, pbt[:], ALU.add))
                        dv(lambda e: e.tensor_reduce(mx[:], gm3, AX.X, ALU.max))
                        dv(lambda e: e.tensor_tensor(ee[:], gm3, mx_bc, ALU.is_ge))
                        dv(lambda e: e.scalar_tensor_tensor(g2[:], ee[:], -1e30, gm3, ALU.mult, ALU.add))
                        dv(lambda e: e.tensor_reduce(mx[:], g2[:], AX.X, ALU.max))
                        dv(lambda e: e.tensor_tensor(ee[:], g2[:], mx_bc, ALU.is_ge))
                        dv(lambda e: e.scalar_tensor_tensor(g2[:], ee[:], -1e30, g2[:], ALU.mult, ALU.add))
                        dv(lambda e: e.tensor_reduce(mx[:], g2[:], AX.X, ALU.max))
                        dv(lambda e: e.tensor_tensor(ee[:], gm3, mx_bc, ALU.is_ge))
                        dv(lambda e: e.scalar_tensor_tensor(ee[:], ee[:], -NEG, b2t[:], ALU.mult, ALU.add))
                        dv(lambda e: e.tensor_scalar(nmb[:].rearrange("p t (h b) -> p (t h) b", h=8), ee[:], 0.0, None, ALU.min))
                        for t in range(NT):
                            j = t % 2
                            pt = banks[6 + j][:].bitcast(BF16)
                            T.op("pe", lambda e, t=t, pt=pt: e.transpose(pt[:, 0:128], nmb[:, t, :], idb[:]),
                                 reads=[r_tk, r_cb], writes=[r_bank[6 + j]])
                            T.op("act", lambda e, t=t, pt=pt: e.activation(nmT[:, t * 128:(t + 1) * 128], pt[:, 0:128], AF.Copy),
                                 reads=[r_bank[6 + j]], writes=[r_nmT[t // 4]])
                        T.free([r_tk, r_pb])
                    T.free([r_gate])
                with ExitStack() as ph3:
                    oT = sb("oT_a", [128, 8, TOK], BF16, ph3)
                    r_oT = [T.new_res("oTa%d" % i) for i in range(NTG)]
                    kTs = [sb("kTs%d" % i, [128, 2, TOK], BF16, ph3) for i in range(2)]
                    r_kTs = [T.new_res("kTs%d" % i) for i in range(2)]
                    Vs = [sb("Vs%d" % i, [128, 32, 128], BF16, ph3) for i in range(2)]
                    r_Vs = [T.new_res("Vs%d" % i) for i in range(2)]
                    pTb = [sb("pTb%d" % i, [128, 512], BF16, ph3) for i in range(3)]
                    r_pT = [T.new_res("pTb%d" % i) for i in range(3)]
                    lns = [sb("lna0", [128, 512], F32, ph3)] * 2
                    r_lns = [T.new_res("lna0")] * 2
                    mk0 = sb("mk0", [128, 768], BF16, ph3)
                    mkt = sb("mkt", [128, 768], BF16, ph3)
                    r_mk = T.new_res("mk")
                    ones32 = sb("ones32", [128, 128], F32, ph3)
                    T.op("dve", lambda e: e.memset(ones32[:], 1.0), writes=[r_mk])
                    T.dma("pool", mk0[:], cst_d[:, C_BIG0:C_BIG0 + 768], writes=[r_mk])
                    T.dma("pool", mkt[:], cst_d[:, C_BIGT:C_BIGT + 768], writes=[r_mk])
                    smask = [mk0[:, 256:768], mkt[:, 256:768], mk0[:, 0:512], mkt[:, 0:512]]
                    kTp, kTo = kT_view(kv_prev_d), kT_view(kv_own_d)
                    vp, vo = v_view(kv_prev_d), v_view(kv_own_d)
                    def load_head(h):
                        hb = h % 2
                        T.dma("sp", kTs[hb][:, 0, :], kTp[h], writes=[r_kTs[hb]], after=[r_kvdp])
                        T.dma("sp", kTs[hb][:, 1, :], kTo[h], writes=[r_kTs[hb]], after=[r_kvd])
                        T.dma("sp", Vs[hb][:, 0:16, :], vp[:, h * 128:(h + 1) * 128].rearrange("(c p) f -> p c f", p=128),
                              writes=[r_Vs[hb]], after=[r_kvdp])
                        T.dma("sp", Vs[hb][:, 16:32, :], vo[:, h * 128:(h + 1) * 128].rearrange("(c p) f -> p c f", p=128),
                              writes=[r_Vs[hb]], after=[r_kvd])

                    items = []
                    ai = 0
                    for h in range(8):
                        for g in range(NTG):
                            a_ = ai % 2
                            ai += 1
                            chunks = [(0, cc_, h * 16 + cc_ // 2, None) for cc_ in range(16)]
                            chunks += [(1, cc_, h * 16 + 8 + cc_ // 2, (cc_ - 4 * g) if cc_ >= 4 * g else None)
                                       for cc_ in range(4 * g + 4)]
                            n = len(chunks)
                            for i_, (half, cc_, row, dj) in enumerate(chunks):
                                items.append(dict(h=h, g=g, a_=a_, half=half, cc_=cc_, row=row, dj=dj, i_=i_, n=n,
                                                  p_=len(items) % 3, last_of_head=(g == NTG - 1 and i_ == n - 1)))

                    def emit_S(it):
                        h, g, half, cc_, row, dj, p_ = it["h"], it["g"], it["half"], it["cc_"], it["row"], it["dj"], it["p_"]
                        hb = h % 2
                        stb, rst = banks[p_], r_bank[p_]
                        T.op("pe", lambda e: e.matmul(
                            stb[:], kTs[hb][:, half, cc_ * 128:(cc_ + 1) * 128], qT[:, h, g * 512:(g + 1) * 512],
                            start=True, stop=False),
                            reads=[r_kTs[hb], r_qT[g]], writes=[rst], sig=False)
                        T.op("pe", lambda e: e.matmul(
                            stb[:], idb[:, row:row + 1].to_broadcast([128, 128]), nmT[:, g * 512:(g + 1) * 512],
                            start=False, stop=(dj is None)),
                            reads=[r_cb, r_nmT[g]], writes=[rst], sig=(dj is None))
                        if dj is not None:
                            T.op("pe", lambda e: e.matmul(stb[:], idb[:], smask[dj], start=False, stop=True),
                                 reads=[r_cb, r_mk], writes=[rst])
                        T.op("act", lambda e: e.activation(pTb[p_][:], stb[:], AF.Exp, scale=SC),
                             reads=[rst], writes=[r_pT[p_]])

                    def emit_PV(it):
                        h, g, half, cc_, p_, i_, n, a_ = it["h"], it["g"], it["half"], it["cc_"], it["p_"], it["i_"], it["n"], it["a_"]
                        hb = h % 2
                        ob, rob = banks[3 + a_], r_bank[3 + a_]
                        sbk, rsb = banks[5 + a_], r_bank[5 + a_]
                        vi = cc_ if half == 0 else 16 + cc_
                        T.op("pe", lambda e: e.matmul(
                            ob[:], Vs[hb][:, vi, :], pTb[p_][:], start=(i_ == 0), stop=(i_ == n - 1)),
                            reads=[r_Vs[hb], r_pT[p_]], writes=[rob], sig=True)
                        if i_ % 3 == 0:
                            T.op("pe", lambda e: e.matmul(
                                sbk[:], onesb[:], pTb[p_][:], start=(i_ == 0), stop=False),
                                reads=[r_cb, r_pT[p_]], writes=[rsb], sig=True)
                        elif i_ == 1:
                            T.op("dve", lambda e: e.tensor_copy(banks[7][:], pTb[p_][:]), reads=[r_pT[p_]], writes=[r_bank[7]])
                        else:
                            T.op("dve", lambda e: e.tensor_tensor(banks[7][:], banks[7][:], pTb[p_][:], ALU.add),
                                 reads=[r_pT[p_], r_bank[7]], writes=[r_bank[7]])
                        if i_ == n - 1:
                            T.op("dve", lambda e: e.tensor_copy(lns[a_][:], banks[7][:]), reads=[r_bank[7]], writes=[r_lns[a_]])
                            T.op("pe", lambda e: e.matmul(sbk[:], ones32[:], lns[a_][:], start=False, stop=True),
                                 reads=[r_mk, r_lns[a_]], writes=[rsb])
                            T.op("act", lambda e: e.activation(lns[a_][:], sbk[:], AF.Ln), reads=[rsb], writes=[r_lns[a_]])
                            T.op("act", lambda e: e.activation(lns[a_][:], lns[a_][:], AF.Exp, scale=-1.0),
                                 reads=[r_lns[a_]], writes=[r_lns[a_]])
                            T.op("dve", lambda e: e.tensor_tensor(
                                oT[:, h, g * 512:(g + 1) * 512], ob[:], lns[a_][:], ALU.mult),
                                reads=[rob, r_lns[a_]], writes=[r_oT[g]])
                        if it["last_of_head"] and h + 2 < 8:
                            load_head(h + 2)

                    LA = 2
                    load_head(0)
                    load_head(1)
                    for idx in range(len(items) + LA):
                        if idx < len(items):
                            emit_S(items[idx])
                        if idx - LA >= 0:
                            emit_PV(items[idx - LA])

                    def slabs(half):
                        a = wslab(wsrc(W["moba_o"], 0, 8, half * 512, 512), [8, 512])
                        return [(a[0], a[1], 8)]

                    down_proj(oT, r_oT, 8, slabs)
                    T.free(r_oT + r_kTs + r_Vs + r_pT + r_lns + [r_mk])
                T.free(r_qT + r_nmT)

        def final_phase():
            with ExitStack() as ph:
                gf = sb("gf", [128, 1, D], F32, ph)
                r_gf = T.new_res("gf")
                yb = [sb("yb%d" % i, [128, D], F32, ph) for i in range(2)]
                r_yb = [T.new_res("yb%d" % i) for i in range(2)]
                T.dma("sp", gf[:], gfin_d.partition_broadcast(128), writes=[r_gf])
                slots = dict(c.pre) if len(c.pre) == NT else {}
                for t0 in ((0, 8) if not slots else ()):
                    blk = 0 if t0 < 8 else 1
                    ks = [t0 + i for i in range(8)]
                    for i in range(8):
                        T.op("act", lambda e, t=t0 + i, k=ks[i]: e.activation(junk[:], xres[:, t, :], AF.Square, accum_out=ss[:, k:k + 1]),
                             reads=[r_x[t0 + i]], writes=[r_ss[ks[i]], r_junk])
                        slots[t0 + i] = (ks[i], blk)
                    lo, hi = ks[0], ks[-1] + 1
                    T.op("act", lambda e, lo=lo, hi=hi: e.activation(rs_[:, lo:hi], ss[:, lo:hi], AF.Sqrt, bias=EPS, scale=1.0 / D),
                         reads=[r_ss[k] for k in ks], writes=[r_rsb[blk]])
                    T.op("dve", lambda e, lo=lo, hi=hi: e.reciprocal(rs_[:, lo:hi], rs_[:, lo:hi]), reads=[r_rsb[blk]], writes=[r_rsb[blk]])
                for t in range(NT):
                    k, blk = slots[t]
                    j = t % 2
                    T.op("dve", lambda e, t=t, k=k, j=j: e.scalar_tensor_tensor(
                        yb[j][:], xres[:, t, :], rs_[:, k:k + 1], gf[:, 0, :], ALU.mult, ALU.mult),
                        reads=[r_x[t], r_rsb[blk], r_gf], writes=[r_yb[j]])
                    T.dma("sp", y_d[t * 128:(t + 1) * 128, :], yb[j][:], reads=[r_yb[j]], accum=[r_y])
                c.nrm += NT
                T.free([r_gf] + r_yb)

        if mode == "F":
            for t in range(NT):
                T.dma("sp", xres[:, t, :], xp_d[t * 128:(t + 1) * 128, :], writes=[r_x[t]])
            conv_phase(xhp_d)
            memattn(0, G_MEM0, G_MEMKV0)
            ffn(0, G_FFN0, bg_jobs=[(posp_d, csp_d, r_csdp), (pos_d, cs_d, r_csd)])
            def reload_x():
                c.pre = {}
                for t in range(NT):
                    T.dma("sp", xres[:, t, :], x_d[t * 128:(t + 1) * 128, :], writes=[r_x[t]])

            kv_phase(csp_d, r_csdp, kv_prev_d, km_prev_d, r_kvdp, r_kmdp, after_norm=reload_x)
        if doA:
            if mode != "F":
                cs_tables(pos_d, cs_d, r_csd)
            if stage >= 1 and "conv" not in SKIP:
                conv_phase(xh_d)
            if stage >= 2 and "mem" not in SKIP:
                memattn(0, G_MEM0, G_MEMKV0)
            if stage >= 3 and "ffn" not in SKIP:
                ffn(0, G_FFN0)
            if stage >= 4:
                kv_phase(cs_d, r_csd, kv_own_d, km_own_d, r_kvd, r_kmd)
                finals += [r_kvd, r_kmd, r_csd]
        if doB:
            if stage >= 5:
                moba_phase()
            if stage >= 6:
                memattn(1, G_MEM1, G_MEMKV1)
            if stage >= 7:
                ffn(1, G_FFN1)
            final_phase()
        if mode == "A":
            for t in range(NT):
                T.dma("sp", y_d[t * 128:(t + 1) * 128, :], xres[:, t, :], reads=[r_x[t]], accum=[r_y])
        T.finish(final=finals)
    return nc


def _consts(second_half):
    cst = np.zeros((128, C_W), np.float32)
    cst[:, C_ID:C_ID + 128] = np.eye(128, dtype=np.float32)
    rt = np.zeros((128, 128), np.float32)
    for d_ in range(16):
        rt[d_ + 16, d_] = -1.0
        rt[d_, d_ + 16] = 1.0
    cst[:, C_RT:C_RT + 128] = rt
    invf = np.float32(500000.0) ** (-(np.arange(0, 32, 2, dtype=np.float32)) / np.float32(32))
    cst[0:16, C_INVF] = invf
    cst[16:32, C_INVF] = invf
    k = np.arange(128)[:, None]
    qq = np.arange(128)[None, :]
    tri = np.where(k <= qq, 0.0, NEG).astype(np.float32)
    neg = np.full((128, 128), NEG, np.float32)
    cst[:, C_BIG0 + 256:C_BIG0 + 384] = tri
    cst[:, C_BIGT + 256:C_BIGT + 384] = neg
    cst[:, C_BIGT + 384:C_BIGT + 512] = tri
    pb = np.zeros((16, 16), np.float32)
    b2 = np.zeros((16, 16), np.float32)
    for t in range(16):
        qb = t // 2
        for col in range(16):
            if col < 8:
                past, own = bool(second_half), False
            else:
                past, own = (col - 8) < qb, (col - 8) == qb
            pb[t, col] = 0.0 if past else -1e30
            b2[t, col] = NEG if past else (0.0 if own else 2 * NEG)
    cst[:, C_PB:C_PB + 2048] = np.repeat(pb[:, None, :], 8, axis=1).reshape(1, 2048)
    cst[:, C_B2:C_B2 + 2048] = np.repeat(b2[:, None, :], 8, axis=1).reshape(1, 2048)
    return cst


def _small(inp):
    gains = np.stack([inp["norm_mix"][0], inp["norm_mem"][0], inp["norm_memkv"][0], inp["norm_ffn"][0], inp["kv_norm"],
                      inp["norm_mix"][1], inp["norm_mem"][1], inp["norm_memkv"][1], inp["norm_ffn"][1], inp["norm_final"]])
    sm = np.zeros((128, 104), np.float32)
    sm[:, 0:80] = gains.reshape(10, 8, 128).transpose(2, 0, 1).reshape(128, 80)
    sm[:, 80:104] = inp["conv_w"][0].reshape(3, 8, 128).transpose(2, 0, 1).reshape(128, 24)
    return np.ascontiguousarray(sm)


def _core_inputs(inp, core, mode, x_override=None):
    b, half = core // 2, core % 2
    x = inp["x"] if x_override is None else x_override
    m = {
        "x": np.ascontiguousarray(x[b, half * TOK:(half + 1) * TOK]),
        "cst": _consts(half == 1),
        "sm": _small(inp),
        "gfin": np.ascontiguousarray(inp["norm_final"].reshape(1, D)),
        "mem": np.ascontiguousarray(inp["mem"][b]),
    }
    if mode in ("A", "F"):
        m["xh"] = (np.ascontiguousarray(inp["x"][b, TOK - 128:TOK]) if half == 1 else np.zeros((128, D), np.float32))
        m["pos"] = np.ascontiguousarray(inp["positions"][b, half * TOK:(half + 1) * TOK].reshape(1, TOK).astype(np.int32))
        m["conv_w_in"] = inp["conv_w_in"][0]
        m["conv_w_out"] = inp["conv_w_out"][0]
        m["w_kv"] = inp["w_kv"]
    if mode == "F":
        if half == 1:
            m["xp"] = np.ascontiguousarray(inp["x"][b, 0:TOK])
            m["posp"] = np.ascontiguousarray(inp["positions"][b, 0:TOK].reshape(1, TOK).astype(np.int32))
        else:
            m["xp"] = np.zeros((TOK, D), np.float32)
            m["posp"] = np.zeros((1, TOK), np.int32)
        m["xhp"] = np.zeros((128, D), np.float32)
    if mode in ("B", "F"):
        m["moba_w_q"] = inp["moba_w_q"][0]
        m["moba_w_o"] = inp["moba_w_o"][0]
    for l in ([0] if mode in ("A", "F") else []) + ([1] if mode in ("B", "F") else []):
        m["mem_w_q%d" % l] = inp["mem_w_q"][l]
        m["mem_w_kv%d" % l] = inp["mem_w_kv"][l]
        m["mem_w_o%d" % l] = inp["mem_w_o"][l]
        m["ffn_w_gu%d" % l] = inp["ffn_w_gu"][l]
        m["ffn_w_down%d" % l] = inp["ffn_w_down"][l]
    return m


def kernel(**inputs):
    inp = {k: np.asarray(v) for k, v in inputs.items()}
    n = 8
    nc = build("F")
    maps = [_core_inputs(inp, c, "F") for c in range(n)]
    res = run_bass_kernel_spmd(nc, maps, core_ids=list(range(n))).results
    y = np.stack([res[c]["y"] for c in range(n)]).reshape(4, 4096, D)
    return y.astype(np.float32)
```
